# Optimizing a Trainium2 kernel written in Bass

```python
import math
import jax, jax.numpy as jnp
from jax import lax
import numpy as np

D_MODEL = 2048
BATCH = 4
SEQ = 2048
DEPTH = 1
DEC_BATCH = 128
DEC_SEQ = 4
PAST_LEN = 2048
PAGE_SIZE = 128

ATTN_WIDTH = D_MODEL // 2
POOL_WIDTH = D_MODEL - ATTN_WIDTH
N_HEADS = 8
HEAD_DIM = ATTN_WIDTH // (2 * N_HEADS)
V_DIM = 2 * HEAD_DIM
POOL_WINDOWS = (2, 4, 8, 16)
N_POOL_GROUPS = len(POOL_WINDOWS)
POOL_GROUP = POOL_WIDTH // N_POOL_GROUPS
POOL_BUF = max(POOL_WINDOWS) - 1
IN_WIDTH = 3 * ATTN_WIDTH + POOL_WIDTH
D_FF = ((8 * D_MODEL // 3 + 127) // 128) * 128
FFN_RESIDUAL = 0.5
Q_BLOCK = 128
NORM_EPS = 1e-6
SUBLN_EPS = 1e-5
MASK_VALUE = -1e30

kernel_name = "hybrid_diffattn_pool_macaron_step"


def rmsnorm(x, g, eps=NORM_EPS):
    xf = x.astype(jnp.float32)
    y = xf * lax.rsqrt(jnp.mean(xf * xf, axis=-1, keepdims=True) + eps)
    return (y * g.astype(jnp.float32)).astype(x.dtype)


def half_step_ffn(x, g, w_gate, w_up, w_down):
    h = rmsnorm(x, g)
    return x + FFN_RESIDUAL * ((jax.nn.silu(h @ w_gate) * (h @ w_up)) @ w_down)


def lambda_init(layer):
    return 0.8 - 0.6 * math.exp(-0.3 * layer)


def diff_lambda(lq1, lk1, lq2, lk2, lam_init):
    f = jnp.float32
    return (jnp.exp(jnp.sum(lq1.astype(f) * lk1.astype(f)))
            - jnp.exp(jnp.sum(lq2.astype(f) * lk2.astype(f))) + lam_init)


def mixer_projections(x, g, w_in):
    B, T, _ = x.shape
    p = rmsnorm(x, g) @ w_in
    q = p[..., :ATTN_WIDTH].reshape(B, T, N_HEADS, V_DIM)
    k = p[..., ATTN_WIDTH:2 * ATTN_WIDTH].reshape(B, T, N_HEADS, V_DIM)
    v = p[..., 2 * ATTN_WIDTH:3 * ATTN_WIDTH].reshape(B, T, N_HEADS, V_DIM)
    u = p[..., 3 * ATTN_WIDTH:]
    return q, k, v, u


def diff_attend(q, k, v, mask, lam):
    scale = HEAD_DIM ** -0.5
    s1 = jnp.einsum('bqhd,bkhd->bhqk', q[..., :HEAD_DIM], k[..., :HEAD_DIM]).astype(jnp.float32) * scale
    s2 = jnp.einsum('bqhd,bkhd->bhqk', q[..., HEAD_DIM:], k[..., HEAD_DIM:]).astype(jnp.float32) * scale
    a = (jax.nn.softmax(jnp.where(mask, s1, MASK_VALUE), axis=-1)
         - lam * jax.nn.softmax(jnp.where(mask, s2, MASK_VALUE), axis=-1))
    return jnp.einsum('bhqk,bkhd->bqhd', a.astype(v.dtype), v)


def prompt_attention(q, k, v, lam):
    B, T = q.shape[:2]
    n_blk = T // Q_BLOCK
    qb = jnp.moveaxis(q.reshape(B, n_blk, Q_BLOCK, N_HEADS, V_DIM), 1, 0)
    kpos = jnp.arange(T)

    def block(args):
        qi, start = args
        qpos = start + jnp.arange(Q_BLOCK)
        return diff_attend(qi, k, v, kpos[None, :] <= qpos[:, None], lam)

    o = lax.map(block, (qb, jnp.arange(n_blk) * Q_BLOCK))
    return jnp.moveaxis(o, 0, 1).reshape(B, T, N_HEADS, V_DIM)


def sample_attention(q, k_new, v_new, k_pool, v_pool, page_table, lam):
    Bd, Tn = q.shape[:2]
    past = page_table.shape[1] * k_pool.shape[1]
    k_past = k_pool[page_table].reshape(Bd, past, N_HEADS, V_DIM)
    v_past = v_pool[page_table].reshape(Bd, past, N_HEADS, V_DIM)
    k_all = jnp.concatenate([k_past, k_new], axis=1)
    v_all = jnp.concatenate([v_past, v_new], axis=1)
    qpos = past + jnp.arange(Tn)
    kpos = jnp.arange(past + Tn)
    return diff_attend(q, k_all, v_all, kpos[None, :] <= qpos[:, None], lam)


def pool_mixer(u, prefix, pos0, w_pool, pool_scale):
    B, T, C = u.shape
    P = prefix.shape[1]
    ext = jnp.concatenate([prefix, u], axis=1).astype(jnp.float32)
    cs = jnp.concatenate([jnp.zeros((B, 1, C), jnp.float32), jnp.cumsum(ext, axis=1)], axis=1)
    cur = ext[:, P:]
    pos = pos0 + jnp.arange(T)
    outs = []
    for g, w in enumerate(POOL_WINDOWS):
        sl = slice(g * POOL_GROUP, (g + 1) * POOL_GROUP)
        win_sum = cs[:, P + 1:P + 1 + T, sl] - cs[:, P + 1 - w:P + 1 - w + T, sl]
        cnt = jnp.minimum(w, pos + 1).astype(jnp.float32)
        outs.append(win_sum / cnt[None, :, None] - cur[..., sl])
    d = jnp.stack(outs, axis=2).astype(u.dtype)
    z = jnp.einsum('btgc,gcd->btgd', d, w_pool).reshape(B, T, C) * pool_scale
    new_buf = ext[:, -POOL_BUF:].astype(u.dtype)
    return z, new_buf


def merge_heads(o, lam_init, subln_gain, z, w_out):
    B, T = o.shape[:2]
    o = rmsnorm(o, subln_gain, SUBLN_EPS) * (1.0 - lam_init)
    cat = jnp.concatenate([o.reshape(B, T, ATTN_WIDTH), z], axis=-1)
    return cat @ w_out


def setup_inputs(seed: int = 0) -> dict:
    key = jax.random.key(seed)
    ks = jax.random.split(key, 32)
    f = jnp.float32
    n_pages = PAST_LEN // PAGE_SIZE
    n_used = DEC_BATCH * n_pages
    n_phys = (n_used * 5) // 4
    nrm = lambda k, shape, s: jax.random.normal(k, shape, f) * s
    page_table = jax.random.permutation(ks[5], n_phys)[:n_used].reshape(DEC_BATCH, n_pages).astype(jnp.int32)
    return {
        "x_prompt": nrm(ks[0], (BATCH, SEQ, D_MODEL), 1.0),
        "x_sample": nrm(ks[1], (DEC_BATCH, DEC_SEQ, D_MODEL), 1.0),
        "cache_k": nrm(ks[2], (DEPTH, n_phys, PAGE_SIZE, N_HEADS, V_DIM), 1.0),
        "cache_v": nrm(ks[3], (DEPTH, n_phys, PAGE_SIZE, N_HEADS, V_DIM), 1.0),
        "state_pool": nrm(ks[4], (DEPTH, DEC_BATCH, POOL_BUF, POOL_WIDTH), 1.0),
        "page_table": page_table,
        "ffn1_norm": 1.0 + nrm(ks[6], (DEPTH, D_MODEL), 0.05),
        "ffn1_w_gate": nrm(ks[7], (DEPTH, D_MODEL, D_FF), D_MODEL ** -0.5),
        "ffn1_w_up": nrm(ks[8], (DEPTH, D_MODEL, D_FF), D_MODEL ** -0.5),
        "ffn1_w_down": nrm(ks[9], (DEPTH, D_FF, D_MODEL), D_FF ** -0.5),
        "mix_norm": 1.0 + nrm(ks[10], (DEPTH, D_MODEL), 0.05),
        "w_in": nrm(ks[11], (DEPTH, D_MODEL, IN_WIDTH), D_MODEL ** -0.5),
        "lambda_q1": nrm(ks[12], (DEPTH, HEAD_DIM), 0.1),
        "lambda_k1": nrm(ks[13], (DEPTH, HEAD_DIM), 0.1),
        "lambda_q2": nrm(ks[14], (DEPTH, HEAD_DIM), 0.1),
        "lambda_k2": nrm(ks[15], (DEPTH, HEAD_DIM), 0.1),
        "subln_gain": 1.0 + nrm(ks[16], (DEPTH, V_DIM), 0.05),
        "w_pool": nrm(ks[17], (DEPTH, N_POOL_GROUPS, POOL_GROUP, POOL_GROUP), POOL_GROUP ** -0.5),
        "pool_scale": 0.5 + nrm(ks[18], (DEPTH, POOL_WIDTH), 0.1),
        "w_out": nrm(ks[19], (DEPTH, ATTN_WIDTH + POOL_WIDTH, D_MODEL), (ATTN_WIDTH + POOL_WIDTH) ** -0.5),
        "ffn2_norm": 1.0 + nrm(ks[20], (DEPTH, D_MODEL), 0.05),
        "ffn2_w_gate": nrm(ks[21], (DEPTH, D_MODEL, D_FF), D_MODEL ** -0.5),
        "ffn2_w_up": nrm(ks[22], (DEPTH, D_MODEL, D_FF), D_MODEL ** -0.5),
        "ffn2_w_down": nrm(ks[23], (DEPTH, D_FF, D_MODEL), D_FF ** -0.5),
        "final_norm": 1.0 + nrm(ks[24], (D_MODEL,), 0.05),
    }


def reference(x_prompt, x_sample, cache_k, cache_v, state_pool, page_table,
              ffn1_norm, ffn1_w_gate, ffn1_w_up, ffn1_w_down,
              mix_norm, w_in, lambda_q1, lambda_k1, lambda_q2, lambda_k2,
              subln_gain, w_pool, pool_scale, w_out,
              ffn2_norm, ffn2_w_gate, ffn2_w_up, ffn2_w_down, final_norm):
    xp, xs = x_prompt, x_sample
    past_len = page_table.shape[1] * cache_k.shape[2]
    k_p, v_p, buf_p, k_s, v_s, buf_s = [], [], [], [], [], []
    for l in range(DEPTH):
        lam_init = lambda_init(l)
        xp = half_step_ffn(xp, ffn1_norm[l], ffn1_w_gate[l], ffn1_w_up[l], ffn1_w_down[l])
        xs = half_step_ffn(xs, ffn1_norm[l], ffn1_w_gate[l], ffn1_w_up[l], ffn1_w_down[l])
        lam = diff_lambda(lambda_q1[l], lambda_k1[l], lambda_q2[l], lambda_k2[l], lam_init)
        qp, kp, vp, up = mixer_projections(xp, mix_norm[l], w_in[l])
        op = prompt_attention(qp, kp, vp, lam)
        zp, bp = pool_mixer(up, jnp.zeros((up.shape[0], POOL_BUF, POOL_WIDTH), up.dtype), 0,
                            w_pool[l], pool_scale[l])
        xp = xp + merge_heads(op, lam_init, subln_gain[l], zp, w_out[l])
        qs, ks_, vs, us = mixer_projections(xs, mix_norm[l], w_in[l])
        os_ = sample_attention(qs, ks_, vs, cache_k[l], cache_v[l], page_table, lam)
        zs, bs = pool_mixer(us, state_pool[l], past_len, w_pool[l], pool_scale[l])
        xs = xs + merge_heads(os_, lam_init, subln_gain[l], zs, w_out[l])
        xp = half_step_ffn(xp, ffn2_norm[l], ffn2_w_gate[l], ffn2_w_up[l], ffn2_w_down[l])
        xs = half_step_ffn(xs, ffn2_norm[l], ffn2_w_gate[l], ffn2_w_up[l], ffn2_w_down[l])
        k_p.append(kp); v_p.append(vp); buf_p.append(bp)
        k_s.append(ks_); v_s.append(vs); buf_s.append(bs)
    y_prompt = rmsnorm(xp, final_norm)
    y_sample = rmsnorm(xs, final_norm)
    return (y_prompt, y_sample,
            jnp.stack(k_p), jnp.stack(v_p), jnp.stack(buf_p),
            jnp.stack(k_s), jnp.stack(v_s), jnp.stack(buf_s))
```

```python
import numpy as np
import concourse.bass as bass
import concourse.mybir as mybir
from concourse.bass_utils import run_bass_kernel_spmd

F32 = mybir.dt.float32
BF16 = mybir.dt.bfloat16
I32 = mybir.dt.int32
AF = mybir.ActivationFunctionType
ALU = mybir.AluOpType
AX = mybir.AxisListType

D = 2048
DFF = 5504
H = 8
TP = 1024
TH = 16
TS = 64
T = TP + TH + TS
SOFF = TP + TH
NB = 16
NPHYS = 2560
TBS = [(0, 512), (512, 512), (1024, 80)]
PREV_TBS = [(0, 512), (512, 512)]
LAM_INIT = 0.8 - 0.6 * 1.0
SCALE = 0.125
NDS = 16
import os
STOP = int(os.environ.get('KSTOP', '0'))


class Trk:
    def __init__(self, nc, sem_iter):
        self.nc = nc
        self.sem_iter = sem_iter
        self.eng = {'pe': nc.tensor, 'act': nc.scalar, 'dve': nc.vector, 'pool': nc.gpsimd, 'sp': nc.sync}
        self.sem = {}
        self.cnt = {}
        for e in ['pe', 'act', 'dve', 'pool']:
            self.sem[e] = next(sem_iter)
            self.cnt[e] = 0
        self.lastw = {}
        self.readers = {}
        self.waited = {}
        self.pending = {e: ([], []) for e in self.sem}
        self.dma_sems = [next(sem_iter) for _ in range(NDS)]
        self.dma_cnt = [0] * NDS
        self.dma_prev = [None] * NDS
        self.dma_i = 0

    def _wait(self, eng, tok):
        sem, val, src = tok
        if src == eng and eng == 'pe':
            return
        key = (eng, id(sem))
        if self.waited.get(key, 0) >= val:
            return
        self.eng[eng].wait_ge(sem, val)
        self.waited[key] = val

    def _deps(self, eng, reads, writes):
        for k in reads:
            w = self.lastw.get(k)
            if w:
                self._wait(eng, w)
        for k in writes:
            w = self.lastw.get(k)
            if w:
                self._wait(eng, w)
            for r in self.readers.get(k, {}).values():
                self._wait(eng, r)

    def _commit(self, tok, reads, writes):
        for k in reads:
            self.readers.setdefault(k, {})[tok[2]] = tok
        for k in writes:
            self.lastw[k] = tok
            self.readers[k] = {}

    def op(self, eng, fn, reads=(), writes=(), inc=True):
        self._deps(eng, reads, writes)
        ins = fn(self.eng[eng])
        pr, pw = self.pending[eng]
        if not inc:
            pr.extend(reads)
            pw.extend(writes)
            return
        if self.cnt[eng] >= 30000:
            self.sem[eng] = next(self.sem_iter)
            self.cnt[eng] = 0
        self.cnt[eng] += 1
        ins.then_inc(self.sem[eng], 1)
        tok = (self.sem[eng], self.cnt[eng], eng)
        self._commit(tok, list(reads) + pr, list(writes) + pw)
        self.pending[eng] = ([], [])

    def dma(self, q, fn, reads=(), writes=()):
        self._deps(q, reads, writes)
        slot = self.dma_i % NDS
        self.dma_i += 1
        prev = self.dma_prev[slot]
        if prev:
            self._wait(q, prev)
        if self.dma_cnt[slot] >= 30000:
            self.dma_sems[slot] = next(self.sem_iter)
            self.dma_cnt[slot] = 0
        ins = fn(self.eng[q])
        self.dma_cnt[slot] += 16
        ins.then_inc(self.dma_sems[slot], 16)
        tok = (self.dma_sems[slot], self.dma_cnt[slot], ('dma', slot))
        self.dma_prev[slot] = tok
        self._commit(tok, reads, writes)

    def barrier(self):
        for e in self.pending:
            assert self.pending[e] == ([], []), e
        toks = [(self.sem[e], self.cnt[e], e) for e in self.sem if self.cnt[e] > 0]
        toks += [tk for tk in self.dma_prev if tk]
        for e in ['pe', 'act', 'dve', 'pool', 'sp']:
            for tk in toks:
                if tk[2] != e:
                    sem, val, src = tk
                    key = (e, id(sem))
                    if self.waited.get(key, 0) >= val:
                        continue
                    self.eng[e].wait_ge(sem, val)
                    self.waited[key] = val
        self.lastw = {}
        self.readers = {}


def build_nc():
    nc = bass.Bass("TRN2", target_bir_lowering=False)

    def din(name, shape, dt=F32):
        return nc.dram_tensor(name, list(shape), dt, kind="ExternalInput").ap()

    def dout(name, shape, dt=F32):
        return nc.dram_tensor(name, list(shape), dt, kind="ExternalOutput").ap()

    xmain = din("xmain", [T, D])
    xprev = din("xprev", [TP, D])
    ck = din("ck", [NPHYS * 128, 1024]) if True else None
    cv = din("cv", [NPHYS * 128, 1024])
    state = din("state", [NB, 15, 1024])
    ptab = din("ptab", [1, NB * 16], I32)
    w1g = din("w1g", [D, DFF]); w1u = din("w1u", [D, DFF]); w1d = din("w1d", [DFF, D])
    w2g = din("w2g", [D, DFF]); w2u = din("w2u", [D, DFF]); w2d = din("w2d", [DFF, D])
    win = din("win", [D, 4096])
    wpool = din("wpool", [4, 256, 256])
    wout = din("wout", [D, D])
    c_ident = din("c_ident", [128, 128])
    c_tri = din("c_tri", [128, 128])
    c_smask = din("c_smask", [128, 64])
    c_invc = din("c_invc", [128, 64])
    c_pbias = din("c_pbias", [128, 1])
    c_gains = din("c_gains", [128, 64])
    c_pscale = din("c_pscale", [128, 8])
    c_subcol = din("c_subcol", [128, 1])
    c_subrow = din("c_subrow", [128, 128])
    c_lams = din("c_lams", [128, 256])

    y_p = dout("y_p", [TP, D]); y_s = dout("y_s", [TS, D])
    k_p = dout("k_p", [TP, 1024]); v_p = dout("v_p", [TP, 1024])
    pool_p = dout("pool_p", [15, 1024])
    k_s = dout("k_s", [TS, 1024]); v_s = dout("v_s", [TS, 1024])
    pool_s = dout("pool_s", [NB, 15, 1024])

    xspill = nc.dram_tensor("xspill", [128, 16 * T], F32, kind="Internal").ap()
    kvspill = nc.dram_tensor("kvspill", [128, 16384], BF16, kind="Internal").ap()

    R1, R2, RZ, R3, R4, RC = 74240, 35328, 17664, 49152, 25600, 6144
    offs = {}
    o = 0
    for nm, sz in [("R1", R1), ("R2", R2), ("RZ", RZ), ("R3", R3), ("R4", R4), ("RC", RC)]:
        offs[nm] = o
        o += sz
    TOTAL = o

    import contextlib
    with contextlib.ExitStack() as es:
        A = es.enter_context(nc.sbuf_tensor("arena", [128, TOTAL // 2], BF16))
        PS = es.enter_context(nc.psum_tensor("ps", [128, 8, 512], F32))
        sems = []
        for i in range(90):
            sems.append(es.enter_context(nc.semaphore(f"s{i}")))
        es.enter_context(nc.Block())
        t = Trk(nc, iter(sems))

        def V_(region, boff, dt, nelem, pat=None, **kw):
            b0 = (offs[region] + boff) // 2
            nb = nelem * (4 if dt in (F32, I32) else 2) // 2
            ap = A[:, b0:b0 + nb]
            if dt != BF16:
                ap = ap.bitcast(dt)
            if pat:
                ap = ap.rearrange(pat, **kw)
            return ap

        def PSb(bank):
            return PS[:, bank, :].bitcast(BF16)

        xT = V_("R1", 0, F32, 16 * T, "p (k n) -> p k n", k=16)
        QTa = V_("R1", 0, BF16, 8 * T, "p (k n) -> p k n", k=8)
        QTb = V_("R1", 17664, BF16, 8 * T, "p (k n) -> p k n", k=8)
        KT = V_("R1", 35328, BF16, 8 * T, "p (k n) -> p k n", k=8)
        KTflat = V_("R1", 35328, BF16, 8 * T + 256)
        Vt = V_("R1", 53504, BF16, 9 * 1024, "p (k n) -> p k n", k=9)
        stT = V_("R3", 36864, F32, 8 * 240, "p (k n) -> p k n", k=8)
        KTp_tmp = V_("R1", 0, BF16, 8 * 1024, "p (k n) -> p k n", k=8)
        Vp_tmp = V_("R1", 16384, BF16, 8 * 1024, "p (k n) -> p k n", k=8)
        hT = V_("R2", 0, BF16, 16 * T, "p (k n) -> p k n", k=16)
        zT = V_("RZ", 0, BF16, 8 * T, "p (k n) -> p k n", k=8)
        WB = [V_("R3", b * 24576, BF16, 12288) for b in range(2)]
        WG = [w[:, 0:4096].rearrange("p (k n) -> p k n", k=16) for w in WB]
        WU = [w[:, 4096:8192].rearrange("p (k n) -> p k n", k=16) for w in WB]
        WD = [w[:, 8192:12288].rearrange("p (k n) -> p k n", k=2) for w in WB]
        WIN = [V_("R3", b * 16384, BF16, 8192, "p (k n) -> p k n", k=16) for b in range(2)]
        WP = V_("R3", 32768, BF16, 2048, "p (g k n) -> p g k n", g=4, k=2)
        SQ = V_("R3", 24576, BF16, 16 * 512, "p (k n) -> p k n", k=16)
        KTprev = V_("R3", 0, BF16, 8 * 1024, "p (k n) -> p k n", k=8)
        Vprev = V_("R3", 16384, BF16, 8 * 1024, "p (k n) -> p k n", k=8)
        Vb = V_("R3", 0, BF16, 16 * 1024, "p (k n) -> p k n", k=16)
        Kp = [V_("R3", 32768 + i * 2048, BF16, 1024) for i in range(2)]
        KTs = [V_("R3", 36864 + i * 2048, BF16, 1024) for i in range(2)]
        PTs = V_("R3", 40960, BF16, 1152)
        Vnew = [V_("R3", 43264 + i * 2048, BF16, 1024) for i in range(2)]
        PTsumH = V_("R3", 47360, BF16, 192)
        PTsumL = V_("R3", 47744, BF16, 192)
        PTnb = V_("R3", 48128, BF16, 192)
        PTn32 = V_("R3", 48512, F32, 64)
        PTsum = V_("R3", 48768, F32, 64)
        XS = [V_("R4", i * 8192, F32, 2048) for i in range(2)]
        RS = V_("R4", 16384, F32, 512)
        RSTD = V_("R4", 18432, F32, 512)
        AT = [V_("R4", i * 2048, BF16, 1024, "p (k n) -> p k n", k=2) for i in range(2)]
        SG = [V_("R4", 4096 + i * 2048, F32, 512) for i in range(2)]
        STG = [V_("R4", i * 2048, F32, 512) for i in range(2)]
        EXP_ = [V_("R4", 4096 + i * 4160, F32, 1040) for i in range(3)]
        EXS = [V_("R4", 16576 + i * 1216, F32, 304, "p (b r) -> p b r", b=16) for i in range(3)]
        DT = V_("R4", 20224, BF16, 2 * T, "p (k n) -> p k n", k=2)
        TMP16 = V_("R4", 24640, F32, 16)
        PT1 = [V_("R4", i * 1024, BF16, 512) for i in range(2)]
        PT2 = [V_("R4", 2048 + i * 1024, BF16, 512) for i in range(2)]
        FR1 = V_("R4", 4096, F32, 512)
        FR2 = V_("R4", 6144, F32, 512)
        FO = V_("R4", 8192, F32, 512)
        FO2 = V_("R4", 10240, F32, 512)
        FRS = V_("R4", 12288, F32, 512)
        FSQ = V_("R4", 14336, BF16, 512)
        OSAMP = V_("R4", 16384, F32, 1024)
        SO = V_("R4", 20480, F32, 1024)
        SO2 = V_("R4", 4096, F32, 1024)
        SSQ = V_("R4", 8192, F32, 1024)
        SSM = V_("R4", 24576, F32, 64)
        identf = V_("RC", 0, F32, 128)
        identb = V_("RC", 512, BF16, 128)
        onesb = V_("RC", 768, BF16, 128)
        trib = V_("RC", 1024, BF16, 128)
        gains = V_("RC", 1280, F32, 64, "p (g k) -> p g k", g=4)
        pscale = V_("RC", 1536, F32, 8)
        gsub = V_("RC", 1568, F32, 1)
        eps6 = V_("RC", 1572, F32, 1)
        eps5 = V_("RC", 1576, F32, 1)
        zerob = V_("RC", 1580, F32, 1)
        pbias = V_("RC", 1584, F32, 1)
        lam = V_("RC", 1588, F32, 1)
        neglam = V_("RC", 1592, F32, 1)
        lt = V_("RC", 1596, F32, 4)
        onesf = V_("RC", 1612, F32, 1)
        invc = V_("RC", 1616, F32, 64, "p (g k) -> p g k", g=4)
        smask = V_("RC", 1872, F32, 64)
        gsubrow = V_("RC", 2128, F32, 128)
        lams = V_("RC", 2640, F32, 256, "p (g k) -> p g k", g=4)
        ltmp = V_("RC", 3664, F32, 64)
        idx = V_("RC", 3920, I32, 256)
        iot = V_("RC", 4944, I32, 256)
        QBLK = V_("R1", 71936, BF16, NB * 64, "p (b h e) -> p b h e", b=NB, h=8)

        def mm(out, lhsT, rhs, start, stop, **kw):
            return lambda e: e.matmul(out, lhsT=lhsT, rhs=rhs, start=start, stop=stop, **kw)

        class _Stop(Exception):
            pass

        def ckpt(k):
            t.barrier()
            if STOP == k:
                raise _Stop()

        def body():
            def load_const(dst, src, key):
                t.dma('sp', lambda e: e.dma_start(out=dst, in_=src), writes=[key])

            load_const(identf, c_ident, 'identf')
            load_const(gains.rearrange("p g k -> p (g k)"), c_gains, 'gains')
            load_const(pscale, c_pscale, 'pscale')
            load_const(gsub, c_subcol, 'gsub')
            load_const(pbias, c_pbias, 'pbias')
            load_const(invc.rearrange("p g k -> p (g k)"), c_invc, 'invc')
            load_const(smask, c_smask, 'smask')
            load_const(gsubrow, c_subrow, 'gsubrow')
            load_const(lams.rearrange("p g k -> p (g k)"), c_lams, 'lams')
            t.dma('pool', lambda e: e.dma_start(out=trib, in_=c_tri), writes=['trib'])
            t.dma('pool', lambda e: e.dma_start(out=idx, in_=ptab[0, :].partition_broadcast(128)), writes=['idx'])
            t.op('pool', lambda e: e.iota(iot, pattern=[[0, 256]], base=0, channel_multiplier=1), writes=['iot'])
            t.op('pool', lambda e: e.tensor_scalar(out=idx, in0=idx, scalar1=128, scalar2=None, op0=ALU.mult),
                 reads=['idx'], writes=['idx'])
            t.op('pool', lambda e: e.tensor_tensor(out=idx, in0=idx, in1=iot, op=ALU.add),
                 reads=['idx', 'iot'], writes=['idx'])
            t.op('dve', lambda e: e.tensor_copy(out=identb, in_=identf), reads=['identf'], writes=['identb'])
            t.op('dve', lambda e: e.memset(onesb, 1.0), writes=['onesb'])
            t.op('dve', lambda e: e.memset(onesf, 1.0), writes=['onesf'])
            t.op('dve', lambda e: e.memset(eps6, 1e-6), writes=['eps6'])
            t.op('dve', lambda e: e.memset(eps5, 1e-5), writes=['eps5'])
            t.op('dve', lambda e: e.memset(zerob, 0.0), writes=['zerob'])
            t.op('dve', lambda e: e.tensor_scalar(out=gsub, in0=gsub, scalar1=1.0 - LAM_INIT, scalar2=None, op0=ALU.mult),
                 reads=['gsub'], writes=['gsub'])
            t.op('dve', lambda e: e.tensor_scalar(out=gsubrow, in0=gsubrow, scalar1=1.0 - LAM_INIT, scalar2=None, op0=ALU.mult),
                 reads=['gsubrow'], writes=['gsubrow'])
            for i in range(2):
                t.op('dve', lambda e: e.tensor_tensor(out=ltmp, in0=lams[:, 2 * i, :], in1=lams[:, 2 * i + 1, :], op=ALU.mult),
                     reads=['lams'], writes=['ltmp'])
                t.op('dve', lambda e: e.tensor_reduce(out=lt[:, i:i + 1], in_=ltmp, axis=AX.X, op=ALU.add),
                     reads=['ltmp'], writes=['lt'])
            t.op('act', lambda e: e.activation(out=lt[:, 2:4], in_=lt[:, 0:2], func=AF.Exp), reads=['lt'], writes=['lt'])
            t.op('dve', lambda e: e.tensor_tensor(out=lam, in0=lt[:, 2:3], in1=lt[:, 3:4], op=ALU.subtract),
                 reads=['lt'], writes=['lam'])
            t.op('dve', lambda e: e.tensor_scalar(out=lam, in0=lam, scalar1=LAM_INIT, scalar2=None, op0=ALU.add),
                 reads=['lam'], writes=['lam'])
            t.op('dve', lambda e: e.tensor_scalar(out=neglam, in0=lam, scalar1=-1.0, scalar2=None, op0=ALU.mult),
                 reads=['lam'], writes=['neglam'])
            t.dma('sp', lambda e: e.dma_start(out=pool_s[:, 0:11, :], in_=state[:, 4:15, :]))
            ckpt(1)

            def load_xT(src, tiles):
                for i, (r0, n, c0) in enumerate(tiles):
                    st = XS[i % 2]
                    bi = c0 // 512
                    t.dma('sp', lambda e: e.dma_start(out=st[0:n, :], in_=src[r0:r0 + n, :]), writes=[('xs', i % 2)])
                    for g in range(4):
                        bank = g % 2
                        for j in range(4):
                            kc = g * 4 + j
                            t.op('pe', lambda e: e.transpose(out=PS[:, bank, j * 128:j * 128 + n],
                                                             in_=st[0:n, kc * 128:(kc + 1) * 128],
                                                             identity=identf[0:n, 0:n]),
                                 reads=[('xs', i % 2)], writes=[('ps', bank)], inc=(j == 3))
                        eng = 'act' if g % 2 == 0 else 'dve'
                        src_ps = PS[:, bank, :].rearrange("p (j n) -> p j n", j=4)[:, :, 0:n]
                        dst = xT[:, g * 4:g * 4 + 4, c0:c0 + n]
                        if eng == 'act':
                            t.op('act', lambda e: e.copy(out=dst, in_=src_ps), reads=[('ps', bank)],
                                 writes=[('xT', kc_, bi) for kc_ in range(g * 4, g * 4 + 4)])
                        else:
                            t.op('dve', lambda e: e.tensor_copy(out=dst, in_=src_ps), reads=[('ps', bank)],
                                 writes=[('xT', kc_, bi) for kc_ in range(g * 4, g * 4 + 4)])

            def norm(gidx, tbs, out_f32=False):
                for bi, (c0, n) in enumerate(tbs):
                    for kc in range(16):
                        t.op('pool', lambda e: e.tensor_tensor(out=SQ[:, kc, 0:n], in0=xT[:, kc, c0:c0 + n],
                                                               in1=xT[:, kc, c0:c0 + n], op=ALU.mult),
                             reads=[('xT', kc, bi)], writes=[('sq', kc)])
                    for kc in range(16):
                        t.op('pe', mm(PS[:, 7, 0:n], onesb, SQ[:, kc, 0:n], kc == 0, kc == 15),
                             reads=[('sq', kc), 'onesb'], writes=[('ps', 7)], inc=(kc == 15))
                    t.op('act', lambda e: e.activation(out=RS[:, 0:n], in_=PS[:, 7, 0:n], func=AF.Sqrt,
                                                       scale=1.0 / D, bias=eps6), reads=[('ps', 7)], writes=['rs'])
                    t.op('dve', lambda e: e.reciprocal(out=RSTD[:, 0:n], in_=RS[:, 0:n]), reads=['rs'], writes=['rstd'])
                    for kc in range(16):
                        if out_f32:
                            t.op('dve', lambda e: e.scalar_tensor_tensor(out=xT[:, kc, c0:c0 + n], in0=xT[:, kc, c0:c0 + n],
                                                                         scalar=gains[:, gidx, kc:kc + 1], in1=RSTD[:, 0:n],
                                                                         op0=ALU.mult, op1=ALU.mult),
                                 reads=[('xT', kc, bi), 'rstd'], writes=[('xT', kc, bi)])
                        else:
                            t.op('dve', lambda e: e.scalar_tensor_tensor(out=hT[:, kc, c0:c0 + n], in0=xT[:, kc, c0:c0 + n],
                                                                         scalar=gains[:, gidx, kc:kc + 1], in1=RSTD[:, 0:n],
                                                                         op0=ALU.mult, op1=ALU.mult),
                                 reads=[('xT', kc, bi), 'rstd'], writes=[('hT', kc, bi)])

            def ffn(wg, wu, wd, tbs):
                wgv = wg.rearrange("(k p) f -> p k f", p=128)
                wuv = wu.rearrange("(k p) f -> p k f", p=128)
                wdv = wd.rearrange("(c p) d -> p c d", p=128)
                NFB = 22

                def prefetch(fb):
                    nfc = 2 if fb < 21 else 1
                    f0 = fb * 256
                    nf = nfc * 128
                    b = fb % 2
                    t.dma('pool', lambda e: e.dma_start(out=WG[b][:, :, 0:nf], in_=wgv[:, :, f0:f0 + nf]), writes=[('wg', b)])
                    t.dma('pool', lambda e: e.dma_start(out=WU[b][:, :, 0:nf], in_=wuv[:, :, f0:f0 + nf]), writes=[('wu', b)])
                    t.dma('pool', lambda e: e.dma_start(out=WD[b][:, 0:nfc, :], in_=wdv[:, fb * 2:fb * 2 + nfc, :]),
                          writes=[('wd', b)])

                ntb = len(tbs)
                units = [(fb, bi) for fb in range(NFB) for bi in range(ntb)]

                def stage1(u):
                    fb, bi = units[u]
                    c0, n = tbs[bi]
                    nfc = 2 if fb < 21 else 1
                    b = fb % 2
                    ab = u % 2
                    for fc in range(nfc):
                        gb = fc % 2
                        ub = 2 + fc % 2
                        for kc in range(16):
                            t.op('pe', mm(PS[:, gb, 0:n], WG[b][:, kc, fc * 128:(fc + 1) * 128], hT[:, kc, c0:c0 + n],
                                          kc == 0, kc == 15),
                                 reads=[('wg', b), ('hT', kc, bi)], writes=[('ps', gb)], inc=(kc == 15))
                        for kc in range(16):
                            t.op('pe', mm(PS[:, ub, 0:n], WU[b][:, kc, fc * 128:(fc + 1) * 128], hT[:, kc, c0:c0 + n],
                                          kc == 0, kc == 15),
                                 reads=[('wu', b), ('hT', kc, bi)], writes=[('ps', ub)], inc=(kc == 15))
                        t.op('act', lambda e: e.activation(out=SG[fc % 2][:, 0:n], in_=PS[:, gb, 0:n], func=AF.Silu),
                             reads=[('ps', gb)], writes=[('sg', fc % 2)])
                        t.op('dve', lambda e: e.tensor_tensor(out=AT[ab][:, fc, 0:n], in0=SG[fc % 2][:, 0:n],
                                                              in1=PS[:, ub, 0:n], op=ALU.mult),
                             reads=[('sg', fc % 2), ('ps', ub)], writes=[('aT', ab, fc)])

                def stage2(u):
                    fb, bi = units[u]
                    c0, n = tbs[bi]
                    nfc = 2 if fb < 21 else 1
                    b = fb % 2
                    ab = u % 2
                    for dc in range(16):
                        yb = 4 + dc % 4
                        for fc in range(nfc):
                            t.op('pe', mm(PS[:, yb, 0:n], WD[b][:, fc, dc * 128:(dc + 1) * 128], AT[ab][:, fc, 0:n],
                                          fc == 0, fc == nfc - 1),
                                 reads=[('wd', b), ('aT', ab, fc)], writes=[('ps', yb)], inc=(fc == nfc - 1))
                        t.op('dve', lambda e: e.scalar_tensor_tensor(out=xT[:, dc, c0:c0 + n], in0=PS[:, yb, 0:n],
                                                                     scalar=0.5, in1=xT[:, dc, c0:c0 + n],
                                                                     op0=ALU.mult, op1=ALU.add),
                             reads=[('ps', yb), ('xT', dc, bi)], writes=[('xT', dc, bi)])

                prefetch(0)
                prefetch(1)
                stage1(0)
                for u in range(len(units)):
                    if u + 1 < len(units):
                        stage1(u + 1)
                    stage2(u)
                    fb, bi = units[u]
                    if bi == ntb - 1 and fb + 2 < NFB:
                        prefetch(fb + 2)

            winv = win.rearrange("(k p) f -> p k f", p=128)
            pcnt = [0]

            def load_win(cb, b):
                t.dma('pool', lambda e: e.dma_start(out=WIN[b], in_=winv[:, :, cb * 512:(cb + 1) * 512]), writes=[('win', b)])

            def proj_fm(b, c, tbs, evac):
                for bi, (c0, n) in enumerate(tbs):
                    bank = pcnt[0] % 4
                    pcnt[0] += 1
                    for kc in range(16):
                        t.op('pe', mm(PS[:, bank, 0:n], WIN[b][:, kc, c * 128:(c + 1) * 128], hT[:, kc, c0:c0 + n],
                                      kc == 0, kc == 15),
                             reads=[('win', b), ('hT', kc, bi)], writes=[('ps', bank)], inc=(kc == 15))
                    evac(bank, bi, c0, n)

            def proj_tm(b, tiles, evac):
                for (tt, c0, n) in tiles:
                    bank = 4 + pcnt[0] % 4
                    pcnt[0] += 1
                    bi = c0 // 512
                    for kc in range(16):
                        t.op('pe', mm(PS[0:n, bank, :], hT[:, kc, c0:c0 + n], WIN[b][:, kc, :], kc == 0, kc == 15),
                             reads=[('win', b), ('hT', kc, bi)], writes=[('ps', bank)], inc=(kc == 15))
                    evac(bank, tt, n)

            ALL_TILES = [(tt, tt * 128, 128) for tt in range(8)] + [(8, 1024, 80)]
            scnt = [0]

            def stage_out(bank, n, dsts):
                si = scnt[0] % 2
                scnt[0] += 1
                t.op('act', lambda e: e.copy(out=STG[si][0:n, :], in_=PS[0:n, bank, :]), reads=[('ps', bank)],
                     writes=[('stg', si)])
                for (dst, p0, p1) in dsts:
                    t.dma('sp', lambda e: e.dma_start(out=dst, in_=STG[si][p0:p1, :]), reads=[('stg', si)])
                return si

            load_xT(xprev, [(i * 128, 128, i * 128) for i in range(8)])
            norm(0, PREV_TBS)
            ckpt(2)
            ffn(w1g, w1u, w1d, PREV_TBS)
            ckpt(3)
            norm(1, PREV_TBS)
            ckpt(4)
            load_win(2, 0)
            for i, cb in enumerate([2, 3, 4, 5]):
                b = i % 2
                if i + 1 < 4:
                    load_win([2, 3, 4, 5][i + 1], (i + 1) % 2)
                if cb in (2, 3):
                    for c in range(4):
                        hh = (cb - 2) * 4 + c

                        def ev(bank, bi, c0, n, hh=hh):
                            t.op('act', lambda e: e.copy(out=KTp_tmp[:, hh, c0:c0 + n], in_=PS[:, bank, 0:n]),
                                 reads=[('ps', bank)], writes=[('ktp', hh, bi)])
                        proj_fm(b, c, PREV_TBS, ev)
                else:
                    col0 = (cb - 4) * 512

                    def ev(bank, tt, n, col0=col0):
                        t.op('act', lambda e: e.copy(out=Vp_tmp[0:n, tt, col0:col0 + 512], in_=PS[0:n, bank, :]),
                             reads=[('ps', bank)], writes=[('vp', tt, col0)])
                    proj_tm(b, ALL_TILES[:8], ev)
            ckpt(5)
            t.dma('sp', lambda e: e.dma_start(out=kvspill[:, 0:8192], in_=KTp_tmp.rearrange("p k n -> p (k n)")))
            t.dma('sp', lambda e: e.dma_start(out=kvspill[:, 8192:16384], in_=Vp_tmp.rearrange("p k n -> p (k n)")))
            ckpt(6)

            tiles = [(i * 128, 128, i * 128) for i in range(8)] + [(1024, 80, 1024)]
            load_xT(xmain, tiles)
            norm(0, TBS)
            ckpt(7)
            ffn(w1g, w1u, w1d, TBS)
            ckpt(8)
            norm(1, TBS)
            ckpt(9)
            t.dma('sp', lambda e: e.dma_start(out=xspill, in_=xT.rearrange("p k n -> p (k n)")))
            ckpt(10)

            stflat = state.rearrange("b r f -> (b r) f")
            for half in range(2):
                t.dma('sp', lambda e: e.dma_start(out=XS[0][0:120, 0:1024], in_=stflat[half * 120:(half + 1) * 120, :]),
                      writes=['xs0'])
                for g in range(2):
                    bank = g
                    for j in range(4):
                        c = g * 4 + j
                        t.op('pe', lambda e: e.transpose(out=PS[:, bank, j * 128:j * 128 + 120],
                                                         in_=XS[0][0:120, c * 128:(c + 1) * 128],
                                                         identity=identf[0:120, 0:120]),
                             reads=['xs0'], writes=[('ps', bank)], inc=(j == 3))
                    t.op('act', lambda e: e.copy(out=stT[:, g * 4:g * 4 + 4, half * 120:(half + 1) * 120],
                                                 in_=PS[:, bank, :].rearrange("p (j n) -> p j n", j=4)[:, :, 0:120]),
                         reads=[('ps', bank)], writes=['stT'])
            ckpt(101)
            t.dma('pool', lambda e: e.dma_start(out=WP, in_=wpool.rearrange("g (k p) n -> p g k n", p=128)), writes=['wp'])
            t.op('dve', lambda e: e.memset(DT[:, :, TP:SOFF], 0.0), writes=['dthalo'])
            t.op('dve', lambda e: e.memset(QTa[64:128, :, :], 0.0), writes=['qtaz'])
            t.op('dve', lambda e: e.memset(QTb[0:64, :, :], 0.0), writes=['qtbz'])

            ckpt(102)
            CBS = [6, 7, 0, 1, 2, 3, 4, 5]
            load_win(CBS[0], 0)
            for i, cb in enumerate(CBS):
                if i > 0 and STOP >= 103:
                    ckpt(102 + i)
                b = i % 2
                if i + 1 < len(CBS):
                    load_win(CBS[i + 1], (i + 1) % 2)
                col0 = (cb % 2) * 512
                if cb >= 6:
                    for c in range(4):
                        uc = (cb - 6) * 4 + c
                        g = uc // 2
                        kk = uc % 2
                        w = 2 ** (g + 1)

                        def ev(bank, bi, c0, n):
                            if bi < 2:
                                t.op('act', lambda e: e.copy(out=EXP_[0][:, 16 + c0:16 + c0 + n], in_=PS[:, bank, 0:n]),
                                     reads=[('ps', bank)], writes=[('exp0', bi)])
                            else:
                                t.op('act', lambda e: e.copy(out=EXP_[0][:, 0:16], in_=PS[:, bank, 0:16]),
                                     reads=[('ps', bank)], writes=[('exp0', 2)])
                                t.op('act', lambda e: e.copy(out=EXS[0][:, :, 15:19],
                                                             in_=PS[:, bank, 16:80].rearrange("p (b q) -> p b q", b=16)),
                                     reads=[('ps', bank)], writes=['exs0n'])
                        proj_fm(b, c, TBS, ev)
                        t.op('act', lambda e: e.copy(out=EXS[0][:, :, 0:15],
                                                     in_=stT[:, uc, :].rearrange("p (b r) -> p b r", b=16)),
                             reads=['stT'], writes=['exs0s'])
                        cur = 0
                        rk = [('exp0', 0), ('exp0', 1), ('exp0', 2)]
                        rks = ['exs0n', 'exs0s']
                        s = 1
                        for stp in range(g + 1):
                            nxt = 1 if cur != 1 else 2
                            t.op('dve', lambda e: e.tensor_tensor(out=EXP_[nxt][:, s:1040], in0=EXP_[cur][:, s:1040],
                                                                  in1=EXP_[cur][:, 0:1040 - s], op=ALU.add),
                                 reads=rk, writes=[('exp', nxt)])
                            t.op('pool', lambda e: e.tensor_tensor(out=EXS[nxt][:, :, s:19], in0=EXS[cur][:, :, s:19],
                                                                   in1=EXS[cur][:, :, 0:19 - s], op=ALU.add),
                                 reads=rks, writes=[('exs', nxt)])
                            cur = nxt
                            rk = [('exp', cur)]
                            rks = [('exs', cur)]
                            s *= 2
                        t.op('dve', lambda e: e.scalar_tensor_tensor(out=DT[:, kk, 0:TP], in0=EXP_[cur][:, 16:1040],
                                                                     scalar=1.0 / w, in1=EXP_[0][:, 16:1040],
                                                                     op0=ALU.mult, op1=ALU.subtract),
                             reads=rk + [('exp0', 0), ('exp0', 1)], writes=[('dt', kk)])
                        t.op('dve', lambda e: e.tensor_tensor(out=TMP16, in0=EXP_[cur][:, 16:32], in1=invc[:, g, :], op=ALU.mult),
                             reads=rk + ['invc'], writes=['tmp16'])
                        t.op('dve', lambda e: e.tensor_tensor(out=DT[:, kk, 0:16], in0=TMP16, in1=EXP_[0][:, 16:32],
                                                              op=ALU.subtract),
                             reads=['tmp16', ('exp0', 0)], writes=[('dt', kk)])
                        t.op('dve', lambda e: e.scalar_tensor_tensor(
                            out=DT[:, kk, SOFF:T].rearrange("p (b q) -> p b q", b=16), in0=EXS[cur][:, :, 15:19],
                            scalar=1.0 / w, in1=EXS[0][:, :, 15:19], op0=ALU.mult, op1=ALU.subtract),
                            reads=rks + ['exs0n'], writes=[('dt', kk)])
                        if kk == 1:
                            for oc in range(2):
                                for bi, (c0, n) in enumerate(TBS):
                                    bank = pcnt[0] % 4
                                    pcnt[0] += 1
                                    for k2 in range(2):
                                        t.op('pe', mm(PS[:, bank, 0:n], WP[:, g, k2, oc * 128:(oc + 1) * 128],
                                                      DT[:, k2, c0:c0 + n], k2 == 0, k2 == 1),
                                             reads=['wp', ('dt', k2), 'dthalo'], writes=[('ps', bank)], inc=(k2 == 1))
                                    zc = 2 * g + oc
                                    t.op('act', lambda e: e.activation(out=zT[:, zc, c0:c0 + n], in_=PS[:, bank, 0:n],
                                                                       func=AF.Copy, scale=pscale[:, zc:zc + 1]),
                                         reads=[('ps', bank)], writes=[('zT', zc, bi)])

                    def ev(bank, tt, n, col0=col0):
                        if tt == 7:
                            stage_out(bank, n, [(pool_p[:, col0:col0 + 512], 113, 128)])
                        else:
                            stage_out(bank, n, [(pool_s[bb, 11:15, col0:col0 + 512], 16 + 4 * bb, 20 + 4 * bb)
                                                for bb in range(NB)])
                    proj_tm(b, ALL_TILES[7:9], ev)
                elif cb in (0, 1, 2, 3):
                    for c in range(4):
                        hh = (cb % 2) * 4 + c

                        def ev(bank, bi, c0, n, hh=hh, isq=(cb < 2)):
                            if isq:
                                t.op('act', lambda e: e.copy(out=QTa[0:64, hh, c0:c0 + n], in_=PS[0:64, bank, 0:n]),
                                     reads=[('ps', bank)], writes=[('QTa', hh, bi)])
                                t.op('act', lambda e: e.copy(out=QTb[64:128, hh, c0:c0 + n], in_=PS[64:128, bank, 0:n]),
                                     reads=[('ps', bank)], writes=[('QTb', hh, bi)])
                            else:
                                t.op('act', lambda e: e.copy(out=KT[:, hh, c0:c0 + n], in_=PS[:, bank, 0:n]),
                                     reads=[('ps', bank)], writes=[('KT', hh, bi)])
                        proj_fm(b, c, TBS, ev)
                    if cb >= 2:
                        def ev(bank, tt, n, col0=col0):
                            if tt < 8:
                                stage_out(bank, n, [(k_p[tt * 128:(tt + 1) * 128, col0:col0 + 512], 0, 128)])
                            else:
                                stage_out(bank, n, [(k_s[:, col0:col0 + 512], 16, 80)])
                        proj_tm(b, ALL_TILES, ev)
                else:
                    def ev(bank, tt, n, col0=col0):
                        if tt < 8:
                            si = stage_out(bank, n, [(v_p[tt * 128:(tt + 1) * 128, col0:col0 + 512], 0, 128)])
                        else:
                            si = stage_out(bank, n, [(v_s[:, col0:col0 + 512], 16, 80)])
                        t.op('pool', lambda e: e.tensor_copy(out=Vt[0:n, tt, col0:col0 + 512], in_=STG[si][0:n, :]),
                             reads=[('stg', si)], writes=[('V', tt, col0)])
                    proj_tm(b, ALL_TILES, ev)
            ckpt(11)

            t.dma('sp', lambda e: e.dma_start(out=KTprev.rearrange("p k n -> p (k n)"), in_=kvspill[:, 0:8192]), writes=['kvp'])
            t.dma('sp', lambda e: e.dma_start(out=Vprev.rearrange("p k n -> p (k n)"), in_=kvspill[:, 8192:16384]), writes=['kvp'])
            for h in range(H):
                for qs in range(2):
                    q0 = qs * 512
                    blocks = [('prev', kb) for kb in range(8)] + [('own', kb) for kb in range(4 * qs + 4)]
                    nblk = len(blocks)
                    def blk(i):
                        src, kb = blocks[i]
                        r = kb - 4 * qs if src == 'own' else -1
                        off = max(r, 0) * 128
                        n = 512 - off
                        if src == 'prev':
                            return (r, off, n, KTprev[:, h, kb * 128:(kb + 1) * 128],
                                    Vprev[:, kb, h * 128:(h + 1) * 128], pbias, ['kvp'])
                        return (r, off, n, KT[:, h, kb * 128:(kb + 1) * 128],
                                Vt[:, kb, h * 128:(h + 1) * 128], zerob, [])

                    def s_stage(i):
                        r, off, n, Ksrc, Vsrc, bias, rkeys = blk(i)
                        sb = i % 2
                        t.op('pe', mm(PS[:, sb, 0:n], Ksrc, QTa[:, h, q0 + off:q0 + 512], True, True),
                             reads=rkeys, writes=[('ps', sb)])
                        t.op('pe', mm(PS[:, 2 + sb, 0:n], Ksrc, QTb[:, h, q0 + off:q0 + 512], True, True),
                             reads=rkeys, writes=[('ps', 2 + sb)])
                        t.op('act', lambda e: e.activation(out=PT1[sb][:, 0:n], in_=PS[:, sb, 0:n], func=AF.Exp,
                                                           scale=SCALE, bias=bias), reads=[('ps', sb)], writes=[('pt1', sb)])
                        t.op('act', lambda e: e.activation(out=PT2[sb][:, 0:n], in_=PS[:, 2 + sb, 0:n], func=AF.Exp,
                                                           scale=SCALE, bias=bias), reads=[('ps', 2 + sb)], writes=[('pt2', sb)])
                        if r >= 0:
                            t.op('pool', lambda e: e.tensor_tensor(out=PT1[sb][:, 0:128], in0=PT1[sb][:, 0:128], in1=trib,
                                                                   op=ALU.mult), reads=[('pt1', sb)], writes=[('pt1', sb)])
                            t.op('pool', lambda e: e.tensor_tensor(out=PT2[sb][:, 0:128], in0=PT2[sb][:, 0:128], in1=trib,
                                                                   op=ALU.mult), reads=[('pt2', sb)], writes=[('pt2', sb)])

                    def pv_stage(i):
                        r, off, n, Ksrc, Vsrc, bias, rkeys = blk(i)
                        sb = i % 2
                        st_ = (i == 0)
                        sp_ = (i == nblk - 1)
                        t.op('pe', mm(PS[:, 4, off:512], Vsrc, PT1[sb][:, 0:n], st_, sp_), reads=[('pt1', sb)] + rkeys,
                             writes=[('ps', 4)], inc=False)
                        t.op('pe', mm(PS[:, 6, off:512], onesb, PT1[sb][:, 0:n], st_, sp_), reads=[('pt1', sb)],
                             writes=[('ps', 6)])
                        t.op('pe', mm(PS[:, 5, off:512], Vsrc, PT2[sb][:, 0:n], st_, sp_), reads=[('pt2', sb)] + rkeys,
                             writes=[('ps', 5)], inc=False)
                        t.op('pe', mm(PS[:, 7, off:512], onesb, PT2[sb][:, 0:n], st_, sp_), reads=[('pt2', sb)],
                             writes=[('ps', 7)])

                    s_stage(0)
                    for i in range(nblk):
                        if i + 1 < nblk:
                            s_stage(i + 1)
                        pv_stage(i)
                    t.op('dve', lambda e: e.reciprocal(out=FR1, in_=PS[:, 6, :]), reads=[('ps', 6)], writes=['fr1'])
                    t.op('dve', lambda e: e.reciprocal(out=FR2, in_=PS[:, 7, :]), reads=[('ps', 7)], writes=['fr2'])
                    t.op('dve', lambda e: e.tensor_scalar(out=FR2, in0=FR2, scalar1=neglam[:, 0:1], scalar2=None, op0=ALU.mult),
                         reads=['fr2'], writes=['fr2'])
                    t.op('dve', lambda e: e.tensor_tensor(out=FO, in0=PS[:, 4, :], in1=FR1, op=ALU.mult),
                         reads=[('ps', 4), 'fr1'], writes=['fo'])
                    t.op('dve', lambda e: e.tensor_tensor(out=FO2, in0=PS[:, 5, :], in1=FR2, op=ALU.mult),
                         reads=[('ps', 5), 'fr2'], writes=['fo2'])
                    t.op('dve', lambda e: e.tensor_tensor(out=FO, in0=FO, in1=FO2, op=ALU.add),
                         reads=['fo', 'fo2'], writes=['fo'])
                    t.op('pool', lambda e: e.tensor_tensor(out=FSQ, in0=FO, in1=FO, op=ALU.mult), reads=['fo'], writes=['fsq'])
                    t.op('pe', mm(PS[:, 6, :], onesb, FSQ, True, True), reads=['fsq'], writes=[('ps', 6)])
                    t.op('act', lambda e: e.activation(out=FRS, in_=PS[:, 6, :], func=AF.Sqrt, scale=1.0 / 128, bias=eps5),
                         reads=[('ps', 6)], writes=['frs'])
                    t.op('dve', lambda e: e.reciprocal(out=FRS, in_=FRS), reads=['frs'], writes=['frs'])
                    t.op('dve', lambda e: e.scalar_tensor_tensor(out=hT[:, h, q0:q0 + 512], in0=FO, scalar=gsub[:, 0:1],
                                                                 in1=FRS, op0=ALU.mult, op1=ALU.mult),
                         reads=['fo', 'frs'], writes=[('oT', h, qs)])
            ckpt(12)

            Kr = [V_("R3", i * 2048, BF16, 1024) for i in range(4)]
            Vr = [V_("R3", 8192 + i * 2048, BF16, 1024) for i in range(4)]
            KTs2 = [V_("R3", 16384 + i * 2048, BF16, 1024) for i in range(2)]
            PTd = [V_("R3", 20480 + i * 2304, BF16, 1152) for i in range(2)]
            t.op('dve', lambda e: e.memset(QBLK.rearrange("p b h e -> p (b h e)"), 0.0), writes=['qblk'])
            for h in range(H):
                t.op('dve', lambda e: e.tensor_copy(out=QBLK[0:64, :, h, 0:4],
                                                    in_=QTa[0:64, h, SOFF:T].rearrange("p (b q) -> p b q", b=NB)),
                     reads=['qblk'], writes=['qblk'])
                t.op('dve', lambda e: e.tensor_copy(out=QBLK[64:128, :, h, 4:8],
                                                    in_=QTb[64:128, h, SOFF:T].rearrange("p (b q) -> p b q", b=NB)),
                     reads=['qblk'], writes=['qblk'])
            t.op('dve', lambda e: e.memset(PTd[0], 0.0), writes=[('ptsall', 0)])
            t.op('dve', lambda e: e.memset(PTd[1], 0.0), writes=[('ptsall', 1)])
            t.op('dve', lambda e: e.memset(PTsumH, 0.0), writes=['ptsumh'])
            t.op('dve', lambda e: e.memset(PTsumL, 0.0), writes=['ptsuml'])
            t.op('dve', lambda e: e.memset(PTnb, 0.0), writes=['ptnb'])
            t.op('dve', lambda e: e.memset(Vnew[0], 0.0), writes=[('vn', 0)])
            t.op('dve', lambda e: e.memset(Vnew[1], 0.0), writes=[('vn', 1)])
            t.op('dve', lambda e: e.memset(OSAMP, 0.0), writes=['osamp'])

            def stage_T(g):
                bb, j = g // 16, g % 16
                col = bb * 16 + j
                kp = Kr[g % 4]
                vp = Vr[g % 4]
                if j == 0:
                    vn_ = Vnew[bb % 2]
                    t.dma('sp', lambda e: e.dma_start(out=vn_[0:4, :], in_=Vt[16 + 4 * bb:20 + 4 * bb, 8, :]),
                          writes=[('vn', bb % 2)])
                t.dma('pool', lambda e: e.indirect_dma_start(
                    out=kp, out_offset=None, in_=ck,
                    in_offset=bass.IndirectOffsetOnAxis(ap=idx[:, col:col + 1], axis=0)), writes=[('kr', g % 4)])
                t.dma('pool', lambda e: e.indirect_dma_start(
                    out=vp, out_offset=None, in_=cv,
                    in_offset=bass.IndirectOffsetOnAxis(ap=idx[:, col:col + 1], axis=0)), writes=[('vr', g % 4)])
                for h in range(H):
                    t.op('pe', lambda e: e.transpose(out=PSb(6)[:, h * 128:(h + 1) * 128],
                                                     in_=kp[:, h * 128:(h + 1) * 128], identity=identb),
                         reads=[('kr', g % 4)], writes=[('ps', 6)], inc=(h == H - 1))
                t.op('act', lambda e: e.copy(out=KTs2[g % 2], in_=PSb(6)), reads=[('ps', 6)], writes=[('kts', g % 2)])
                sbk = j % 2
                for h in range(H):
                    c_ = ((j // 2) * 8 + h) * 8
                    t.op('pe', mm(PS[:, sbk, c_:c_ + 8], KTs2[g % 2][:, h * 128:(h + 1) * 128], QBLK[:, bb, h, :],
                                  (j // 2 == 0 and h == 0), False, skip_group_check=True),
                         reads=[('kts', g % 2), 'qblk'], writes=[('ps', sbk)], inc=(h == H - 1))
                c0_ = (j // 2) * 64
                t.op('act', lambda e: e.activation(out=PTd[bb % 2][:, j * 64:(j + 1) * 64], in_=PS[:, sbk, c0_:c0_ + 64],
                                                   func=AF.Exp, scale=SCALE),
                     reads=[('ps', sbk)], writes=[('pts', bb % 2, j)])

            def stage_PV(g):
                bb, j = g // 16, g % 16
                vp = Vr[g % 4]
                for h in range(H):
                    for half in range(2):
                        ob = 2 + half * 2 + h // 4
                        first = (j == 0 and h % 4 == 0)
                        c_ = (j * 8 + h) * 8 + half * 4
                        t.op('pe', mm(PS[:, ob, (h % 4) * 128:(h % 4 + 1) * 128], PTd[bb % 2][:, c_:c_ + 128],
                                      vp[:, h * 128:(h + 1) * 128], first, False, skip_group_check=True),
                             reads=[('pts', bb % 2, j), ('ptsall', bb % 2), ('vr', g % 4)], writes=[('ps', ob)],
                             inc=(h == 7 and half == 1))

            def tail(bb):
                s0 = SOFF + 4 * bb
                vn = Vnew[bb % 2]
                PTs_ = PTd[bb % 2]
                rkall = [('pts', bb % 2, j) for j in range(16)]
                t.op('dve', lambda e: e.tensor_reduce(out=PTsum, in_=PTs_[:, 0:1024].rearrange("p (j e) -> p e j", j=16),
                                                      axis=AX.X, op=ALU.add), reads=rkall, writes=['ptsum'])
                t.op('dve', lambda e: e.tensor_copy(out=PTsumH[:, 0:64], in_=PTsum), reads=['ptsum'], writes=['ptsumh'])
                t.op('dve', lambda e: e.tensor_tensor(out=PTsumL[:, 0:64], in0=PTsum, in1=PTsumH[:, 0:64], op=ALU.subtract),
                     reads=['ptsum', 'ptsumh'], writes=['ptsuml'])
                for h in range(H):
                    t.op('pe', mm(PS[:, 7, h * 8:(h + 1) * 8], KTflat[:, h * T + s0:h * T + s0 + 128], QBLK[:, bb, h, :],
                                  h == 0, h == 7, skip_group_check=True), reads=['qblk'], writes=[('ps', 7)], inc=(h == 7))
                t.op('act', lambda e: e.activation(out=PTn32[0:4, :], in_=PS[0:4, 7, 0:64], func=AF.Exp, scale=SCALE),
                     reads=[('ps', 7)], writes=['ptn32'])
                t.op('dve', lambda e: e.tensor_tensor(out=PTn32[0:4, :], in0=PTn32[0:4, :], in1=smask[0:4, :], op=ALU.mult),
                     reads=['ptn32'], writes=['ptn32'])
                t.op('dve', lambda e: e.tensor_copy(out=PTnb[0:4, 0:64], in_=PTn32[0:4, :]), reads=['ptn32'], writes=['ptnb'])
                for h in range(H):
                    for half in range(2):
                        ob = 2 + half * 2 + h // 4
                        c_ = h * 8 + half * 4
                        t.op('pe', mm(PS[:, ob, (h % 4) * 128:(h % 4 + 1) * 128], PTnb[:, c_:c_ + 128],
                                      vn[:, h * 128:(h + 1) * 128], False, True, skip_group_check=True),
                             reads=['ptnb', ('vn', bb % 2)], writes=[('ps', ob)], inc=(h == 7 and half == 1))
                for h in range(H):
                    for half in range(2):
                        c_ = 64 + half * 8 + h
                        e0 = h * 8 + half * 4
                        first = (h == 0 and half == 0)
                        t.op('pe', mm(PS[:, 7, c_:c_ + 1], PTsumH[:, e0:e0 + 128], onesb[:, 0:1], first, False,
                                      skip_group_check=True), reads=['ptsumh', 'onesb', 'ptn32'], writes=[('ps', 7)], inc=False)
                        t.op('pe', mm(PS[:, 7, c_:c_ + 1], PTsumL[:, e0:e0 + 128], onesb[:, 0:1], False, False,
                                      skip_group_check=True), reads=['ptsuml'], writes=[('ps', 7)], inc=False)
                        t.op('pe', mm(PS[:, 7, c_:c_ + 1], PTnb[:, e0:e0 + 128], onesb[:, 0:1], False,
                                      (h == 7 and half == 1), skip_group_check=True), reads=['ptnb'], writes=[('ps', 7)],
                             inc=(h == 7 and half == 1))
                t.op('dve', lambda e: e.reciprocal(out=SSM[0:4, 0:16], in_=PS[0:4, 7, 64:80]), reads=[('ps', 7)], writes=['ssm'])
                t.op('dve', lambda e: e.tensor_scalar(out=SSM[0:4, 8:16], in0=SSM[0:4, 8:16], scalar1=neglam[0:4, 0:1],
                                                      scalar2=None, op0=ALU.mult), reads=['ssm'], writes=['ssm'])
                for k in range(2):
                    t.op('dve', lambda e: e.tensor_tensor(
                        out=SO[0:4, k * 512:(k + 1) * 512].rearrange("p (h d) -> p h d", h=4),
                        in0=PS[0:4, 2 + k, :].rearrange("p (h d) -> p h d", h=4),
                        in1=SSM[0:4, 4 * k:4 * k + 4].unsqueeze(2).to_broadcast([4, 4, 128]), op=ALU.mult),
                        reads=[('ps', 2 + k), 'ssm'], writes=[('so', k)])
                    t.op('dve', lambda e: e.tensor_tensor(
                        out=SO2[0:4, k * 512:(k + 1) * 512].rearrange("p (h d) -> p h d", h=4),
                        in0=PS[0:4, 4 + k, :].rearrange("p (h d) -> p h d", h=4),
                        in1=SSM[0:4, 8 + 4 * k:12 + 4 * k].unsqueeze(2).to_broadcast([4, 4, 128]), op=ALU.mult),
                        reads=[('ps', 4 + k), 'ssm'], writes=[('so2', k)])
                t.op('dve', lambda e: e.tensor_tensor(out=SO[0:4, :], in0=SO[0:4, :], in1=SO2[0:4, :], op=ALU.add),
                     reads=[('so', 0), ('so', 1), ('so2', 0), ('so2', 1)], writes=['sof'])
                t.op('dve', lambda e: e.tensor_tensor(out=SSQ[0:4, :], in0=SO[0:4, :], in1=SO[0:4, :], op=ALU.mult),
                     reads=['sof'], writes=['ssq'])
                t.op('dve', lambda e: e.tensor_reduce(out=SSM[0:4, 16:24], in_=SSQ[0:4, :].rearrange("p (h d) -> p h d", h=8),
                                                      axis=AX.X, op=ALU.add), reads=['ssq'], writes=['ssm2'])
                t.op('act', lambda e: e.activation(out=SSM[0:4, 24:32], in_=SSM[0:4, 16:24], func=AF.Sqrt, scale=1.0 / 128,
                                                   bias=eps5[0:4, :]), reads=['ssm2'], writes=['ssm3'])
                t.op('dve', lambda e: e.reciprocal(out=SSM[0:4, 32:40], in_=SSM[0:4, 24:32]), reads=['ssm3'], writes=['ssm4'])
                t.op('dve', lambda e: e.tensor_tensor(
                    out=SSQ[0:4, :].rearrange("p (h d) -> p h d", h=8), in0=SO[0:4, :].rearrange("p (h d) -> p h d", h=8),
                    in1=SSM[0:4, 32:40].unsqueeze(2).to_broadcast([4, 8, 128]), op=ALU.mult),
                    reads=['sof', 'ssm4', 'ssq'], writes=['ssq'])
                t.op('dve', lambda e: e.tensor_tensor(
                    out=SSQ[0:4, :].rearrange("p (h d) -> p h d", h=8), in0=SSQ[0:4, :].rearrange("p (h d) -> p h d", h=8),
                    in1=gsubrow[0:4, :].unsqueeze(1).to_broadcast([4, 8, 128]), op=ALU.mult),
                    reads=['ssq', 'gsubrow'], writes=['ssq'])
                t.dma('sp', lambda e: e.dma_start(out=OSAMP[4 * bb:4 * bb + 4, :], in_=SSQ[0:4, :]), reads=['ssq'],
                      writes=['osamp'])

            NG = NB * 16
            stage_T(0)
            for g in range(NG):
                if g + 1 < NG:
                    stage_T(g + 1)
                stage_PV(g)
                if g % 16 == 15:
                    tail(g // 16)
            for g in range(2):
                for j in range(4):
                    h = g * 4 + j
                    t.op('pe', lambda e: e.transpose(out=PS[:, g, j * 128:(j + 1) * 128], in_=OSAMP[:, h * 128:(h + 1) * 128],
                                                     identity=identf),
                         reads=['osamp'], writes=[('ps', g)], inc=(j == 3))
                t.op('act', lambda e: e.copy(out=hT[:, g * 4:g * 4 + 4, SOFF:T],
                                             in_=PS[:, g, :].rearrange("p (j n) -> p j n", j=4)[:, :, 0:64]),
                     reads=[('ps', g)], writes=[('oTs', g)])
            ckpt(13)

            t.dma('sp', lambda e: e.dma_start(out=xT.rearrange("p k n -> p (k n)"), in_=xspill), writes=['xTall'])
            woutv = wout.rearrange("(k p) f -> p k f", p=128)

            def load_wo(cb, b):
                t.dma('pool', lambda e: e.dma_start(out=WIN[b], in_=woutv[:, :, cb * 512:(cb + 1) * 512]), writes=[('win', b)])
            load_wo(0, 0)
            for cb in range(4):
                b = cb % 2
                if cb + 1 < 4:
                    load_wo(cb + 1, (cb + 1) % 2)
                for c in range(4):
                    dc = cb * 4 + c
                    for bi, (c0, n) in enumerate(TBS):
                        bank = pcnt[0] % 4
                        pcnt[0] += 1
                        for kc in range(16):
                            rhs = hT[:, kc, c0:c0 + n] if kc < 8 else zT[:, kc - 8, c0:c0 + n]
                            t.op('pe', mm(PS[:, bank, 0:n], WIN[b][:, kc, c * 128:(c + 1) * 128], rhs, kc == 0, kc == 15),
                                 reads=[('win', b)], writes=[('ps', bank)], inc=(kc == 15))
                        t.op('dve', lambda e: e.tensor_tensor(out=xT[:, dc, c0:c0 + n], in0=PS[:, bank, 0:n],
                                                              in1=xT[:, dc, c0:c0 + n], op=ALU.add),
                             reads=[('ps', bank), 'xTall', ('xT', dc, bi)], writes=[('xT', dc, bi)])
            ckpt(14)

            norm(2, TBS)
            ckpt(15)
            ffn(w2g, w2u, w2d, TBS)
            ckpt(16)
            norm(3, TBS, out_f32=True)
            ckpt(17)
            for i, (tt, c0, n) in enumerate(ALL_TILES):
                st = XS[i % 2]
                for g in range(4):
                    bank = g
                    for j in range(4):
                        kc = g * 4 + j
                        t.op('pe', lambda e: e.transpose(out=PS[0:n, bank, j * 128:(j + 1) * 128], in_=xT[:, kc, c0:c0 + n],
                                                         identity=identf),
                             writes=[('ps', bank)], inc=(j == 3))
                    eng = 'act' if g % 2 == 0 else 'dve'
                    if eng == 'act':
                        t.op('act', lambda e: e.copy(out=st[0:n, g * 512:(g + 1) * 512], in_=PS[0:n, bank, :]),
                             reads=[('ps', bank)], writes=[('xs', i % 2, g)])
                    else:
                        t.op('dve', lambda e: e.tensor_copy(out=st[0:n, g * 512:(g + 1) * 512], in_=PS[0:n, bank, :]),
                             reads=[('ps', bank)], writes=[('xs', i % 2, g)])
                rk = [('xs', i % 2, g) for g in range(4)]
                if tt < 8:
                    t.dma('sp', lambda e: e.dma_start(out=y_p[tt * 128:(tt + 1) * 128, :], in_=st[0:128, :]), reads=rk)
                else:
                    t.dma('sp', lambda e: e.dma_start(out=y_s, in_=st[16:80, :]), reads=rk)
            ckpt(18)

        try:
            body()
        except _Stop:
            pass
    return nc


_NC_CACHE = {}


def kernel(x_prompt, x_sample, cache_k, cache_v, state_pool, page_table,
           ffn1_norm, ffn1_w_gate, ffn1_w_up, ffn1_w_down,
           mix_norm, w_in, lambda_q1, lambda_k1, lambda_q2, lambda_k2,
           subln_gain, w_pool, pool_scale, w_out,
           ffn2_norm, ffn2_w_gate, ffn2_w_up, ffn2_w_down, final_norm):
    f = np.float32
    a = lambda v: np.ascontiguousarray(np.asarray(v))
    x_prompt = a(x_prompt); x_sample = a(x_sample)
    ckf = a(cache_k).reshape(NPHYS * 128, 1024)
    cvf = a(cache_v).reshape(NPHYS * 128, 1024)
    state_pool = a(state_pool); page_table = a(page_table).astype(np.int32)
    if "nc" not in _NC_CACHE:
        _NC_CACHE["nc"] = build_nc()
    nc = _NC_CACHE["nc"]

    def colmajor(v):
        return a(v).reshape(16, 128).T

    gains = np.concatenate([colmajor(ffn1_norm[0]), colmajor(mix_norm[0]), colmajor(ffn2_norm[0]),
                            colmajor(final_norm)], axis=1).astype(f)
    pscale = a(pool_scale)[0].reshape(8, 128).T.astype(f)
    subcol = a(subln_gain)[0].reshape(128, 1).astype(f)
    subrow = np.broadcast_to(a(subln_gain)[0][None, :], (128, 128)).astype(f)
    lams = np.broadcast_to(np.concatenate([a(lambda_q1)[0], a(lambda_k1)[0], a(lambda_q2)[0], a(lambda_k2)[0]])[None, :],
                           (128, 256)).astype(f)
    ident = np.eye(128, dtype=f)
    tri = (np.arange(128)[:, None] <= np.arange(128)[None, :]).astype(f)
    smask = np.zeros((128, 64), f)
    for k in range(4):
        for e in range(64):
            if k <= e % 4:
                smask[k, e] = 1.0
    shared = {
        "ck": ckf, "cv": cvf,
        "w1g": a(ffn1_w_gate)[0], "w1u": a(ffn1_w_up)[0], "w1d": a(ffn1_w_down)[0],
        "w2g": a(ffn2_w_gate)[0], "w2u": a(ffn2_w_up)[0], "w2d": a(ffn2_w_down)[0],
        "win": a(w_in)[0], "wpool": a(w_pool)[0], "wout": a(w_out)[0],
        "c_ident": ident, "c_tri": tri, "c_smask": smask, "c_gains": a(gains), "c_pscale": a(pscale),
        "c_subcol": subcol, "c_subrow": a(subrow), "c_lams": a(lams),
    }
    in_maps = []
    for c in range(8):
        b, h = c // 2, c % 2
        xm = np.zeros((T, D), f)
        xm[0:TP] = x_prompt[b, h * TP:(h + 1) * TP]
        if h == 1:
            xm[TP:TP + TH] = x_prompt[b, TP - TH:TP]
        xm[SOFF:T] = x_sample[NB * c:NB * (c + 1)].reshape(TS, D)
        xp = x_prompt[b, 0:TP] if h == 1 else np.zeros((TP, D), f)
        invc = np.zeros((128, 4, 16), f)
        for g in range(4):
            w = 2 ** (g + 1)
            for tkn in range(16):
                pos = h * TP + tkn
                invc[:, g, tkn] = 1.0 / min(w, pos + 1)
        m = dict(shared)
        m.update({
            "xmain": xm, "xprev": a(xp),
            "state": a(state_pool[0, NB * c:NB * (c + 1)]),
            "ptab": a(page_table[NB * c:NB * (c + 1)].reshape(1, NB * 16)),
            "c_invc": invc.reshape(128, 64),
            "c_pbias": np.full((128, 1), 0.0 if h == 1 else -30000.0, f),
        })
        in_maps.append(m)
    res = run_bass_kernel_spmd(nc, in_maps, core_ids=list(range(8)))
    R = res.results
    y_prompt = np.zeros((4, 2048, D), f)
    y_sample = np.zeros((128, 4, D), f)
    k_prompt = np.zeros((1, 4, 2048, 8, 128), f)
    v_prompt = np.zeros((1, 4, 2048, 8, 128), f)
    pool_prompt = np.zeros((1, 4, 15, 1024), f)
    k_sample = np.zeros((1, 128, 4, 8, 128), f)
    v_sample = np.zeros((1, 128, 4, 8, 128), f)
    pool_sample = np.zeros((1, 128, 15, 1024), f)
    for c in range(8):
        b, h = c // 2, c % 2
        r = R[c]
        y_prompt[b, h * TP:(h + 1) * TP] = r["y_p"]
        y_sample[NB * c:NB * (c + 1)] = r["y_s"].reshape(NB, 4, D)
        k_prompt[0, b, h * TP:(h + 1) * TP] = r["k_p"].reshape(TP, 8, 128)
        v_prompt[0, b, h * TP:(h + 1) * TP] = r["v_p"].reshape(TP, 8, 128)
        if h == 1:
            pool_prompt[0, b] = r["pool_p"]
        k_sample[0, NB * c:NB * (c + 1)] = r["k_s"].reshape(NB, 4, 8, 128)
        v_sample[0, NB * c:NB * (c + 1)] = r["v_s"].reshape(NB, 4, 8, 128)
        pool_sample[0, NB * c:NB * (c + 1)] = r["pool_s"]
    return (y_prompt, y_sample, k_prompt, v_prompt, pool_prompt, k_sample, v_sample, pool_sample)
```

```python
import numpy as np
import concourse.bass as bass
import concourse.mybir as mybir
from concourse.bass_utils import run_bass_kernel_spmd

F32 = mybir.dt.float32
BF16 = mybir.dt.bfloat16
I32 = mybir.dt.int32
AF = mybir.ActivationFunctionType
ALU = mybir.AluOpType
AX = mybir.AxisListType

D = 2048
DFF = 5504
H = 8
TP = 1024
TH = 16
TS = 64
T = TP + TH + TS
SOFF = TP + TH
NB = 16
NPHYS = 2560
TBS = [(0, 512), (512, 512), (1024, 80)]
PREV_TBS = [(0, 512), (512, 512)]
LAM_INIT = 0.8 - 0.6 * 1.0
SCALE = 0.125
NDS = 16
STOP = 0


class Trk:
    def __init__(self, nc, sem_iter):
        self.nc = nc
        self.sem_iter = sem_iter
        self.eng = {'pe': nc.tensor, 'act': nc.scalar, 'dve': nc.vector, 'pool': nc.gpsimd, 'sp': nc.sync}
        self.sem = {}
        self.cnt = {}
        for e in ['pe', 'act', 'dve', 'pool']:
            self.sem[e] = next(sem_iter)
            self.cnt[e] = 0
        self.lastw = {}
        self.readers = {}
        self.waited = {}
        self.pending = {e: ([], []) for e in self.sem}
        self.dma_sems = [next(sem_iter) for _ in range(NDS)]
        self.dma_cnt = [0] * NDS
        self.dma_prev = [None] * NDS
        self.dma_i = 0

    def _wait(self, eng, tok):
        sem, val, src = tok
        if src == eng and eng == 'pe':
            return
        key = (eng, id(sem))
        if self.waited.get(key, 0) >= val:
            return
        self.eng[eng].wait_ge(sem, val)
        self.waited[key] = val

    def _deps(self, eng, reads, writes):
        for k in reads:
            w = self.lastw.get(k)
            if w:
                self._wait(eng, w)
        for k in writes:
            w = self.lastw.get(k)
            if w:
                self._wait(eng, w)
            for r in self.readers.get(k, {}).values():
                self._wait(eng, r)

    def _commit(self, tok, reads, writes):
        for k in reads:
            self.readers.setdefault(k, {})[tok[2]] = tok
        for k in writes:
            self.lastw[k] = tok
            self.readers[k] = {}

    def op(self, eng, fn, reads=(), writes=(), inc=True):
        self._deps(eng, reads, writes)
        ins = fn(self.eng[eng])
        pr, pw = self.pending[eng]
        if not inc:
            pr.extend(reads)
            pw.extend(writes)
            return
        if self.cnt[eng] >= 30000:
            self.sem[eng] = next(self.sem_iter)
            self.cnt[eng] = 0
        self.cnt[eng] += 1
        ins.then_inc(self.sem[eng], 1)
        tok = (self.sem[eng], self.cnt[eng], eng)
        self._commit(tok, list(reads) + pr, list(writes) + pw)
        self.pending[eng] = ([], [])

    def dma(self, q, fn, reads=(), writes=()):
        self._deps(q, reads, writes)
        slot = self.dma_i % NDS
        self.dma_i += 1
        prev = self.dma_prev[slot]
        if prev:
            self._wait(q, prev)
        if self.dma_cnt[slot] >= 30000:
            self.dma_sems[slot] = next(self.sem_iter)
            self.dma_cnt[slot] = 0
        ins = fn(self.eng[q])
        self.dma_cnt[slot] += 16
        ins.then_inc(self.dma_sems[slot], 16)
        tok = (self.dma_sems[slot], self.dma_cnt[slot], ('dma', slot))
        self.dma_prev[slot] = tok
        self._commit(tok, reads, writes)

    def barrier(self):
        for e in self.pending:
            assert self.pending[e] == ([], []), e
        toks = [(self.sem[e], self.cnt[e], e) for e in self.sem if self.cnt[e] > 0]
        toks += [tk for tk in self.dma_prev if tk]
        for e in ['pe', 'act', 'dve', 'pool', 'sp']:
            for tk in toks:
                if tk[2] != e:
                    sem, val, src = tk
                    key = (e, id(sem))
                    if self.waited.get(key, 0) >= val:
                        continue
                    self.eng[e].wait_ge(sem, val)
                    self.waited[key] = val
        self.lastw = {}
        self.readers = {}


def build_nc():
    nc = bass.Bass("TRN2", target_bir_lowering=False)

    def din(name, shape, dt=F32):
        return nc.dram_tensor(name, list(shape), dt, kind="ExternalInput").ap()

    def dout(name, shape, dt=F32):
        return nc.dram_tensor(name, list(shape), dt, kind="ExternalOutput").ap()

    xmain = din("xmain", [T, D])
    xprev = din("xprev", [TP, D])
    ck = din("ck", [NPHYS * 128, 1024]) if True else None
    cv = din("cv", [NPHYS * 128, 1024])
    state = din("state", [NB, 15, 1024])
    ptab = din("ptab", [1, NB * 16], I32)
    w1g = din("w1g", [D, DFF]); w1u = din("w1u", [D, DFF]); w1d = din("w1d", [DFF, D])
    w2g = din("w2g", [D, DFF]); w2u = din("w2u", [D, DFF]); w2d = din("w2d", [DFF, D])
    win = din("win", [D, 4096])
    wpool = din("wpool", [4, 256, 256])
    wout = din("wout", [D, D])
    c_ident = din("c_ident", [128, 128])
    c_tri = din("c_tri", [128, 128])
    c_smask = din("c_smask", [128, 64])
    c_invc = din("c_invc", [128, 64])
    c_pbias = din("c_pbias", [128, 1])
    c_gains = din("c_gains", [128, 64])
    c_pscale = din("c_pscale", [128, 8])
    c_subcol = din("c_subcol", [128, 1])
    c_subrow = din("c_subrow", [128, 128])
    c_lams = din("c_lams", [128, 256])

    y_p = dout("y_p", [TP, D]); y_s = dout("y_s", [TS, D])
    k_p = dout("k_p", [TP, 1024]); v_p = dout("v_p", [TP, 1024])
    pool_p = dout("pool_p", [15, 1024])
    k_s = dout("k_s", [TS, 1024]); v_s = dout("v_s", [TS, 1024])
    pool_s = dout("pool_s", [NB, 15, 1024])

    xspill = nc.dram_tensor("xspill", [128, 16 * T], F32, kind="Internal").ap()
    kvspill = nc.dram_tensor("kvspill", [128, 16384], BF16, kind="Internal").ap()

    R1, R2, RZ, R3, R4, RC, R5 = 74240, 35328, 17664, 49152, 25600, 6144, 4096
    offs = {}
    o = 0
    for nm, sz in [("R1", R1), ("R2", R2), ("RZ", RZ), ("R3", R3), ("R4", R4), ("RC", RC), ("R5", R5)]:
        offs[nm] = o
        o += sz
    TOTAL = o

    import contextlib
    with contextlib.ExitStack() as es:
        A = es.enter_context(nc.sbuf_tensor("arena", [128, TOTAL // 2], BF16))
        PS = es.enter_context(nc.psum_tensor("ps", [128, 8, 512], F32))
        sems = []
        for i in range(90):
            sems.append(es.enter_context(nc.semaphore(f"s{i}")))
        es.enter_context(nc.Block())
        t = Trk(nc, iter(sems))

        def V_(region, boff, dt, nelem, pat=None, **kw):
            b0 = (offs[region] + boff) // 2
            nb = nelem * (4 if dt in (F32, I32) else 2) // 2
            ap = A[:, b0:b0 + nb]
            if dt != BF16:
                ap = ap.bitcast(dt)
            if pat:
                ap = ap.rearrange(pat, **kw)
            return ap

        def PSb(bank):
            return PS[:, bank, :].bitcast(BF16)

        xT = V_("R1", 0, F32, 16 * T, "p (k n) -> p k n", k=16)
        QTa = V_("R1", 0, BF16, 8 * T, "p (k n) -> p k n", k=8)
        QTb = V_("R1", 17664, BF16, 8 * T, "p (k n) -> p k n", k=8)
        KT = V_("R1", 35328, BF16, 8 * T, "p (k n) -> p k n", k=8)
        KTflat = V_("R1", 35328, BF16, 8 * T + 256)
        Vt = V_("R1", 53504, BF16, 9 * 1024, "p (k n) -> p k n", k=9)
        stT = V_("R3", 36864, F32, 8 * 240, "p (k n) -> p k n", k=8)
        KTp_tmp = V_("R1", 0, BF16, 8 * 1024, "p (k n) -> p k n", k=8)
        Vp_tmp = V_("R1", 16384, BF16, 8 * 1024, "p (k n) -> p k n", k=8)
        hT = V_("R2", 0, BF16, 16 * T, "p (k n) -> p k n", k=16)
        zT = V_("RZ", 0, BF16, 8 * T, "p (k n) -> p k n", k=8)
        WB = [V_("R3", b * 24576, BF16, 12288) for b in range(2)]
        WG = [w[:, 0:4096].rearrange("p (k n) -> p k n", k=16) for w in WB]
        WU = [w[:, 4096:8192].rearrange("p (k n) -> p k n", k=16) for w in WB]
        WD = [w[:, 8192:12288].rearrange("p (k n) -> p k n", k=2) for w in WB]
        WIN = [V_("R3", b * 16384, BF16, 8192, "p (k n) -> p k n", k=16) for b in range(2)]
        WP = V_("R3", 32768, BF16, 2048, "p (g k n) -> p g k n", g=4, k=2)
        SQ = V_("RZ", 0, BF16, 16 * 512, "p (k n) -> p k n", k=16)
        KTprev = V_("R3", 0, BF16, 8 * 1024, "p (k n) -> p k n", k=8)
        Vprev = V_("R3", 16384, BF16, 8 * 1024, "p (k n) -> p k n", k=8)
        Vb = V_("R3", 0, BF16, 16 * 1024, "p (k n) -> p k n", k=16)
        Kp = [V_("R3", 32768 + i * 2048, BF16, 1024) for i in range(2)]
        KTs = [V_("R3", 36864 + i * 2048, BF16, 1024) for i in range(2)]
        PTs = V_("R3", 40960, BF16, 1152)
        Vnew = [V_("R3", 43264 + i * 2048, BF16, 1024) for i in range(2)]
        PTsumH = V_("R3", 47360, BF16, 192)
        PTsumL = V_("R3", 47744, BF16, 192)
        PTnb = V_("R3", 48128, BF16, 192)
        PTn32 = V_("R3", 48512, F32, 64)
        PTsum = V_("R3", 48768, F32, 64)
        XS = [V_("R4", i * 8192, F32, 2048) for i in range(2)]
        RS = V_("R4", 16384, F32, 512)
        RSTD = V_("R4", 18432, F32, 512)
        AT = [V_("R4", 20480 + i * 2048, BF16, 1024, "p (k n) -> p k n", k=2) for i in range(2)]
        SG = [V_("R5", i * 2048, F32, 512) for i in range(2)]
        STG = [V_("R4", i * 2048, F32, 512) for i in range(2)]
        EXP_ = [V_("R4", 4096 + i * 4160, F32, 1040) for i in range(3)]
        EXS = [V_("R4", 16576 + i * 1216, F32, 304, "p (b r) -> p b r", b=16) for i in range(3)]
        DT = V_("R4", 20224, BF16, 2 * T, "p (k n) -> p k n", k=2)
        TMP16 = V_("R4", 24640, F32, 16)
        PT1 = [V_("R4", i * 1024, BF16, 512) for i in range(2)]
        PT2 = [V_("R4", 2048 + i * 1024, BF16, 512) for i in range(2)]
        FR1 = V_("R4", 4096, F32, 512)
        FR2 = V_("R4", 6144, F32, 512)
        FO = V_("R4", 8192, F32, 512)
        FO2 = V_("R4", 10240, F32, 512)
        FRS = V_("R4", 12288, F32, 512)
        FSQ = V_("R4", 14336, BF16, 512)
        OSAMP = V_("R4", 16384, F32, 1024)
        SO = V_("R4", 20480, F32, 1024)
        SO2 = V_("R4", 4096, F32, 1024)
        SSQ = V_("R4", 8192, F32, 1024)
        SSM = V_("R4", 24576, F32, 64)
        identf = V_("RC", 0, F32, 128)
        identb = V_("RC", 512, BF16, 128)
        onesb = V_("RC", 768, BF16, 128)
        trib = V_("RC", 1024, BF16, 128)
        gains = V_("RC", 1280, F32, 64, "p (g k) -> p g k", g=4)
        pscale = V_("RC", 1536, F32, 8)
        gsub = V_("RC", 1568, F32, 1)
        eps6 = V_("RC", 1572, F32, 1)
        eps5 = V_("RC", 1576, F32, 1)
        zerob = V_("RC", 1580, F32, 1)
        pbias = V_("RC", 1584, F32, 1)
        lam = V_("RC", 1588, F32, 1)
        neglam = V_("RC", 1592, F32, 1)
        lt = V_("RC", 1596, F32, 4)
        onesf = V_("RC", 1612, F32, 1)
        invc = V_("RC", 1616, F32, 64, "p (g k) -> p g k", g=4)
        smask = V_("RC", 1872, F32, 64)
        gsubrow = V_("RC", 2128, F32, 128)
        lams = V_("RC", 2640, F32, 256, "p (g k) -> p g k", g=4)
        ltmp = V_("RC", 3664, F32, 64)
        idx = V_("RC", 3920, I32, 256)
        iot = V_("RC", 4944, I32, 256)
        QBLK = V_("R1", 71936, BF16, NB * 64, "p (b h e) -> p b h e", b=NB, h=8)

        def mm(out, lhsT, rhs, start, stop, **kw):
            return lambda e: e.matmul(out, lhsT=lhsT, rhs=rhs, start=start, stop=stop, **kw)

        class _Stop(Exception):
            pass

        def ckpt(k):
            t.barrier()
            if STOP == k:
                raise _Stop()

        def body():
            def load_const(dst, src, key):
                t.dma('sp', lambda e: e.dma_start(out=dst, in_=src), writes=[key])

            load_const(identf, c_ident, 'identf')
            load_const(gains.rearrange("p g k -> p (g k)"), c_gains, 'gains')
            load_const(pscale, c_pscale, 'pscale')
            load_const(gsub, c_subcol, 'gsub')
            load_const(pbias, c_pbias, 'pbias')
            load_const(invc.rearrange("p g k -> p (g k)"), c_invc, 'invc')
            load_const(smask, c_smask, 'smask')
            load_const(gsubrow, c_subrow, 'gsubrow')
            load_const(lams.rearrange("p g k -> p (g k)"), c_lams, 'lams')
            t.dma('pool', lambda e: e.dma_start(out=trib, in_=c_tri), writes=['trib'])
            t.dma('pool', lambda e: e.dma_start(out=idx, in_=ptab[0, :].partition_broadcast(128)), writes=['idx'])
            t.op('pool', lambda e: e.iota(iot, pattern=[[0, 256]], base=0, channel_multiplier=1), writes=['iot'])
            t.op('pool', lambda e: e.tensor_scalar(out=idx, in0=idx, scalar1=128, scalar2=None, op0=ALU.mult),
                 reads=['idx'], writes=['idx'])
            t.op('pool', lambda e: e.tensor_tensor(out=idx, in0=idx, in1=iot, op=ALU.add),
                 reads=['idx', 'iot'], writes=['idx'])
            t.op('dve', lambda e: e.tensor_copy(out=identb, in_=identf), reads=['identf'], writes=['identb'])
            t.op('dve', lambda e: e.memset(onesb, 1.0), writes=['onesb'])
            t.op('dve', lambda e: e.memset(onesf, 1.0), writes=['onesf'])
            t.op('dve', lambda e: e.memset(eps6, 1e-6), writes=['eps6'])
            t.op('dve', lambda e: e.memset(eps5, 1e-5), writes=['eps5'])
            t.op('dve', lambda e: e.memset(zerob, 0.0), writes=['zerob'])
            t.op('dve', lambda e: e.tensor_scalar(out=gsub, in0=gsub, scalar1=1.0 - LAM_INIT, scalar2=None, op0=ALU.mult),
                 reads=['gsub'], writes=['gsub'])
            t.op('dve', lambda e: e.tensor_scalar(out=gsubrow, in0=gsubrow, scalar1=1.0 - LAM_INIT, scalar2=None, op0=ALU.mult),
                 reads=['gsubrow'], writes=['gsubrow'])
            for i in range(2):
                t.op('dve', lambda e: e.tensor_tensor(out=ltmp, in0=lams[:, 2 * i, :], in1=lams[:, 2 * i + 1, :], op=ALU.mult),
                     reads=['lams'], writes=['ltmp'])
                t.op('dve', lambda e: e.tensor_reduce(out=lt[:, i:i + 1], in_=ltmp, axis=AX.X, op=ALU.add),
                     reads=['ltmp'], writes=['lt'])
            t.op('act', lambda e: e.activation(out=lt[:, 2:4], in_=lt[:, 0:2], func=AF.Exp), reads=['lt'], writes=['lt'])
            t.op('dve', lambda e: e.tensor_tensor(out=lam, in0=lt[:, 2:3], in1=lt[:, 3:4], op=ALU.subtract),
                 reads=['lt'], writes=['lam'])
            t.op('dve', lambda e: e.tensor_scalar(out=lam, in0=lam, scalar1=LAM_INIT, scalar2=None, op0=ALU.add),
                 reads=['lam'], writes=['lam'])
            t.op('dve', lambda e: e.tensor_scalar(out=neglam, in0=lam, scalar1=-1.0, scalar2=None, op0=ALU.mult),
                 reads=['lam'], writes=['neglam'])
            t.dma('sp', lambda e: e.dma_start(out=pool_s[:, 0:11, :], in_=state[:, 4:15, :]))
            ckpt(1)

            def load_xT(src, tiles):
                for i, (r0, n, c0) in enumerate(tiles):
                    st = XS[i % 2]
                    bi = c0 // 512
                    t.dma('sp', lambda e: e.dma_start(out=st[0:n, :], in_=src[r0:r0 + n, :]), writes=[('xs', i % 2)])
                    for g in range(4):
                        bank = g % 2
                        for j in range(4):
                            kc = g * 4 + j
                            t.op('pe', lambda e: e.transpose(out=PS[:, bank, j * 128:j * 128 + n],
                                                             in_=st[0:n, kc * 128:(kc + 1) * 128],
                                                             identity=identf[0:n, 0:n]),
                                 reads=[('xs', i % 2)], writes=[('ps', bank)], inc=(j == 3))
                        eng = 'act' if g % 2 == 0 else 'dve'
                        src_ps = PS[:, bank, :].rearrange("p (j n) -> p j n", j=4)[:, :, 0:n]
                        dst = xT[:, g * 4:g * 4 + 4, c0:c0 + n]
                        if eng == 'act':
                            t.op('act', lambda e: e.copy(out=dst, in_=src_ps), reads=[('ps', bank)],
                                 writes=[('xT', kc_, bi) for kc_ in range(g * 4, g * 4 + 4)])
                        else:
                            t.op('dve', lambda e: e.tensor_copy(out=dst, in_=src_ps), reads=[('ps', bank)],
                                 writes=[('xT', kc_, bi) for kc_ in range(g * 4, g * 4 + 4)])

            def norm(gidx, tbs, out_f32=False):
                for bi, (c0, n) in enumerate(tbs):
                    for kc in range(16):
                        t.op('pool', lambda e: e.tensor_tensor(out=SQ[:, kc, 0:n], in0=xT[:, kc, c0:c0 + n],
                                                               in1=xT[:, kc, c0:c0 + n], op=ALU.mult),
                             reads=[('xT', kc, bi)], writes=[('sq', kc)])
                    for kc in range(16):
                        t.op('pe', mm(PS[:, 7, 0:n], onesb, SQ[:, kc, 0:n], kc == 0, kc == 15),
                             reads=[('sq', kc), 'onesb'], writes=[('ps', 7)], inc=(kc == 15))
                    t.op('act', lambda e: e.activation(out=RS[:, 0:n], in_=PS[:, 7, 0:n], func=AF.Sqrt,
                                                       scale=1.0 / D, bias=eps6), reads=[('ps', 7)], writes=['rs'])
                    t.op('dve', lambda e: e.reciprocal(out=RSTD[:, 0:n], in_=RS[:, 0:n]), reads=['rs'], writes=['rstd'])
                    for kc in range(16):
                        if out_f32:
                            t.op('dve', lambda e: e.scalar_tensor_tensor(out=xT[:, kc, c0:c0 + n], in0=xT[:, kc, c0:c0 + n],
                                                                         scalar=gains[:, gidx, kc:kc + 1], in1=RSTD[:, 0:n],
                                                                         op0=ALU.mult, op1=ALU.mult),
                                 reads=[('xT', kc, bi), 'rstd'], writes=[('xT', kc, bi)])
                        else:
                            t.op('dve', lambda e: e.scalar_tensor_tensor(out=hT[:, kc, c0:c0 + n], in0=xT[:, kc, c0:c0 + n],
                                                                         scalar=gains[:, gidx, kc:kc + 1], in1=RSTD[:, 0:n],
                                                                         op0=ALU.mult, op1=ALU.mult),
                                 reads=[('xT', kc, bi), 'rstd'], writes=[('hT', kc, bi)])

            def ffn(wg, wu, wd, tbs, head_only=False, skip_head=False):
                wgv = wg.rearrange("(k p) f -> p k f", p=128)
                wuv = wu.rearrange("(k p) f -> p k f", p=128)
                wdv = wd.rearrange("(c p) d -> p c d", p=128)
                NFB = 22

                def prefetch(fb):
                    nfc = 2 if fb < 21 else 1
                    f0 = fb * 256
                    nf = nfc * 128
                    b = fb % 2
                    t.dma('pool', lambda e: e.dma_start(out=WG[b][:, :, 0:nf], in_=wgv[:, :, f0:f0 + nf]), writes=[('wg', b)])
                    t.dma('pool', lambda e: e.dma_start(out=WU[b][:, :, 0:nf], in_=wuv[:, :, f0:f0 + nf]), writes=[('wu', b)])
                    t.dma('pool', lambda e: e.dma_start(out=WD[b][:, 0:nfc, :], in_=wdv[:, fb * 2:fb * 2 + nfc, :]),
                          writes=[('wd', b)])

                ntb = len(tbs)
                units = [(fb, bi) for fb in range(NFB) for bi in range(ntb)]

                def stage1(u):
                    fb, bi = units[u]
                    c0, n = tbs[bi]
                    nfc = 2 if fb < 21 else 1
                    b = fb % 2
                    ab = u % 2
                    for fc in range(nfc):
                        gb = fc % 2
                        ub = 2 + fc % 2
                        for kc in range(16):
                            t.op('pe', mm(PS[:, gb, 0:n], WG[b][:, kc, fc * 128:(fc + 1) * 128], hT[:, kc, c0:c0 + n],
                                          kc == 0, kc == 15),
                                 reads=[('wg', b), ('hT', kc, bi)], writes=[('ps', gb)], inc=(kc == 15))
                        for kc in range(16):
                            t.op('pe', mm(PS[:, ub, 0:n], WU[b][:, kc, fc * 128:(fc + 1) * 128], hT[:, kc, c0:c0 + n],
                                          kc == 0, kc == 15),
                                 reads=[('wu', b), ('hT', kc, bi)], writes=[('ps', ub)], inc=(kc == 15))
                        t.op('act', lambda e: e.activation(out=SG[fc % 2][:, 0:n], in_=PS[:, gb, 0:n], func=AF.Silu),
                             reads=[('ps', gb)], writes=[('sg', fc % 2)])
                        t.op('dve', lambda e: e.tensor_tensor(out=AT[ab][:, fc, 0:n], in0=SG[fc % 2][:, 0:n],
                                                              in1=PS[:, ub, 0:n], op=ALU.mult),
                             reads=[('sg', fc % 2), ('ps', ub)], writes=[('aT', ab, fc)])

                def stage2(u):
                    fb, bi = units[u]
                    c0, n = tbs[bi]
                    nfc = 2 if fb < 21 else 1
                    b = fb % 2
                    ab = u % 2
                    for dc in range(16):
                        yb = 4 + dc % 4
                        for fc in range(nfc):
                            t.op('pe', mm(PS[:, yb, 0:n], WD[b][:, fc, dc * 128:(dc + 1) * 128], AT[ab][:, fc, 0:n],
                                          fc == 0, fc == nfc - 1),
                                 reads=[('wd', b), ('aT', ab, fc)], writes=[('ps', yb)], inc=(fc == nfc - 1))
                        t.op('dve', lambda e: e.scalar_tensor_tensor(out=xT[:, dc, c0:c0 + n], in0=PS[:, yb, 0:n],
                                                                     scalar=0.5, in1=xT[:, dc, c0:c0 + n],
                                                                     op0=ALU.mult, op1=ALU.add),
                             reads=[('ps', yb), ('xT', dc, bi)], writes=[('xT', dc, bi)])

                if not skip_head:
                    prefetch(0)
                    prefetch(1)
                if head_only:
                    return
                stage1(0)
                for u in range(len(units)):
                    if u + 1 < len(units):
                        stage1(u + 1)
                    stage2(u)
                    fb, bi = units[u]
                    if bi == ntb - 1 and fb + 2 < NFB:
                        prefetch(fb + 2)

            winv = win.rearrange("(k p) f -> p k f", p=128)
            pcnt = [0]

            def load_win(cb, b):
                t.dma('pool', lambda e: e.dma_start(out=WIN[b], in_=winv[:, :, cb * 512:(cb + 1) * 512]), writes=[('win', b)])

            def proj_fm(b, c, tbs, evac):
                for bi, (c0, n) in enumerate(tbs):
                    bank = pcnt[0] % 4
                    pcnt[0] += 1
                    for kc in range(16):
                        t.op('pe', mm(PS[:, bank, 0:n], WIN[b][:, kc, c * 128:(c + 1) * 128], hT[:, kc, c0:c0 + n],
                                      kc == 0, kc == 15),
                             reads=[('win', b), ('hT', kc, bi)], writes=[('ps', bank)], inc=(kc == 15))
                    evac(bank, bi, c0, n)

            def proj_tm(b, tiles, evac):
                for (tt, c0, n) in tiles:
                    bank = 4 + pcnt[0] % 4
                    pcnt[0] += 1
                    bi = c0 // 512
                    for kc in range(16):
                        t.op('pe', mm(PS[0:n, bank, :], hT[:, kc, c0:c0 + n], WIN[b][:, kc, :], kc == 0, kc == 15),
                             reads=[('win', b), ('hT', kc, bi)], writes=[('ps', bank)], inc=(kc == 15))
                    evac(bank, tt, n)

            ALL_TILES = [(tt, tt * 128, 128) for tt in range(8)] + [(8, 1024, 80)]
            scnt = [0]

            def stage_out(bank, n, dsts):
                si = scnt[0] % 2
                scnt[0] += 1
                t.op('act', lambda e: e.copy(out=STG[si][0:n, :], in_=PS[0:n, bank, :]), reads=[('ps', bank)],
                     writes=[('stg', si)])
                for (dst, p0, p1) in dsts:
                    t.dma('sp', lambda e: e.dma_start(out=dst, in_=STG[si][p0:p1, :]), reads=[('stg', si)])
                return si

            ffn(w1g, w1u, w1d, PREV_TBS, head_only=True)
            load_xT(xprev, [(i * 128, 128, i * 128) for i in range(8)])
            norm(0, PREV_TBS)
            ffn(w1g, w1u, w1d, PREV_TBS, skip_head=True)
            norm(1, PREV_TBS)
            ckpt(4)
            load_win(2, 0)
            for i, cb in enumerate([2, 3, 4, 5]):
                b = i % 2
                if i + 1 < 4:
                    load_win([2, 3, 4, 5][i + 1], (i + 1) % 2)
                if cb in (2, 3):
                    for c in range(4):
                        hh = (cb - 2) * 4 + c

                        def ev(bank, bi, c0, n, hh=hh):
                            t.op('act', lambda e: e.copy(out=KTp_tmp[:, hh, c0:c0 + n], in_=PS[:, bank, 0:n]),
                                 reads=[('ps', bank)], writes=[('ktp', hh, bi)])
                        proj_fm(b, c, PREV_TBS, ev)
                else:
                    col0 = (cb - 4) * 512

                    def ev(bank, tt, n, col0=col0):
                        t.op('act', lambda e: e.copy(out=Vp_tmp[0:n, tt, col0:col0 + 512], in_=PS[0:n, bank, :]),
                             reads=[('ps', bank)], writes=[('vp', tt, col0)])
                    proj_tm(b, ALL_TILES[:8], ev)
            ckpt(5)
            t.dma('sp', lambda e: e.dma_start(out=kvspill[:, 0:8192], in_=KTp_tmp.rearrange("p k n -> p (k n)")))
            t.dma('sp', lambda e: e.dma_start(out=kvspill[:, 8192:16384], in_=Vp_tmp.rearrange("p k n -> p (k n)")))
            ckpt(6)

            tiles = [(i * 128, 128, i * 128) for i in range(8)] + [(1024, 80, 1024)]
            ffn(w1g, w1u, w1d, TBS, head_only=True)
            load_xT(xmain, tiles)
            norm(0, TBS)
            ffn(w1g, w1u, w1d, TBS, skip_head=True)
            norm(1, TBS)
            ckpt(9)
            t.dma('sp', lambda e: e.dma_start(out=xspill, in_=xT.rearrange("p k n -> p (k n)")))
            ckpt(10)

            stflat = state.rearrange("b r f -> (b r) f")
            for half in range(2):
                t.dma('sp', lambda e: e.dma_start(out=XS[0][0:120, 0:1024], in_=stflat[half * 120:(half + 1) * 120, :]),
                      writes=['xs0'])
                for g in range(2):
                    bank = g
                    for j in range(4):
                        c = g * 4 + j
                        t.op('pe', lambda e: e.transpose(out=PS[:, bank, j * 128:j * 128 + 120],
                                                         in_=XS[0][0:120, c * 128:(c + 1) * 128],
                                                         identity=identf[0:120, 0:120]),
                             reads=['xs0'], writes=[('ps', bank)], inc=(j == 3))
                    t.op('act', lambda e: e.copy(out=stT[:, g * 4:g * 4 + 4, half * 120:(half + 1) * 120],
                                                 in_=PS[:, bank, :].rearrange("p (j n) -> p j n", j=4)[:, :, 0:120]),
                         reads=[('ps', bank)], writes=['stT'])
            ckpt(101)
            t.dma('pool', lambda e: e.dma_start(out=WP, in_=wpool.rearrange("g (k p) n -> p g k n", p=128)), writes=['wp'])
            t.op('dve', lambda e: e.memset(DT[:, :, TP:SOFF], 0.0), writes=['dthalo'])
            t.op('dve', lambda e: e.memset(QTa[64:128, :, :], 0.0), writes=['qtaz'])
            t.op('dve', lambda e: e.memset(QTb[0:64, :, :], 0.0), writes=['qtbz'])

            ckpt(102)
            CBS = [6, 7, 0, 1, 2, 3, 4, 5]
            load_win(CBS[0], 0)
            for i, cb in enumerate(CBS):
                if i > 0 and STOP >= 103:
                    ckpt(102 + i)
                b = i % 2
                if i + 1 < len(CBS):
                    load_win(CBS[i + 1], (i + 1) % 2)
                col0 = (cb % 2) * 512
                if cb >= 6:
                    for c in range(4):
                        uc = (cb - 6) * 4 + c
                        g = uc // 2
                        kk = uc % 2
                        w = 2 ** (g + 1)

                        def ev(bank, bi, c0, n):
                            if bi < 2:
                                t.op('act', lambda e: e.copy(out=EXP_[0][:, 16 + c0:16 + c0 + n], in_=PS[:, bank, 0:n]),
                                     reads=[('ps', bank)], writes=[('exp0', bi)])
                            else:
                                t.op('act', lambda e: e.copy(out=EXP_[0][:, 0:16], in_=PS[:, bank, 0:16]),
                                     reads=[('ps', bank)], writes=[('exp0', 2)])
                                t.op('act', lambda e: e.copy(out=EXS[0][:, :, 15:19],
                                                             in_=PS[:, bank, 16:80].rearrange("p (b q) -> p b q", b=16)),
                                     reads=[('ps', bank)], writes=['exs0n'])
                        proj_fm(b, c, TBS, ev)
                        t.op('act', lambda e: e.copy(out=EXS[0][:, :, 0:15],
                                                     in_=stT[:, uc, :].rearrange("p (b r) -> p b r", b=16)),
                             reads=['stT'], writes=['exs0s'])
                        cur = 0
                        rk = [('exp0', 0), ('exp0', 1), ('exp0', 2)]
                        rks = ['exs0n', 'exs0s']
                        s = 1
                        for stp in range(g + 1):
                            nxt = 1 if cur != 1 else 2
                            t.op('dve', lambda e: e.tensor_tensor(out=EXP_[nxt][:, s:1040], in0=EXP_[cur][:, s:1040],
                                                                  in1=EXP_[cur][:, 0:1040 - s], op=ALU.add),
                                 reads=rk, writes=[('exp', nxt)])
                            t.op('pool', lambda e: e.tensor_tensor(out=EXS[nxt][:, :, s:19], in0=EXS[cur][:, :, s:19],
                                                                   in1=EXS[cur][:, :, 0:19 - s], op=ALU.add),
                                 reads=rks, writes=[('exs', nxt)])
                            cur = nxt
                            rk = [('exp', cur)]
                            rks = [('exs', cur)]
                            s *= 2
                        t.op('dve', lambda e: e.scalar_tensor_tensor(out=DT[:, kk, 0:TP], in0=EXP_[cur][:, 16:1040],
                                                                     scalar=1.0 / w, in1=EXP_[0][:, 16:1040],
                                                                     op0=ALU.mult, op1=ALU.subtract),
                             reads=rk + [('exp0', 0), ('exp0', 1)], writes=[('dt', kk)])
                        t.op('dve', lambda e: e.tensor_tensor(out=TMP16, in0=EXP_[cur][:, 16:32], in1=invc[:, g, :], op=ALU.mult),
                             reads=rk + ['invc'], writes=['tmp16'])
                        t.op('dve', lambda e: e.tensor_tensor(out=DT[:, kk, 0:16], in0=TMP16, in1=EXP_[0][:, 16:32],
                                                              op=ALU.subtract),
                             reads=['tmp16', ('exp0', 0)], writes=[('dt', kk)])
                        t.op('dve', lambda e: e.scalar_tensor_tensor(
                            out=DT[:, kk, SOFF:T].rearrange("p (b q) -> p b q", b=16), in0=EXS[cur][:, :, 15:19],
                            scalar=1.0 / w, in1=EXS[0][:, :, 15:19], op0=ALU.mult, op1=ALU.subtract),
                            reads=rks + ['exs0n'], writes=[('dt', kk)])
                        if kk == 1:
                            for oc in range(2):
                                for bi, (c0, n) in enumerate(TBS):
                                    bank = pcnt[0] % 4
                                    pcnt[0] += 1
                                    for k2 in range(2):
                                        t.op('pe', mm(PS[:, bank, 0:n], WP[:, g, k2, oc * 128:(oc + 1) * 128],
                                                      DT[:, k2, c0:c0 + n], k2 == 0, k2 == 1),
                                             reads=['wp', ('dt', k2), 'dthalo'], writes=[('ps', bank)], inc=(k2 == 1))
                                    zc = 2 * g + oc
                                    t.op('act', lambda e: e.activation(out=zT[:, zc, c0:c0 + n], in_=PS[:, bank, 0:n],
                                                                       func=AF.Copy, scale=pscale[:, zc:zc + 1]),
                                         reads=[('ps', bank)], writes=[('zT', zc, bi)])

                    def ev(bank, tt, n, col0=col0):
                        if tt == 7:
                            stage_out(bank, n, [(pool_p[:, col0:col0 + 512], 113, 128)])
                        else:
                            stage_out(bank, n, [(pool_s[bb, 11:15, col0:col0 + 512], 16 + 4 * bb, 20 + 4 * bb)
                                                for bb in range(NB)])
                    proj_tm(b, ALL_TILES[7:9], ev)
                elif cb in (0, 1, 2, 3):
                    for c in range(4):
                        hh = (cb % 2) * 4 + c

                        def ev(bank, bi, c0, n, hh=hh, isq=(cb < 2)):
                            if isq:
                                t.op('act', lambda e: e.copy(out=QTa[0:64, hh, c0:c0 + n], in_=PS[0:64, bank, 0:n]),
                                     reads=[('ps', bank)], writes=[('QTa', hh, bi)])
                                t.op('act', lambda e: e.copy(out=QTb[64:128, hh, c0:c0 + n], in_=PS[64:128, bank, 0:n]),
                                     reads=[('ps', bank)], writes=[('QTb', hh, bi)])
                            else:
                                t.op('act', lambda e: e.copy(out=KT[:, hh, c0:c0 + n], in_=PS[:, bank, 0:n]),
                                     reads=[('ps', bank)], writes=[('KT', hh, bi)])
                        proj_fm(b, c, TBS, ev)
                    if cb >= 2:
                        def ev(bank, tt, n, col0=col0):
                            if tt < 8:
                                stage_out(bank, n, [(k_p[tt * 128:(tt + 1) * 128, col0:col0 + 512], 0, 128)])
                            else:
                                stage_out(bank, n, [(k_s[:, col0:col0 + 512], 16, 80)])
                        proj_tm(b, ALL_TILES, ev)
                else:
                    def ev(bank, tt, n, col0=col0):
                        if tt < 8:
                            si = stage_out(bank, n, [(v_p[tt * 128:(tt + 1) * 128, col0:col0 + 512], 0, 128)])
                        else:
                            si = stage_out(bank, n, [(v_s[:, col0:col0 + 512], 16, 80)])
                        t.op('pool', lambda e: e.tensor_copy(out=Vt[0:n, tt, col0:col0 + 512], in_=STG[si][0:n, :]),
                             reads=[('stg', si)], writes=[('V', tt, col0)])
                    proj_tm(b, ALL_TILES, ev)
            ckpt(11)

            t.dma('sp', lambda e: e.dma_start(out=KTprev.rearrange("p k n -> p (k n)"), in_=kvspill[:, 0:8192]), writes=['kvp'])
            t.dma('sp', lambda e: e.dma_start(out=Vprev.rearrange("p k n -> p (k n)"), in_=kvspill[:, 8192:16384]), writes=['kvp'])
            for h in range(H):
                for qs in range(2):
                    q0 = qs * 512
                    blocks = [('prev', kb) for kb in range(8)] + [('own', kb) for kb in range(4 * qs + 4)]
                    nblk = len(blocks)
                    def blk(i):
                        src, kb = blocks[i]
                        r = kb - 4 * qs if src == 'own' else -1
                        off = max(r, 0) * 128
                        n = 512 - off
                        if src == 'prev':
                            return (r, off, n, KTprev[:, h, kb * 128:(kb + 1) * 128],
                                    Vprev[:, kb, h * 128:(h + 1) * 128], pbias, ['kvp'])
                        return (r, off, n, KT[:, h, kb * 128:(kb + 1) * 128],
                                Vt[:, kb, h * 128:(h + 1) * 128], zerob, [])

                    def s_stage(i):
                        r, off, n, Ksrc, Vsrc, bias, rkeys = blk(i)
                        sb = i % 2
                        t.op('pe', mm(PS[:, sb, 0:n], Ksrc, QTa[:, h, q0 + off:q0 + 512], True, True),
                             reads=rkeys, writes=[('ps', sb)])
                        t.op('pe', mm(PS[:, 2 + sb, 0:n], Ksrc, QTb[:, h, q0 + off:q0 + 512], True, True),
                             reads=rkeys, writes=[('ps', 2 + sb)])
                        t.op('act', lambda e: e.activation(out=PT1[sb][:, 0:n], in_=PS[:, sb, 0:n], func=AF.Exp,
                                                           scale=SCALE, bias=bias), reads=[('ps', sb)], writes=[('pt1', sb)])
                        t.op('act', lambda e: e.activation(out=PT2[sb][:, 0:n], in_=PS[:, 2 + sb, 0:n], func=AF.Exp,
                                                           scale=SCALE, bias=bias), reads=[('ps', 2 + sb)], writes=[('pt2', sb)])
                        if r >= 0:
                            t.op('pool', lambda e: e.tensor_tensor(out=PT1[sb][:, 0:128], in0=PT1[sb][:, 0:128], in1=trib,
                                                                   op=ALU.mult), reads=[('pt1', sb)], writes=[('pt1', sb)])
                            t.op('pool', lambda e: e.tensor_tensor(out=PT2[sb][:, 0:128], in0=PT2[sb][:, 0:128], in1=trib,
                                                                   op=ALU.mult), reads=[('pt2', sb)], writes=[('pt2', sb)])

                    def pv_stage(i):
                        r, off, n, Ksrc, Vsrc, bias, rkeys = blk(i)
                        sb = i % 2
                        st_ = (i == 0)
                        sp_ = (i == nblk - 1)
                        t.op('pe', mm(PS[:, 4, off:512], Vsrc, PT1[sb][:, 0:n], st_, sp_), reads=[('pt1', sb)] + rkeys,
                             writes=[('ps', 4)], inc=False)
                        t.op('pe', mm(PS[:, 6, off:512], onesb, PT1[sb][:, 0:n], st_, sp_), reads=[('pt1', sb)],
                             writes=[('ps', 6)])
                        t.op('pe', mm(PS[:, 5, off:512], Vsrc, PT2[sb][:, 0:n], st_, sp_), reads=[('pt2', sb)] + rkeys,
                             writes=[('ps', 5)], inc=False)
                        t.op('pe', mm(PS[:, 7, off:512], onesb, PT2[sb][:, 0:n], st_, sp_), reads=[('pt2', sb)],
                             writes=[('ps', 7)])

                    s_stage(0)
                    for i in range(nblk):
                        if i + 1 < nblk:
                            s_stage(i + 1)
                        pv_stage(i)
                    t.op('dve', lambda e: e.reciprocal(out=FR1, in_=PS[:, 6, :]), reads=[('ps', 6)], writes=['fr1'])
                    t.op('dve', lambda e: e.reciprocal(out=FR2, in_=PS[:, 7, :]), reads=[('ps', 7)], writes=['fr2'])
                    t.op('dve', lambda e: e.tensor_scalar(out=FR2, in0=FR2, scalar1=neglam[:, 0:1], scalar2=None, op0=ALU.mult),
                         reads=['fr2'], writes=['fr2'])
                    t.op('dve', lambda e: e.tensor_tensor(out=FO, in0=PS[:, 4, :], in1=FR1, op=ALU.mult),
                         reads=[('ps', 4), 'fr1'], writes=['fo'])
                    t.op('dve', lambda e: e.tensor_tensor(out=FO2, in0=PS[:, 5, :], in1=FR2, op=ALU.mult),
                         reads=[('ps', 5), 'fr2'], writes=['fo2'])
                    t.op('dve', lambda e: e.tensor_tensor(out=FO, in0=FO, in1=FO2, op=ALU.add),
                         reads=['fo', 'fo2'], writes=['fo'])
                    t.op('pool', lambda e: e.tensor_tensor(out=FSQ, in0=FO, in1=FO, op=ALU.mult), reads=['fo'], writes=['fsq'])
                    t.op('pe', mm(PS[:, 6, :], onesb, FSQ, True, True), reads=['fsq'], writes=[('ps', 6)])
                    t.op('act', lambda e: e.activation(out=FRS, in_=PS[:, 6, :], func=AF.Sqrt, scale=1.0 / 128, bias=eps5),
                         reads=[('ps', 6)], writes=['frs'])
                    t.op('dve', lambda e: e.reciprocal(out=FRS, in_=FRS), reads=['frs'], writes=['frs'])
                    t.op('dve', lambda e: e.scalar_tensor_tensor(out=hT[:, h, q0:q0 + 512], in0=FO, scalar=gsub[:, 0:1],
                                                                 in1=FRS, op0=ALU.mult, op1=ALU.mult),
                         reads=['fo', 'frs'], writes=[('oT', h, qs)])
            ckpt(12)

            Kr = [V_("R3", i * 2048, BF16, 1024) for i in range(4)]
            Vr = [V_("R3", 8192 + i * 2048, BF16, 1024) for i in range(4)]
            KTs2 = [V_("R3", 16384 + i * 2048, BF16, 1024) for i in range(2)]
            PTd = [V_("R3", 20480 + i * 2304, BF16, 1152) for i in range(2)]
            t.op('dve', lambda e: e.memset(QBLK.rearrange("p b h e -> p (b h e)"), 0.0), writes=['qblk'])
            for h in range(H):
                t.op('dve', lambda e: e.tensor_copy(out=QBLK[0:64, :, h, 0:4],
                                                    in_=QTa[0:64, h, SOFF:T].rearrange("p (b q) -> p b q", b=NB)),
                     reads=['qblk'], writes=['qblk'])
                t.op('dve', lambda e: e.tensor_copy(out=QBLK[64:128, :, h, 4:8],
                                                    in_=QTb[64:128, h, SOFF:T].rearrange("p (b q) -> p b q", b=NB)),
                     reads=['qblk'], writes=['qblk'])
            t.op('dve', lambda e: e.memset(PTd[0], 0.0), writes=[('ptsall', 0)])
            t.op('dve', lambda e: e.memset(PTd[1], 0.0), writes=[('ptsall', 1)])
            t.op('dve', lambda e: e.memset(PTsumH, 0.0), writes=['ptsumh'])
            t.op('dve', lambda e: e.memset(PTsumL, 0.0), writes=['ptsuml'])
            t.op('dve', lambda e: e.memset(PTnb, 0.0), writes=['ptnb'])
            t.op('dve', lambda e: e.memset(Vnew[0], 0.0), writes=[('vn', 0)])
            t.op('dve', lambda e: e.memset(Vnew[1], 0.0), writes=[('vn', 1)])
            t.op('dve', lambda e: e.memset(OSAMP, 0.0), writes=['osamp'])

            def stage_T(g):
                bb, j = g // 16, g % 16
                col = bb * 16 + j
                kp = Kr[g % 4]
                vp = Vr[g % 4]
                if j == 0:
                    vn_ = Vnew[bb % 2]
                    t.dma('sp', lambda e: e.dma_start(out=vn_[0:4, :], in_=Vt[16 + 4 * bb:20 + 4 * bb, 8, :]),
                          writes=[('vn', bb % 2)])
                t.dma('pool', lambda e: e.indirect_dma_start(
                    out=kp, out_offset=None, in_=ck,
                    in_offset=bass.IndirectOffsetOnAxis(ap=idx[:, col:col + 1], axis=0)), writes=[('kr', g % 4)])
                t.dma('pool', lambda e: e.indirect_dma_start(
                    out=vp, out_offset=None, in_=cv,
                    in_offset=bass.IndirectOffsetOnAxis(ap=idx[:, col:col + 1], axis=0)), writes=[('vr', g % 4)])
                for h in range(H):
                    t.op('pe', lambda e: e.transpose(out=PSb(6)[:, h * 128:(h + 1) * 128],
                                                     in_=kp[:, h * 128:(h + 1) * 128], identity=identb),
                         reads=[('kr', g % 4)], writes=[('ps', 6)], inc=(h == H - 1))
                t.op('act', lambda e: e.copy(out=KTs2[g % 2], in_=PSb(6)), reads=[('ps', 6)], writes=[('kts', g % 2)])
                sbk = j % 2
                for h in range(H):
                    c_ = ((j // 2) * 8 + h) * 8
                    t.op('pe', mm(PS[:, sbk, c_:c_ + 8], KTs2[g % 2][:, h * 128:(h + 1) * 128], QBLK[:, bb, h, :],
                                  (j // 2 == 0 and h == 0), False, skip_group_check=True),
                         reads=[('kts', g % 2), 'qblk'], writes=[('ps', sbk)], inc=(h == H - 1))
                c0_ = (j // 2) * 64
                t.op('act', lambda e: e.activation(out=PTd[bb % 2][:, j * 64:(j + 1) * 64], in_=PS[:, sbk, c0_:c0_ + 64],
                                                   func=AF.Exp, scale=SCALE),
                     reads=[('ps', sbk)], writes=[('pts', bb % 2, j)])

            def stage_PV(g):
                bb, j = g // 16, g % 16
                vp = Vr[g % 4]
                for h in range(H):
                    for half in range(2):
                        ob = 2 + half * 2 + h // 4
                        first = (j == 0 and h % 4 == 0)
                        c_ = (j * 8 + h) * 8 + half * 4
                        t.op('pe', mm(PS[:, ob, (h % 4) * 128:(h % 4 + 1) * 128], PTd[bb % 2][:, c_:c_ + 128],
                                      vp[:, h * 128:(h + 1) * 128], first, False, skip_group_check=True),
                             reads=[('pts', bb % 2, j), ('ptsall', bb % 2), ('vr', g % 4)], writes=[('ps', ob)],
                             inc=(h == 7 and half == 1))

            def tail(bb):
                s0 = SOFF + 4 * bb
                vn = Vnew[bb % 2]
                PTs_ = PTd[bb % 2]
                rkall = [('pts', bb % 2, j) for j in range(16)]
                t.op('dve', lambda e: e.tensor_reduce(out=PTsum, in_=PTs_[:, 0:1024].rearrange("p (j e) -> p e j", j=16),
                                                      axis=AX.X, op=ALU.add), reads=rkall, writes=['ptsum'])
                t.op('dve', lambda e: e.tensor_copy(out=PTsumH[:, 0:64], in_=PTsum), reads=['ptsum'], writes=['ptsumh'])
                t.op('dve', lambda e: e.tensor_tensor(out=PTsumL[:, 0:64], in0=PTsum, in1=PTsumH[:, 0:64], op=ALU.subtract),
                     reads=['ptsum', 'ptsumh'], writes=['ptsuml'])
                for h in range(H):
                    t.op('pe', mm(PS[:, 7, h * 8:(h + 1) * 8], KTflat[:, h * T + s0:h * T + s0 + 128], QBLK[:, bb, h, :],
                                  h == 0, h == 7, skip_group_check=True), reads=['qblk'], writes=[('ps', 7)], inc=(h == 7))
                t.op('act', lambda e: e.activation(out=PTn32[0:4, :], in_=PS[0:4, 7, 0:64], func=AF.Exp, scale=SCALE),
                     reads=[('ps', 7)], writes=['ptn32'])
                t.op('dve', lambda e: e.tensor_tensor(out=PTn32[0:4, :], in0=PTn32[0:4, :], in1=smask[0:4, :], op=ALU.mult),
                     reads=['ptn32'], writes=['ptn32'])
                t.op('dve', lambda e: e.tensor_copy(out=PTnb[0:4, 0:64], in_=PTn32[0:4, :]), reads=['ptn32'], writes=['ptnb'])
                for h in range(H):
                    for half in range(2):
                        ob = 2 + half * 2 + h // 4
                        c_ = h * 8 + half * 4
                        t.op('pe', mm(PS[:, ob, (h % 4) * 128:(h % 4 + 1) * 128], PTnb[:, c_:c_ + 128],
                                      vn[:, h * 128:(h + 1) * 128], False, True, skip_group_check=True),
                             reads=['ptnb', ('vn', bb % 2)], writes=[('ps', ob)], inc=(h == 7 and half == 1))
                for h in range(H):
                    for half in range(2):
                        c_ = 64 + half * 8 + h
                        e0 = h * 8 + half * 4
                        first = (h == 0 and half == 0)
                        t.op('pe', mm(PS[:, 7, c_:c_ + 1], PTsumH[:, e0:e0 + 128], onesb[:, 0:1], first, False,
                                      skip_group_check=True), reads=['ptsumh', 'onesb', 'ptn32'], writes=[('ps', 7)], inc=False)
                        t.op('pe', mm(PS[:, 7, c_:c_ + 1], PTsumL[:, e0:e0 + 128], onesb[:, 0:1], False, False,
                                      skip_group_check=True), reads=['ptsuml'], writes=[('ps', 7)], inc=False)
                        t.op('pe', mm(PS[:, 7, c_:c_ + 1], PTnb[:, e0:e0 + 128], onesb[:, 0:1], False,
                                      (h == 7 and half == 1), skip_group_check=True), reads=['ptnb'], writes=[('ps', 7)],
                             inc=(h == 7 and half == 1))
                t.op('dve', lambda e: e.reciprocal(out=SSM[0:4, 0:16], in_=PS[0:4, 7, 64:80]), reads=[('ps', 7)], writes=['ssm'])
                t.op('dve', lambda e: e.tensor_scalar(out=SSM[0:4, 8:16], in0=SSM[0:4, 8:16], scalar1=neglam[0:4, 0:1],
                                                      scalar2=None, op0=ALU.mult), reads=['ssm'], writes=['ssm'])
                for k in range(2):
                    t.op('dve', lambda e: e.tensor_tensor(
                        out=SO[0:4, k * 512:(k + 1) * 512].rearrange("p (h d) -> p h d", h=4),
                        in0=PS[0:4, 2 + k, :].rearrange("p (h d) -> p h d", h=4),
                        in1=SSM[0:4, 4 * k:4 * k + 4].unsqueeze(2).to_broadcast([4, 4, 128]), op=ALU.mult),
                        reads=[('ps', 2 + k), 'ssm'], writes=[('so', k)])
                    t.op('dve', lambda e: e.tensor_tensor(
                        out=SO2[0:4, k * 512:(k + 1) * 512].rearrange("p (h d) -> p h d", h=4),
                        in0=PS[0:4, 4 + k, :].rearrange("p (h d) -> p h d", h=4),
                        in1=SSM[0:4, 8 + 4 * k:12 + 4 * k].unsqueeze(2).to_broadcast([4, 4, 128]), op=ALU.mult),
                        reads=[('ps', 4 + k), 'ssm'], writes=[('so2', k)])
                t.op('dve', lambda e: e.tensor_tensor(out=SO[0:4, :], in0=SO[0:4, :], in1=SO2[0:4, :], op=ALU.add),
                     reads=[('so', 0), ('so', 1), ('so2', 0), ('so2', 1)], writes=['sof'])
                t.op('dve', lambda e: e.tensor_tensor(out=SSQ[0:4, :], in0=SO[0:4, :], in1=SO[0:4, :], op=ALU.mult),
                     reads=['sof'], writes=['ssq'])
                t.op('dve', lambda e: e.tensor_reduce(out=SSM[0:4, 16:24], in_=SSQ[0:4, :].rearrange("p (h d) -> p h d", h=8),
                                                      axis=AX.X, op=ALU.add), reads=['ssq'], writes=['ssm2'])
                t.op('act', lambda e: e.activation(out=SSM[0:4, 24:32], in_=SSM[0:4, 16:24], func=AF.Sqrt, scale=1.0 / 128,
                                                   bias=eps5[0:4, :]), reads=['ssm2'], writes=['ssm3'])
                t.op('dve', lambda e: e.reciprocal(out=SSM[0:4, 32:40], in_=SSM[0:4, 24:32]), reads=['ssm3'], writes=['ssm4'])
                t.op('dve', lambda e: e.tensor_tensor(
                    out=SSQ[0:4, :].rearrange("p (h d) -> p h d", h=8), in0=SO[0:4, :].rearrange("p (h d) -> p h d", h=8),
                    in1=SSM[0:4, 32:40].unsqueeze(2).to_broadcast([4, 8, 128]), op=ALU.mult),
                    reads=['sof', 'ssm4', 'ssq'], writes=['ssq'])
                t.op('dve', lambda e: e.tensor_tensor(
                    out=SSQ[0:4, :].rearrange("p (h d) -> p h d", h=8), in0=SSQ[0:4, :].rearrange("p (h d) -> p h d", h=8),
                    in1=gsubrow[0:4, :].unsqueeze(1).to_broadcast([4, 8, 128]), op=ALU.mult),
                    reads=['ssq', 'gsubrow'], writes=['ssq'])
                t.dma('sp', lambda e: e.dma_start(out=OSAMP[4 * bb:4 * bb + 4, :], in_=SSQ[0:4, :]), reads=['ssq'],
                      writes=['osamp'])

            NG = NB * 16
            stage_T(0)
            for g in range(NG):
                if g + 1 < NG:
                    stage_T(g + 1)
                stage_PV(g)
                if g % 16 == 15:
                    tail(g // 16)
            for g in range(2):
                for j in range(4):
                    h = g * 4 + j
                    t.op('pe', lambda e: e.transpose(out=PS[:, g, j * 128:(j + 1) * 128], in_=OSAMP[:, h * 128:(h + 1) * 128],
                                                     identity=identf),
                         reads=['osamp'], writes=[('ps', g)], inc=(j == 3))
                t.op('act', lambda e: e.copy(out=hT[:, g * 4:g * 4 + 4, SOFF:T],
                                             in_=PS[:, g, :].rearrange("p (j n) -> p j n", j=4)[:, :, 0:64]),
                     reads=[('ps', g)], writes=[('oTs', g)])
            ckpt(13)

            t.dma('sp', lambda e: e.dma_start(out=xT.rearrange("p k n -> p (k n)"), in_=xspill), writes=['xTall'])
            woutv = wout.rearrange("(k p) f -> p k f", p=128)

            def load_wo(cb, b):
                t.dma('pool', lambda e: e.dma_start(out=WIN[b], in_=woutv[:, :, cb * 512:(cb + 1) * 512]), writes=[('win', b)])
            load_wo(0, 0)
            for cb in range(4):
                b = cb % 2
                if cb + 1 < 4:
                    load_wo(cb + 1, (cb + 1) % 2)
                for c in range(4):
                    dc = cb * 4 + c
                    for bi, (c0, n) in enumerate(TBS):
                        bank = pcnt[0] % 4
                        pcnt[0] += 1
                        for kc in range(16):
                            rhs = hT[:, kc, c0:c0 + n] if kc < 8 else zT[:, kc - 8, c0:c0 + n]
                            t.op('pe', mm(PS[:, bank, 0:n], WIN[b][:, kc, c * 128:(c + 1) * 128], rhs, kc == 0, kc == 15),
                                 reads=[('win', b)], writes=[('ps', bank)], inc=(kc == 15))
                        t.op('dve', lambda e: e.tensor_tensor(out=xT[:, dc, c0:c0 + n], in0=PS[:, bank, 0:n],
                                                              in1=xT[:, dc, c0:c0 + n], op=ALU.add),
                             reads=[('ps', bank), 'xTall', ('xT', dc, bi)], writes=[('xT', dc, bi)])
            ckpt(14)

            ffn(w2g, w2u, w2d, TBS, head_only=True)
            norm(2, TBS)
            ffn(w2g, w2u, w2d, TBS, skip_head=True)
            norm(3, TBS, out_f32=True)
            for i, (tt, c0, n) in enumerate(ALL_TILES):
                st = XS[i % 2]
                for g in range(4):
                    bank = g
                    for j in range(4):
                        kc = g * 4 + j
                        t.op('pe', lambda e: e.transpose(out=PS[0:n, bank, j * 128:(j + 1) * 128], in_=xT[:, kc, c0:c0 + n],
                                                         identity=identf),
                             reads=[('xT', kc, c0 // 512)], writes=[('ps', bank)], inc=(j == 3))
                    eng = 'act' if g % 2 == 0 else 'dve'
                    if eng == 'act':
                        t.op('act', lambda e: e.copy(out=st[0:n, g * 512:(g + 1) * 512], in_=PS[0:n, bank, :]),
                             reads=[('ps', bank)], writes=[('xs', i % 2, g)])
                    else:
                        t.op('dve', lambda e: e.tensor_copy(out=st[0:n, g * 512:(g + 1) * 512], in_=PS[0:n, bank, :]),
                             reads=[('ps', bank)], writes=[('xs', i % 2, g)])
                rk = [('xs', i % 2, g) for g in range(4)]
                if tt < 8:
                    t.dma('sp', lambda e: e.dma_start(out=y_p[tt * 128:(tt + 1) * 128, :], in_=st[0:128, :]), reads=rk)
                else:
                    t.dma('sp', lambda e: e.dma_start(out=y_s, in_=st[16:80, :]), reads=rk)
            ckpt(18)

        try:
            body()
        except _Stop:
            pass
    return nc


_NC_CACHE = {}


def kernel(x_prompt, x_sample, cache_k, cache_v, state_pool, page_table,
           ffn1_norm, ffn1_w_gate, ffn1_w_up, ffn1_w_down,
           mix_norm, w_in, lambda_q1, lambda_k1, lambda_q2, lambda_k2,
           subln_gain, w_pool, pool_scale, w_out,
           ffn2_norm, ffn2_w_gate, ffn2_w_up, ffn2_w_down, final_norm):
    f = np.float32
    a = lambda v: np.ascontiguousarray(np.asarray(v))
    x_prompt = a(x_prompt); x_sample = a(x_sample)
    ckf = a(cache_k).reshape(NPHYS * 128, 1024)
    cvf = a(cache_v).reshape(NPHYS * 128, 1024)
    state_pool = a(state_pool); page_table = a(page_table).astype(np.int32)
    if "nc" not in _NC_CACHE:
        _NC_CACHE["nc"] = build_nc()
    nc = _NC_CACHE["nc"]

    def colmajor(v):
        return a(v).reshape(16, 128).T

    gains = np.concatenate([colmajor(ffn1_norm[0]), colmajor(mix_norm[0]), colmajor(ffn2_norm[0]),
                            colmajor(final_norm)], axis=1).astype(f)
    pscale = a(pool_scale)[0].reshape(8, 128).T.astype(f)
    subcol = a(subln_gain)[0].reshape(128, 1).astype(f)
    subrow = np.broadcast_to(a(subln_gain)[0][None, :], (128, 128)).astype(f)
    lams = np.broadcast_to(np.concatenate([a(lambda_q1)[0], a(lambda_k1)[0], a(lambda_q2)[0], a(lambda_k2)[0]])[None, :],
                           (128, 256)).astype(f)
    ident = np.eye(128, dtype=f)
    tri = (np.arange(128)[:, None] <= np.arange(128)[None, :]).astype(f)
    smask = np.zeros((128, 64), f)
    for k in range(4):
        for e in range(64):
            if k <= e % 4:
                smask[k, e] = 1.0
    shared = {
        "ck": ckf, "cv": cvf,
        "w1g": a(ffn1_w_gate)[0], "w1u": a(ffn1_w_up)[0], "w1d": a(ffn1_w_down)[0],
        "w2g": a(ffn2_w_gate)[0], "w2u": a(ffn2_w_up)[0], "w2d": a(ffn2_w_down)[0],
        "win": a(w_in)[0], "wpool": a(w_pool)[0], "wout": a(w_out)[0],
        "c_ident": ident, "c_tri": tri, "c_smask": smask, "c_gains": a(gains), "c_pscale": a(pscale),
        "c_subcol": subcol, "c_subrow": a(subrow), "c_lams": a(lams),
    }
    in_maps = []
    for c in range(8):
        b, h = c // 2, c % 2
        xm = np.zeros((T, D), f)
        xm[0:TP] = x_prompt[b, h * TP:(h + 1) * TP]
        if h == 1:
            xm[TP:TP + TH] = x_prompt[b, TP - TH:TP]
        xm[SOFF:T] = x_sample[NB * c:NB * (c + 1)].reshape(TS, D)
        xp = x_prompt[b, 0:TP] if h == 1 else np.zeros((TP, D), f)
        invc = np.zeros((128, 4, 16), f)
        for g in range(4):
            w = 2 ** (g + 1)
            for tkn in range(16):
                pos = h * TP + tkn
                invc[:, g, tkn] = 1.0 / min(w, pos + 1)
        m = dict(shared)
        m.update({
            "xmain": xm, "xprev": a(xp),
            "state": a(state_pool[0, NB * c:NB * (c + 1)]),
            "ptab": a(page_table[NB * c:NB * (c + 1)].reshape(1, NB * 16)),
            "c_invc": invc.reshape(128, 64),
            "c_pbias": np.full((128, 1), 0.0 if h == 1 else -30000.0, f),
        })
        in_maps.append(m)
    res = run_bass_kernel_spmd(nc, in_maps, core_ids=list(range(8)))
    R = res.results
    y_prompt = np.zeros((4, 2048, D), f)
    y_sample = np.zeros((128, 4, D), f)
    k_prompt = np.zeros((1, 4, 2048, 8, 128), f)
    v_prompt = np.zeros((1, 4, 2048, 8, 128), f)
    pool_prompt = np.zeros((1, 4, 15, 1024), f)
    k_sample = np.zeros((1, 128, 4, 8, 128), f)
    v_sample = np.zeros((1, 128, 4, 8, 128), f)
    pool_sample = np.zeros((1, 128, 15, 1024), f)
    for c in range(8):
        b, h = c // 2, c % 2
        r = R[c]
        y_prompt[b, h * TP:(h + 1) * TP] = r["y_p"]
        y_sample[NB * c:NB * (c + 1)] = r["y_s"].reshape(NB, 4, D)
        k_prompt[0, b, h * TP:(h + 1) * TP] = r["k_p"].reshape(TP, 8, 128)
        v_prompt[0, b, h * TP:(h + 1) * TP] = r["v_p"].reshape(TP, 8, 128)
        if h == 1:
            pool_prompt[0, b] = r["pool_p"]
        k_sample[0, NB * c:NB * (c + 1)] = r["k_s"].reshape(NB, 4, 8, 128)
        v_sample[0, NB * c:NB * (c + 1)] = r["v_s"].reshape(NB, 4, 8, 128)
        pool_sample[0, NB * c:NB * (c + 1)] = r["pool_s"]
    return (y_prompt, y_sample, k_prompt, v_prompt, pool_prompt, k_sample, v_sample, pool_sample)
```

```python
import numpy as np
import concourse.bass as bass
import concourse.mybir as mybir
from concourse.bass_utils import run_bass_kernel_spmd

F32 = mybir.dt.float32
BF16 = mybir.dt.bfloat16
I32 = mybir.dt.int32
AF = mybir.ActivationFunctionType
ALU = mybir.AluOpType
AX = mybir.AxisListType

D = 2048
DFF = 5504
H = 8
TP = 1024
TH = 16
TS = 64
T = TP + TH + TS
SOFF = TP + TH
NB = 16
NPHYS = 2560
TBS = [(0, 512), (512, 512), (1024, 80)]
PREV_TBS = [(0, 512), (512, 512)]
LAM_INIT = 0.8 - 0.6 * 1.0
SCALE = 0.125
NDS = 24
STOP = 0


class Trk:
    def __init__(self, nc, sem_iter):
        self.nc = nc
        self.sem_iter = sem_iter
        self.eng = {'pe': nc.tensor, 'act': nc.scalar, 'dve': nc.vector, 'pool': nc.gpsimd, 'sp': nc.sync}
        self.sem = {}
        self.cnt = {}
        for e in ['pe', 'act', 'dve', 'pool']:
            self.sem[e] = next(sem_iter)
            self.cnt[e] = 0
        self.lastw = {}
        self.readers = {}
        self.waited = {}
        self.pending = {e: ([], []) for e in self.sem}
        self.dma_sems = [next(sem_iter) for _ in range(NDS)]
        self.dma_cnt = [0] * NDS
        self.dma_prev = [None] * NDS
        self.dma_i = 0

    def _wait(self, eng, tok):
        sem, val, src = tok
        if src == eng and eng == 'pe':
            return
        key = (eng, id(sem))
        if self.waited.get(key, 0) >= val:
            return
        self.eng[eng].wait_ge(sem, val)
        self.waited[key] = val

    def _deps(self, eng, reads, writes):
        for k in reads:
            w = self.lastw.get(k)
            if w:
                self._wait(eng, w)
        for k in writes:
            w = self.lastw.get(k)
            if w:
                self._wait(eng, w)
            for r in self.readers.get(k, {}).values():
                self._wait(eng, r)

    def _commit(self, tok, reads, writes):
        for k in reads:
            self.readers.setdefault(k, {})[tok[2]] = tok
        for k in writes:
            self.lastw[k] = tok
            self.readers[k] = {}

    def op(self, eng, fn, reads=(), writes=(), inc=True):
        self._deps(eng, reads, writes)
        ins = fn(self.eng[eng])
        pr, pw = self.pending[eng]
        if not inc:
            pr.extend(reads)
            pw.extend(writes)
            return
        if self.cnt[eng] >= 30000:
            self.sem[eng] = next(self.sem_iter)
            self.cnt[eng] = 0
        self.cnt[eng] += 1
        ins.then_inc(self.sem[eng], 1)
        tok = (self.sem[eng], self.cnt[eng], eng)
        self._commit(tok, list(reads) + pr, list(writes) + pw)
        self.pending[eng] = ([], [])

    def dma(self, q, fn, reads=(), writes=()):
        self._deps(q, reads, writes)
        slot = self.dma_i % NDS
        self.dma_i += 1
        prev = self.dma_prev[slot]
        if prev:
            self._wait(q, prev)
        if self.dma_cnt[slot] >= 30000:
            self.dma_sems[slot] = next(self.sem_iter)
            self.dma_cnt[slot] = 0
        ins = fn(self.eng[q])
        self.dma_cnt[slot] += 16
        ins.then_inc(self.dma_sems[slot], 16)
        tok = (self.dma_sems[slot], self.dma_cnt[slot], ('dma', slot))
        self.dma_prev[slot] = tok
        self._commit(tok, reads, writes)

    def barrier(self):
        for e in self.pending:
            assert self.pending[e] == ([], []), e
        toks = [(self.sem[e], self.cnt[e], e) for e in self.sem if self.cnt[e] > 0]
        toks += [tk for tk in self.dma_prev if tk]
        for e in ['pe', 'act', 'dve', 'pool', 'sp']:
            for tk in toks:
                if tk[2] != e:
                    sem, val, src = tk
                    key = (e, id(sem))
                    if self.waited.get(key, 0) >= val:
                        continue
                    self.eng[e].wait_ge(sem, val)
                    self.waited[key] = val
        self.lastw = {}
        self.readers = {}


def build_nc():
    nc = bass.Bass("TRN2", target_bir_lowering=False)

    def din(name, shape, dt=F32):
        return nc.dram_tensor(name, list(shape), dt, kind="ExternalInput").ap()

    def dout(name, shape, dt=F32):
        return nc.dram_tensor(name, list(shape), dt, kind="ExternalOutput").ap()

    xmain = din("xmain", [T, D])
    xprev = din("xprev", [TP, D])
    ck = din("ck", [NPHYS * 128, 1024]) if True else None
    cv = din("cv", [NPHYS * 128, 1024])
    state = din("state", [NB, 15, 1024])
    ptab = din("ptab", [1, NB * 16], I32)
    w1g = din("w1g", [D, DFF]); w1u = din("w1u", [D, DFF]); w1d = din("w1d", [DFF, D])
    w2g = din("w2g", [D, DFF]); w2u = din("w2u", [D, DFF]); w2d = din("w2d", [DFF, D])
    win = din("win", [D, 4096])
    wpool = din("wpool", [4, 256, 256])
    wout = din("wout", [D, D])
    c_ident = din("c_ident", [128, 128])
    c_tri = din("c_tri", [128, 128])
    c_smask = din("c_smask", [128, 64])
    c_invc = din("c_invc", [128, 64])
    c_pbias = din("c_pbias", [128, 1])
    c_gains = din("c_gains", [128, 64])
    c_pscale = din("c_pscale", [128, 8])
    c_subcol = din("c_subcol", [128, 1])
    c_subrow = din("c_subrow", [128, 128])
    c_lams = din("c_lams", [128, 256])

    y_p = dout("y_p", [TP, D]); y_s = dout("y_s", [TS, D])
    k_p = dout("k_p", [TP, 1024]); v_p = dout("v_p", [TP, 1024])
    pool_p = dout("pool_p", [15, 1024])
    k_s = dout("k_s", [TS, 1024]); v_s = dout("v_s", [TS, 1024])
    pool_s = dout("pool_s", [NB, 15, 1024])

    xspill = nc.dram_tensor("xspill", [128, 16 * T], F32, kind="Internal").ap()
    kvspill = nc.dram_tensor("kvspill", [128, 16384], BF16, kind="Internal").ap()

    R1, R2, RZ, R3, R4, RC, R5 = 74240, 35328, 17664, 49152, 25600, 6144, 4096
    offs = {}
    o = 0
    for nm, sz in [("R1", R1), ("R2", R2), ("RZ", RZ), ("R3", R3), ("R4", R4), ("RC", RC), ("R5", R5)]:
        offs[nm] = o
        o += sz
    TOTAL = o

    import contextlib
    with contextlib.ExitStack() as es:
        A = es.enter_context(nc.sbuf_tensor("arena", [128, TOTAL // 2], BF16))
        PS = es.enter_context(nc.psum_tensor("ps", [128, 8, 512], F32))
        sems = []
        for i in range(90):
            sems.append(es.enter_context(nc.semaphore(f"s{i}")))
        es.enter_context(nc.Block())
        t = Trk(nc, iter(sems))

        def V_(region, boff, dt, nelem, pat=None, **kw):
            b0 = (offs[region] + boff) // 2
            nb = nelem * (4 if dt in (F32, I32) else 2) // 2
            ap = A[:, b0:b0 + nb]
            if dt != BF16:
                ap = ap.bitcast(dt)
            if pat:
                ap = ap.rearrange(pat, **kw)
            return ap

        def PSb(bank):
            return PS[:, bank, :].bitcast(BF16)

        xT = V_("R1", 0, F32, 16 * T, "p (k n) -> p k n", k=16)
        QTa = V_("R1", 0, BF16, 8 * T, "p (k n) -> p k n", k=8)
        QTb = V_("R1", 17664, BF16, 8 * T, "p (k n) -> p k n", k=8)
        KT = V_("R1", 35328, BF16, 8 * T, "p (k n) -> p k n", k=8)
        KTflat = V_("R1", 35328, BF16, 8 * T + 256)
        Vt = V_("R1", 53504, BF16, 9 * 1024, "p (k n) -> p k n", k=9)
        stT = V_("R3", 36864, F32, 8 * 240, "p (k n) -> p k n", k=8)
        KTp_tmp = V_("R1", 0, BF16, 8 * 1024, "p (k n) -> p k n", k=8)
        Vp_tmp = V_("R1", 16384, BF16, 8 * 1024, "p (k n) -> p k n", k=8)
        hT = V_("R2", 0, BF16, 16 * T, "p (k n) -> p k n", k=16)
        zT = V_("RZ", 0, BF16, 8 * T, "p (k n) -> p k n", k=8)
        WB = [V_("R3", b * 24576, BF16, 12288) for b in range(2)]
        WG = [w[:, 0:4096].rearrange("p (k n) -> p k n", k=16) for w in WB]
        WU = [w[:, 4096:8192].rearrange("p (k n) -> p k n", k=16) for w in WB]
        WD = [w[:, 8192:12288].rearrange("p (k n) -> p k n", k=2) for w in WB]
        WIN = [V_("R3", b * 16384, BF16, 8192, "p (k n) -> p k n", k=16) for b in range(2)]
        WP = V_("R3", 32768, BF16, 2048, "p (g k n) -> p g k n", g=4, k=2)
        SQ = V_("RZ", 0, BF16, 16 * 512, "p (k n) -> p k n", k=16)
        KTprev = V_("R3", 0, BF16, 8 * 1024, "p (k n) -> p k n", k=8)
        Vprev = V_("R3", 16384, BF16, 8 * 1024, "p (k n) -> p k n", k=8)
        Vb = V_("R3", 0, BF16, 16 * 1024, "p (k n) -> p k n", k=16)
        Kp = [V_("R3", 32768 + i * 2048, BF16, 1024) for i in range(2)]
        KTs = [V_("R3", 36864 + i * 2048, BF16, 1024) for i in range(2)]
        PTs = V_("R3", 40960, BF16, 1152)
        Vnew = [V_("R3", 43264 + i * 2048, BF16, 1024) for i in range(2)]
        PTsumH = V_("R3", 47360, BF16, 192)
        PTsumL = V_("R3", 47744, BF16, 192)
        PTnb = V_("R3", 48128, BF16, 192)
        PTn32 = V_("R3", 48512, F32, 64)
        PTsum = V_("R3", 48768, F32, 64)
        XS = [V_("R4", i * 8192, F32, 2048) for i in range(2)]
        RS = V_("R4", 16384, F32, 512)
        RSTD = V_("R4", 18432, F32, 512)
        AT = [V_("R4", 20480 + i * 2048, BF16, 1024, "p (k n) -> p k n", k=2) for i in range(2)]
        SG = [V_("R5", i * 2048, F32, 512) for i in range(2)]
        STG = [V_("R4", i * 2048, F32, 512) for i in range(2)]
        EXP_ = [V_("R4", 4096 + i * 4160, F32, 1040) for i in range(3)]
        EXS = [V_("R4", 16576 + i * 1216, F32, 304, "p (b r) -> p b r", b=16) for i in range(3)]
        DT = V_("R4", 20224, BF16, 2 * T, "p (k n) -> p k n", k=2)
        TMP16 = V_("R4", 24640, F32, 16)
        PT1 = [V_("R4", i * 1024, BF16, 512) for i in range(2)]
        PT2 = [V_("R4", 2048 + i * 1024, BF16, 512) for i in range(2)]
        FR1 = V_("R4", 4096, F32, 512)
        FR2 = V_("R4", 6144, F32, 512)
        FO = V_("R4", 8192, F32, 512)
        FO2 = V_("R4", 10240, F32, 512)
        FRS = V_("R4", 12288, F32, 512)
        FSQ = V_("R4", 14336, BF16, 512)
        OSAMP = V_("R4", 16384, F32, 1024)
        SO = V_("R4", 20480, F32, 1024)
        SO2 = V_("R4", 4096, F32, 1024)
        SSQ = V_("R4", 8192, F32, 1024)
        SSM = V_("R4", 24576, F32, 64)
        identf = V_("RC", 0, F32, 128)
        identb = V_("RC", 512, BF16, 128)
        onesb = V_("RC", 768, BF16, 128)
        trib = V_("RC", 1024, BF16, 128)
        gains = V_("RC", 1280, F32, 64, "p (g k) -> p g k", g=4)
        pscale = V_("RC", 1536, F32, 8)
        gsub = V_("RC", 1568, F32, 1)
        eps6 = V_("RC", 1572, F32, 1)
        eps5 = V_("RC", 1576, F32, 1)
        zerob = V_("RC", 1580, F32, 1)
        pbias = V_("RC", 1584, F32, 1)
        lam = V_("RC", 1588, F32, 1)
        neglam = V_("RC", 1592, F32, 1)
        lt = V_("RC", 1596, F32, 4)
        onesf = V_("RC", 1612, F32, 1)
        invc = V_("RC", 1616, F32, 64, "p (g k) -> p g k", g=4)
        smask = V_("RC", 1872, F32, 64)
        gsubrow = V_("RC", 2128, F32, 128)
        lams = V_("RC", 2640, F32, 256, "p (g k) -> p g k", g=4)
        ltmp = V_("RC", 3664, F32, 64)
        idx = V_("RC", 3920, I32, 256)
        iot = V_("RC", 4944, I32, 256)
        QBLK = V_("R1", 71936, BF16, NB * 64, "p (b h e) -> p b h e", b=NB, h=8)

        def mm(out, lhsT, rhs, start, stop, **kw):
            return lambda e: e.matmul(out, lhsT=lhsT, rhs=rhs, start=start, stop=stop, **kw)

        class _Stop(Exception):
            pass

        def ckpt(k):
            t.barrier()
            if STOP == k:
                raise _Stop()

        def body():
            def load_const(dst, src, key):
                t.dma('sp', lambda e: e.dma_start(out=dst, in_=src), writes=[key])

            load_const(identf, c_ident, 'identf')
            load_const(gains.rearrange("p g k -> p (g k)"), c_gains, 'gains')
            load_const(pscale, c_pscale, 'pscale')
            load_const(gsub, c_subcol, 'gsub')
            load_const(pbias, c_pbias, 'pbias')
            load_const(invc.rearrange("p g k -> p (g k)"), c_invc, 'invc')
            load_const(smask, c_smask, 'smask')
            load_const(gsubrow, c_subrow, 'gsubrow')
            load_const(lams.rearrange("p g k -> p (g k)"), c_lams, 'lams')
            t.dma('pool', lambda e: e.dma_start(out=trib, in_=c_tri), writes=['trib'])
            t.dma('pool', lambda e: e.dma_start(out=idx, in_=ptab[0, :].partition_broadcast(128)), writes=['idx'])
            t.op('pool', lambda e: e.iota(iot, pattern=[[0, 256]], base=0, channel_multiplier=1), writes=['iot'])
            t.op('pool', lambda e: e.tensor_scalar(out=idx, in0=idx, scalar1=128, scalar2=None, op0=ALU.mult),
                 reads=['idx'], writes=['idx'])
            t.op('pool', lambda e: e.tensor_tensor(out=idx, in0=idx, in1=iot, op=ALU.add),
                 reads=['idx', 'iot'], writes=['idx'])
            t.op('dve', lambda e: e.tensor_copy(out=identb, in_=identf), reads=['identf'], writes=['identb'])
            t.op('dve', lambda e: e.memset(onesb, 1.0), writes=['onesb'])
            t.op('dve', lambda e: e.memset(onesf, 1.0), writes=['onesf'])
            t.op('dve', lambda e: e.memset(eps6, 1e-6), writes=['eps6'])
            t.op('dve', lambda e: e.memset(eps5, 1e-5), writes=['eps5'])
            t.op('dve', lambda e: e.memset(zerob, 0.0), writes=['zerob'])
            t.op('dve', lambda e: e.tensor_scalar(out=gsub, in0=gsub, scalar1=1.0 - LAM_INIT, scalar2=None, op0=ALU.mult),
                 reads=['gsub'], writes=['gsub'])
            t.op('dve', lambda e: e.tensor_scalar(out=gsubrow, in0=gsubrow, scalar1=1.0 - LAM_INIT, scalar2=None, op0=ALU.mult),
                 reads=['gsubrow'], writes=['gsubrow'])
            for i in range(2):
                t.op('dve', lambda e: e.tensor_tensor(out=ltmp, in0=lams[:, 2 * i, :], in1=lams[:, 2 * i + 1, :], op=ALU.mult),
                     reads=['lams'], writes=['ltmp'])
                t.op('dve', lambda e: e.tensor_reduce(out=lt[:, i:i + 1], in_=ltmp, axis=AX.X, op=ALU.add),
                     reads=['ltmp'], writes=['lt'])
            t.op('act', lambda e: e.activation(out=lt[:, 2:4], in_=lt[:, 0:2], func=AF.Exp), reads=['lt'], writes=['lt'])
            t.op('dve', lambda e: e.tensor_tensor(out=lam, in0=lt[:, 2:3], in1=lt[:, 3:4], op=ALU.subtract),
                 reads=['lt'], writes=['lam'])
            t.op('dve', lambda e: e.tensor_scalar(out=lam, in0=lam, scalar1=LAM_INIT, scalar2=None, op0=ALU.add),
                 reads=['lam'], writes=['lam'])
            t.op('dve', lambda e: e.tensor_scalar(out=neglam, in0=lam, scalar1=-1.0, scalar2=None, op0=ALU.mult),
                 reads=['lam'], writes=['neglam'])
            t.dma('sp', lambda e: e.dma_start(out=pool_s[:, 0:11, :], in_=state[:, 4:15, :]))
            ckpt(1)

            def load_xT(src, tiles):
                for i, (r0, n, c0) in enumerate(tiles):
                    st = XS[i % 2]
                    bi = c0 // 512
                    t.dma('sp', lambda e: e.dma_start(out=st[0:n, :], in_=src[r0:r0 + n, :]), writes=[('xs', i % 2)])
                    for g in range(4):
                        bank = g % 2
                        for j in range(4):
                            kc = g * 4 + j
                            t.op('pe', lambda e: e.transpose(out=PS[:, bank, j * 128:j * 128 + n],
                                                             in_=st[0:n, kc * 128:(kc + 1) * 128],
                                                             identity=identf[0:n, 0:n]),
                                 reads=[('xs', i % 2)], writes=[('ps', bank)], inc=(j == 3))
                        eng = 'act' if g % 2 == 0 else 'dve'
                        src_ps = PS[:, bank, :].rearrange("p (j n) -> p j n", j=4)[:, :, 0:n]
                        dst = xT[:, g * 4:g * 4 + 4, c0:c0 + n]
                        if eng == 'act':
                            t.op('act', lambda e: e.copy(out=dst, in_=src_ps), reads=[('ps', bank)],
                                 writes=[('xT', kc_, bi) for kc_ in range(g * 4, g * 4 + 4)])
                        else:
                            t.op('dve', lambda e: e.tensor_copy(out=dst, in_=src_ps), reads=[('ps', bank)],
                                 writes=[('xT', kc_, bi) for kc_ in range(g * 4, g * 4 + 4)])

            def norm(gidx, tbs, out_f32=False):
                for bi, (c0, n) in enumerate(tbs):
                    for kc in range(16):
                        t.op('pool', lambda e: e.tensor_tensor(out=SQ[:, kc, 0:n], in0=xT[:, kc, c0:c0 + n],
                                                               in1=xT[:, kc, c0:c0 + n], op=ALU.mult),
                             reads=[('xT', kc, bi)], writes=[('sq', kc)])
                    for kc in range(16):
                        t.op('pe', mm(PS[:, 7, 0:n], onesb, SQ[:, kc, 0:n], kc == 0, kc == 15),
                             reads=[('sq', kc), 'onesb'], writes=[('ps', 7)], inc=(kc == 15))
                    t.op('act', lambda e: e.activation(out=RS[:, 0:n], in_=PS[:, 7, 0:n], func=AF.Sqrt,
                                                       scale=1.0 / D, bias=eps6), reads=[('ps', 7)], writes=['rs'])
                    t.op('dve', lambda e: e.reciprocal(out=RSTD[:, 0:n], in_=RS[:, 0:n]), reads=['rs'], writes=['rstd'])
                    for kc in range(16):
                        if out_f32:
                            t.op('dve', lambda e: e.scalar_tensor_tensor(out=xT[:, kc, c0:c0 + n], in0=xT[:, kc, c0:c0 + n],
                                                                         scalar=gains[:, gidx, kc:kc + 1], in1=RSTD[:, 0:n],
                                                                         op0=ALU.mult, op1=ALU.mult),
                                 reads=[('xT', kc, bi), 'rstd'], writes=[('xT', kc, bi)])
                        else:
                            t.op('dve', lambda e: e.scalar_tensor_tensor(out=hT[:, kc, c0:c0 + n], in0=xT[:, kc, c0:c0 + n],
                                                                         scalar=gains[:, gidx, kc:kc + 1], in1=RSTD[:, 0:n],
                                                                         op0=ALU.mult, op1=ALU.mult),
                                 reads=[('xT', kc, bi), 'rstd'], writes=[('hT', kc, bi)])

            def ffn(wg, wu, wd, tbs, head_only=False, skip_head=False):
                wgv = wg.rearrange("(k p) f -> p k f", p=128)
                wuv = wu.rearrange("(k p) f -> p k f", p=128)
                wdv = wd.rearrange("(c p) d -> p c d", p=128)
                NFB = 22

                def prefetch(fb):
                    nfc = 2 if fb < 21 else 1
                    f0 = fb * 256
                    nf = nfc * 128
                    b = fb % 2
                    t.dma('pool', lambda e: e.dma_start(out=WG[b][:, :, 0:nf], in_=wgv[:, :, f0:f0 + nf]), writes=[('wg', b)])
                    t.dma('pool', lambda e: e.dma_start(out=WU[b][:, :, 0:nf], in_=wuv[:, :, f0:f0 + nf]), writes=[('wu', b)])
                    t.dma('pool', lambda e: e.dma_start(out=WD[b][:, 0:nfc, :], in_=wdv[:, fb * 2:fb * 2 + nfc, :]),
                          writes=[('wd', b)])

                ntb = len(tbs)
                units = [(fb, bi) for fb in range(NFB) for bi in range(ntb)]

                def stage1(u):
                    fb, bi = units[u]
                    c0, n = tbs[bi]
                    nfc = 2 if fb < 21 else 1
                    b = fb % 2
                    ab = u % 2
                    for fc in range(nfc):
                        gb = fc % 2
                        ub = 2 + fc % 2
                        for kc in range(16):
                            t.op('pe', mm(PS[:, gb, 0:n], WG[b][:, kc, fc * 128:(fc + 1) * 128], hT[:, kc, c0:c0 + n],
                                          kc == 0, kc == 15),
                                 reads=[('wg', b), ('hT', kc, bi)], writes=[('ps', gb)], inc=(kc == 15))
                        for kc in range(16):
                            t.op('pe', mm(PS[:, ub, 0:n], WU[b][:, kc, fc * 128:(fc + 1) * 128], hT[:, kc, c0:c0 + n],
                                          kc == 0, kc == 15),
                                 reads=[('wu', b), ('hT', kc, bi)], writes=[('ps', ub)], inc=(kc == 15))
                        t.op('act', lambda e: e.activation(out=SG[fc % 2][:, 0:n], in_=PS[:, gb, 0:n], func=AF.Silu),
                             reads=[('ps', gb)], writes=[('sg', fc % 2)])
                        t.op('dve', lambda e: e.tensor_tensor(out=AT[ab][:, fc, 0:n], in0=SG[fc % 2][:, 0:n],
                                                              in1=PS[:, ub, 0:n], op=ALU.mult),
                             reads=[('sg', fc % 2), ('ps', ub)], writes=[('aT', ab, fc)])

                def stage2(u):
                    fb, bi = units[u]
                    c0, n = tbs[bi]
                    nfc = 2 if fb < 21 else 1
                    b = fb % 2
                    ab = u % 2
                    for dc in range(16):
                        yb = 4 + dc % 4
                        for fc in range(nfc):
                            t.op('pe', mm(PS[:, yb, 0:n], WD[b][:, fc, dc * 128:(dc + 1) * 128], AT[ab][:, fc, 0:n],
                                          fc == 0, fc == nfc - 1),
                                 reads=[('wd', b), ('aT', ab, fc)], writes=[('ps', yb)], inc=(fc == nfc - 1))
                        t.op('dve', lambda e: e.scalar_tensor_tensor(out=xT[:, dc, c0:c0 + n], in0=PS[:, yb, 0:n],
                                                                     scalar=0.5, in1=xT[:, dc, c0:c0 + n],
                                                                     op0=ALU.mult, op1=ALU.add),
                             reads=[('ps', yb), ('xT', dc, bi)], writes=[('xT', dc, bi)])

                if not skip_head:
                    prefetch(0)
                    prefetch(1)
                if head_only:
                    return
                stage1(0)
                for u in range(len(units)):
                    if u + 1 < len(units):
                        stage1(u + 1)
                    stage2(u)
                    fb, bi = units[u]
                    if bi == ntb - 1 and fb + 2 < NFB:
                        prefetch(fb + 2)

            winv = win.rearrange("(k p) f -> p k f", p=128)
            pcnt = [0]

            def load_win(cb, b):
                t.dma('pool', lambda e: e.dma_start(out=WIN[b], in_=winv[:, :, cb * 512:(cb + 1) * 512]), writes=[('win', b)])

            def proj_fm(b, c, tbs, evac):
                for bi, (c0, n) in enumerate(tbs):
                    bank = pcnt[0] % 4
                    pcnt[0] += 1
                    for kc in range(16):
                        t.op('pe', mm(PS[:, bank, 0:n], WIN[b][:, kc, c * 128:(c + 1) * 128], hT[:, kc, c0:c0 + n],
                                      kc == 0, kc == 15),
                             reads=[('win', b), ('hT', kc, bi)], writes=[('ps', bank)], inc=(kc == 15))
                    evac(bank, bi, c0, n)

            def proj_tm(b, tiles, evac):
                for (tt, c0, n) in tiles:
                    bank = 4 + pcnt[0] % 4
                    pcnt[0] += 1
                    bi = c0 // 512
                    for kc in range(16):
                        t.op('pe', mm(PS[0:n, bank, :], hT[:, kc, c0:c0 + n], WIN[b][:, kc, :], kc == 0, kc == 15),
                             reads=[('win', b), ('hT', kc, bi)], writes=[('ps', bank)], inc=(kc == 15))
                    evac(bank, tt, n)

            ALL_TILES = [(tt, tt * 128, 128) for tt in range(8)] + [(8, 1024, 80)]
            scnt = [0]

            def stage_out(bank, n, dsts):
                si = scnt[0] % 2
                scnt[0] += 1
                t.op('act', lambda e: e.copy(out=STG[si][0:n, :], in_=PS[0:n, bank, :]), reads=[('ps', bank)],
                     writes=[('stg', si)])
                for (dst, p0, p1) in dsts:
                    t.dma('sp', lambda e: e.dma_start(out=dst, in_=STG[si][p0:p1, :]), reads=[('stg', si)])
                return si

            ffn(w1g, w1u, w1d, PREV_TBS, head_only=True)
            load_xT(xprev, [(i * 128, 128, i * 128) for i in range(8)])
            norm(0, PREV_TBS)
            ffn(w1g, w1u, w1d, PREV_TBS, skip_head=True)
            norm(1, PREV_TBS)
            ckpt(4)
            load_win(2, 0)
            for i, cb in enumerate([2, 3, 4, 5]):
                b = i % 2
                if i + 1 < 4:
                    load_win([2, 3, 4, 5][i + 1], (i + 1) % 2)
                if cb in (2, 3):
                    for c in range(4):
                        hh = (cb - 2) * 4 + c

                        def ev(bank, bi, c0, n, hh=hh):
                            t.op('act', lambda e: e.copy(out=KTp_tmp[:, hh, c0:c0 + n], in_=PS[:, bank, 0:n]),
                                 reads=[('ps', bank)], writes=[('ktp', hh, bi)])
                        proj_fm(b, c, PREV_TBS, ev)
                else:
                    col0 = (cb - 4) * 512

                    def ev(bank, tt, n, col0=col0):
                        t.op('act', lambda e: e.copy(out=Vp_tmp[0:n, tt, col0:col0 + 512], in_=PS[0:n, bank, :]),
                             reads=[('ps', bank)], writes=[('vp', tt, col0)])
                    proj_tm(b, ALL_TILES[:8], ev)
            ckpt(5)
            t.dma('sp', lambda e: e.dma_start(out=kvspill[:, 0:8192], in_=KTp_tmp.rearrange("p k n -> p (k n)")))
            t.dma('sp', lambda e: e.dma_start(out=kvspill[:, 8192:16384], in_=Vp_tmp.rearrange("p k n -> p (k n)")))
            ckpt(6)

            tiles = [(i * 128, 128, i * 128) for i in range(8)] + [(1024, 80, 1024)]
            ffn(w1g, w1u, w1d, TBS, head_only=True)
            load_xT(xmain, tiles)
            norm(0, TBS)
            ffn(w1g, w1u, w1d, TBS, skip_head=True)
            norm(1, TBS)
            ckpt(9)
            t.dma('sp', lambda e: e.dma_start(out=xspill, in_=xT.rearrange("p k n -> p (k n)")))
            ckpt(10)

            stflat = state.rearrange("b r f -> (b r) f")
            for half in range(2):
                t.dma('sp', lambda e: e.dma_start(out=XS[0][0:120, 0:1024], in_=stflat[half * 120:(half + 1) * 120, :]),
                      writes=['xs0'])
                for g in range(2):
                    bank = g
                    for j in range(4):
                        c = g * 4 + j
                        t.op('pe', lambda e: e.transpose(out=PS[:, bank, j * 128:j * 128 + 120],
                                                         in_=XS[0][0:120, c * 128:(c + 1) * 128],
                                                         identity=identf[0:120, 0:120]),
                             reads=['xs0'], writes=[('ps', bank)], inc=(j == 3))
                    t.op('act', lambda e: e.copy(out=stT[:, g * 4:g * 4 + 4, half * 120:(half + 1) * 120],
                                                 in_=PS[:, bank, :].rearrange("p (j n) -> p j n", j=4)[:, :, 0:120]),
                         reads=[('ps', bank)], writes=['stT'])
            ckpt(101)
            t.dma('pool', lambda e: e.dma_start(out=WP, in_=wpool.rearrange("g (k p) n -> p g k n", p=128)), writes=['wp'])
            t.op('dve', lambda e: e.memset(DT[:, :, TP:SOFF], 0.0), writes=['dthalo'])
            t.op('dve', lambda e: e.memset(QTa[64:128, :, :], 0.0), writes=['qtaz'])
            t.op('dve', lambda e: e.memset(QTb[0:64, :, :], 0.0), writes=['qtbz'])

            ckpt(102)
            CBS = [6, 7, 0, 1, 2, 3, 4, 5]
            load_win(CBS[0], 0)
            for i, cb in enumerate(CBS):
                if i > 0 and STOP >= 103:
                    ckpt(102 + i)
                b = i % 2
                if i + 1 < len(CBS):
                    load_win(CBS[i + 1], (i + 1) % 2)
                col0 = (cb % 2) * 512
                if cb >= 6:
                    for c in range(4):
                        uc = (cb - 6) * 4 + c
                        g = uc // 2
                        kk = uc % 2
                        w = 2 ** (g + 1)

                        def ev(bank, bi, c0, n):
                            if bi < 2:
                                t.op('act', lambda e: e.copy(out=EXP_[0][:, 16 + c0:16 + c0 + n], in_=PS[:, bank, 0:n]),
                                     reads=[('ps', bank)], writes=[('exp0', bi)])
                            else:
                                t.op('act', lambda e: e.copy(out=EXP_[0][:, 0:16], in_=PS[:, bank, 0:16]),
                                     reads=[('ps', bank)], writes=[('exp0', 2)])
                                t.op('act', lambda e: e.copy(out=EXS[0][:, :, 15:19],
                                                             in_=PS[:, bank, 16:80].rearrange("p (b q) -> p b q", b=16)),
                                     reads=[('ps', bank)], writes=['exs0n'])
                        proj_fm(b, c, TBS, ev)
                        t.op('act', lambda e: e.copy(out=EXS[0][:, :, 0:15],
                                                     in_=stT[:, uc, :].rearrange("p (b r) -> p b r", b=16)),
                             reads=['stT'], writes=['exs0s'])
                        cur = 0
                        rk = [('exp0', 0), ('exp0', 1), ('exp0', 2)]
                        rks = ['exs0n', 'exs0s']
                        s = 1
                        for stp in range(g + 1):
                            nxt = 1 if cur != 1 else 2
                            t.op('dve', lambda e: e.tensor_tensor(out=EXP_[nxt][:, s:1040], in0=EXP_[cur][:, s:1040],
                                                                  in1=EXP_[cur][:, 0:1040 - s], op=ALU.add),
                                 reads=rk, writes=[('exp', nxt)])
                            t.op('pool', lambda e: e.tensor_tensor(out=EXS[nxt][:, :, s:19], in0=EXS[cur][:, :, s:19],
                                                                   in1=EXS[cur][:, :, 0:19 - s], op=ALU.add),
                                 reads=rks, writes=[('exs', nxt)])
                            cur = nxt
                            rk = [('exp', cur)]
                            rks = [('exs', cur)]
                            s *= 2
                        t.op('dve', lambda e: e.scalar_tensor_tensor(out=DT[:, kk, 0:TP], in0=EXP_[cur][:, 16:1040],
                                                                     scalar=1.0 / w, in1=EXP_[0][:, 16:1040],
                                                                     op0=ALU.mult, op1=ALU.subtract),
                             reads=rk + [('exp0', 0), ('exp0', 1)], writes=[('dt', kk)])
                        t.op('dve', lambda e: e.tensor_tensor(out=TMP16, in0=EXP_[cur][:, 16:32], in1=invc[:, g, :], op=ALU.mult),
                             reads=rk + ['invc'], writes=['tmp16'])
                        t.op('dve', lambda e: e.tensor_tensor(out=DT[:, kk, 0:16], in0=TMP16, in1=EXP_[0][:, 16:32],
                                                              op=ALU.subtract),
                             reads=['tmp16', ('exp0', 0)], writes=[('dt', kk)])
                        t.op('dve', lambda e: e.scalar_tensor_tensor(
                            out=DT[:, kk, SOFF:T].rearrange("p (b q) -> p b q", b=16), in0=EXS[cur][:, :, 15:19],
                            scalar=1.0 / w, in1=EXS[0][:, :, 15:19], op0=ALU.mult, op1=ALU.subtract),
                            reads=rks + ['exs0n'], writes=[('dt', kk)])
                        if kk == 1:
                            for oc in range(2):
                                for bi, (c0, n) in enumerate(TBS):
                                    bank = pcnt[0] % 4
                                    pcnt[0] += 1
                                    for k2 in range(2):
                                        t.op('pe', mm(PS[:, bank, 0:n], WP[:, g, k2, oc * 128:(oc + 1) * 128],
                                                      DT[:, k2, c0:c0 + n], k2 == 0, k2 == 1),
                                             reads=['wp', ('dt', k2), 'dthalo'], writes=[('ps', bank)], inc=(k2 == 1))
                                    zc = 2 * g + oc
                                    t.op('act', lambda e: e.activation(out=zT[:, zc, c0:c0 + n], in_=PS[:, bank, 0:n],
                                                                       func=AF.Copy, scale=pscale[:, zc:zc + 1]),
                                         reads=[('ps', bank)], writes=[('zT', zc, bi)])

                    def ev(bank, tt, n, col0=col0):
                        if tt == 7:
                            stage_out(bank, n, [(pool_p[:, col0:col0 + 512], 113, 128)])
                        else:
                            stage_out(bank, n, [(pool_s[bb, 11:15, col0:col0 + 512], 16 + 4 * bb, 20 + 4 * bb)
                                                for bb in range(NB)])
                    proj_tm(b, ALL_TILES[7:9], ev)
                elif cb in (0, 1, 2, 3):
                    for c in range(4):
                        hh = (cb % 2) * 4 + c

                        def ev(bank, bi, c0, n, hh=hh, isq=(cb < 2)):
                            if isq:
                                t.op('act', lambda e: e.copy(out=QTa[0:64, hh, c0:c0 + n], in_=PS[0:64, bank, 0:n]),
                                     reads=[('ps', bank)], writes=[('QTa', hh, bi)])
                                t.op('act', lambda e: e.copy(out=QTb[64:128, hh, c0:c0 + n], in_=PS[64:128, bank, 0:n]),
                                     reads=[('ps', bank)], writes=[('QTb', hh, bi)])
                            else:
                                t.op('act', lambda e: e.copy(out=KT[:, hh, c0:c0 + n], in_=PS[:, bank, 0:n]),
                                     reads=[('ps', bank)], writes=[('KT', hh, bi)])
                        proj_fm(b, c, TBS, ev)
                    if cb >= 2:
                        def ev(bank, tt, n, col0=col0):
                            if tt < 8:
                                stage_out(bank, n, [(k_p[tt * 128:(tt + 1) * 128, col0:col0 + 512], 0, 128)])
                            else:
                                stage_out(bank, n, [(k_s[:, col0:col0 + 512], 16, 80)])
                        proj_tm(b, ALL_TILES, ev)
                else:
                    def ev(bank, tt, n, col0=col0):
                        if tt < 8:
                            si = stage_out(bank, n, [(v_p[tt * 128:(tt + 1) * 128, col0:col0 + 512], 0, 128)])
                        else:
                            si = stage_out(bank, n, [(v_s[:, col0:col0 + 512], 16, 80)])
                        t.op('pool', lambda e: e.tensor_copy(out=Vt[0:n, tt, col0:col0 + 512], in_=STG[si][0:n, :]),
                             reads=[('stg', si)], writes=[('V', tt, col0)])
                    proj_tm(b, ALL_TILES, ev)
            ckpt(11)

            t.dma('sp', lambda e: e.dma_start(out=KTprev.rearrange("p k n -> p (k n)"), in_=kvspill[:, 0:8192]), writes=['kvp'])
            t.dma('sp', lambda e: e.dma_start(out=Vprev.rearrange("p k n -> p (k n)"), in_=kvspill[:, 8192:16384]), writes=['kvp'])
            for h in range(H):
                for qs in range(2):
                    q0 = qs * 512
                    blocks = [('prev', kb) for kb in range(8)] + [('own', kb) for kb in range(4 * qs + 4)]
                    nblk = len(blocks)
                    def blk(i):
                        src, kb = blocks[i]
                        r = kb - 4 * qs if src == 'own' else -1
                        off = max(r, 0) * 128
                        n = 512 - off
                        if src == 'prev':
                            return (r, off, n, KTprev[:, h, kb * 128:(kb + 1) * 128],
                                    Vprev[:, kb, h * 128:(h + 1) * 128], pbias, ['kvp'])
                        return (r, off, n, KT[:, h, kb * 128:(kb + 1) * 128],
                                Vt[:, kb, h * 128:(h + 1) * 128], zerob, [])

                    def s_stage(i):
                        r, off, n, Ksrc, Vsrc, bias, rkeys = blk(i)
                        sb = i % 2
                        t.op('pe', mm(PS[:, sb, 0:n], Ksrc, QTa[:, h, q0 + off:q0 + 512], True, True),
                             reads=rkeys, writes=[('ps', sb)])
                        t.op('pe', mm(PS[:, 2 + sb, 0:n], Ksrc, QTb[:, h, q0 + off:q0 + 512], True, True),
                             reads=rkeys, writes=[('ps', 2 + sb)])
                        t.op('act', lambda e: e.activation(out=PT1[sb][:, 0:n], in_=PS[:, sb, 0:n], func=AF.Exp,
                                                           scale=SCALE, bias=bias), reads=[('ps', sb)], writes=[('pt1', sb)])
                        t.op('act', lambda e: e.activation(out=PT2[sb][:, 0:n], in_=PS[:, 2 + sb, 0:n], func=AF.Exp,
                                                           scale=SCALE, bias=bias), reads=[('ps', 2 + sb)], writes=[('pt2', sb)])
                        if r >= 0:
                            t.op('pool', lambda e: e.tensor_tensor(out=PT1[sb][:, 0:128], in0=PT1[sb][:, 0:128], in1=trib,
                                                                   op=ALU.mult), reads=[('pt1', sb)], writes=[('pt1', sb)])
                            t.op('pool', lambda e: e.tensor_tensor(out=PT2[sb][:, 0:128], in0=PT2[sb][:, 0:128], in1=trib,
                                                                   op=ALU.mult), reads=[('pt2', sb)], writes=[('pt2', sb)])

                    def pv_stage(i):
                        r, off, n, Ksrc, Vsrc, bias, rkeys = blk(i)
                        sb = i % 2
                        st_ = (i == 0)
                        sp_ = (i == nblk - 1)
                        t.op('pe', mm(PS[:, 4, off:512], Vsrc, PT1[sb][:, 0:n], st_, sp_), reads=[('pt1', sb)] + rkeys,
                             writes=[('ps', 4)], inc=False)
                        t.op('pe', mm(PS[:, 6, off:512], onesb, PT1[sb][:, 0:n], st_, sp_), reads=[('pt1', sb)],
                             writes=[('ps', 6)])
                        t.op('pe', mm(PS[:, 5, off:512], Vsrc, PT2[sb][:, 0:n], st_, sp_), reads=[('pt2', sb)] + rkeys,
                             writes=[('ps', 5)], inc=False)
                        t.op('pe', mm(PS[:, 7, off:512], onesb, PT2[sb][:, 0:n], st_, sp_), reads=[('pt2', sb)],
                             writes=[('ps', 7)])

                    s_stage(0)
                    for i in range(nblk):
                        if i + 1 < nblk:
                            s_stage(i + 1)
                        pv_stage(i)
                    t.op('dve', lambda e: e.reciprocal(out=FR1, in_=PS[:, 6, :]), reads=[('ps', 6)], writes=['fr1'])
                    t.op('dve', lambda e: e.reciprocal(out=FR2, in_=PS[:, 7, :]), reads=[('ps', 7)], writes=['fr2'])
                    t.op('dve', lambda e: e.tensor_scalar(out=FR2, in0=FR2, scalar1=neglam[:, 0:1], scalar2=None, op0=ALU.mult),
                         reads=['fr2'], writes=['fr2'])
                    t.op('dve', lambda e: e.tensor_tensor(out=FO, in0=PS[:, 4, :], in1=FR1, op=ALU.mult),
                         reads=[('ps', 4), 'fr1'], writes=['fo'])
                    t.op('dve', lambda e: e.tensor_tensor(out=FO2, in0=PS[:, 5, :], in1=FR2, op=ALU.mult),
                         reads=[('ps', 5), 'fr2'], writes=['fo2'])
                    t.op('dve', lambda e: e.tensor_tensor(out=FO, in0=FO, in1=FO2, op=ALU.add),
                         reads=['fo', 'fo2'], writes=['fo'])
                    t.op('pool', lambda e: e.tensor_tensor(out=FSQ, in0=FO, in1=FO, op=ALU.mult), reads=['fo'], writes=['fsq'])
                    t.op('pe', mm(PS[:, 6, :], onesb, FSQ, True, True), reads=['fsq'], writes=[('ps', 6)])
                    t.op('act', lambda e: e.activation(out=FRS, in_=PS[:, 6, :], func=AF.Sqrt, scale=1.0 / 128, bias=eps5),
                         reads=[('ps', 6)], writes=['frs'])
                    t.op('dve', lambda e: e.reciprocal(out=FRS, in_=FRS), reads=['frs'], writes=['frs'])
                    t.op('dve', lambda e: e.scalar_tensor_tensor(out=hT[:, h, q0:q0 + 512], in0=FO, scalar=gsub[:, 0:1],
                                                                 in1=FRS, op0=ALU.mult, op1=ALU.mult),
                         reads=['fo', 'frs'], writes=[('oT', h, qs)])
            ckpt(12)

            RING = 8
            Kr = [V_("R3", i * 2048, BF16, 1024) for i in range(RING)]
            Vr = [V_("R3", 16384 + i * 2048, BF16, 1024) for i in range(RING)]
            KTs2 = [V_("R3", 32768 + i * 2048, BF16, 1024) for i in range(2)]
            PTd = [V_("R3", 36864 + i * 2304, BF16, 1152) for i in range(2)]
            t.op('dve', lambda e: e.memset(QBLK.rearrange("p b h e -> p (b h e)"), 0.0), writes=['qblk'])
            for h in range(H):
                t.op('dve', lambda e: e.tensor_copy(out=QBLK[0:64, :, h, 0:4],
                                                    in_=QTa[0:64, h, SOFF:T].rearrange("p (b q) -> p b q", b=NB)),
                     reads=['qblk'], writes=['qblk'])
                t.op('dve', lambda e: e.tensor_copy(out=QBLK[64:128, :, h, 4:8],
                                                    in_=QTb[64:128, h, SOFF:T].rearrange("p (b q) -> p b q", b=NB)),
                     reads=['qblk'], writes=['qblk'])
            t.op('dve', lambda e: e.memset(PTd[0], 0.0), writes=[('ptsall', 0)])
            t.op('dve', lambda e: e.memset(PTd[1], 0.0), writes=[('ptsall', 1)])
            t.op('dve', lambda e: e.memset(PTsumH, 0.0), writes=['ptsumh'])
            t.op('dve', lambda e: e.memset(PTsumL, 0.0), writes=['ptsuml'])
            t.op('dve', lambda e: e.memset(PTnb, 0.0), writes=['ptnb'])
            t.op('dve', lambda e: e.memset(Vnew[0], 0.0), writes=[('vn', 0)])
            t.op('dve', lambda e: e.memset(Vnew[1], 0.0), writes=[('vn', 1)])
            t.op('dve', lambda e: e.memset(OSAMP, 0.0), writes=['osamp'])

            def stage_T(g):
                bb, j = g // 16, g % 16
                col = bb * 16 + j
                kp = Kr[g % RING]
                vp = Vr[g % RING]
                if j == 0:
                    vn_ = Vnew[bb % 2]
                    t.dma('sp', lambda e: e.dma_start(out=vn_[0:4, :], in_=Vt[16 + 4 * bb:20 + 4 * bb, 8, :]),
                          writes=[('vn', bb % 2)])
                t.dma('pool', lambda e: e.indirect_dma_start(
                    out=kp, out_offset=None, in_=ck,
                    in_offset=bass.IndirectOffsetOnAxis(ap=idx[:, col:col + 1], axis=0)), writes=[('kr', g % RING)])
                t.dma('pool', lambda e: e.indirect_dma_start(
                    out=vp, out_offset=None, in_=cv,
                    in_offset=bass.IndirectOffsetOnAxis(ap=idx[:, col:col + 1], axis=0)), writes=[('vr', g % RING)])
                for h in range(H):
                    t.op('pe', lambda e: e.transpose(out=PSb(6)[:, h * 128:(h + 1) * 128],
                                                     in_=kp[:, h * 128:(h + 1) * 128], identity=identb),
                         reads=[('kr', g % RING)], writes=[('ps', 6)], inc=(h == H - 1))
                t.op('act', lambda e: e.copy(out=KTs2[g % 2], in_=PSb(6)), reads=[('ps', 6)], writes=[('kts', g % 2)])
                sbk = j % 2
                for h in range(H):
                    c_ = ((j // 2) * 8 + h) * 8
                    t.op('pe', mm(PS[:, sbk, c_:c_ + 8], KTs2[g % 2][:, h * 128:(h + 1) * 128], QBLK[:, bb, h, :],
                                  (j // 2 == 0 and h == 0), False, skip_group_check=True),
                         reads=[('kts', g % 2), 'qblk'], writes=[('ps', sbk)], inc=(h == H - 1))
                c0_ = (j // 2) * 64
                t.op('act', lambda e: e.activation(out=PTd[bb % 2][:, j * 64:(j + 1) * 64], in_=PS[:, sbk, c0_:c0_ + 64],
                                                   func=AF.Exp, scale=SCALE),
                     reads=[('ps', sbk)], writes=[('pts', bb % 2, j)])

            def stage_PV(g):
                bb, j = g // 16, g % 16
                vp = Vr[g % RING]
                for h in range(H):
                    for half in range(2):
                        ob = 2 + half * 2 + h // 4
                        first = (j == 0 and h % 4 == 0)
                        c_ = (j * 8 + h) * 8 + half * 4
                        t.op('pe', mm(PS[:, ob, (h % 4) * 128:(h % 4 + 1) * 128], PTd[bb % 2][:, c_:c_ + 128],
                                      vp[:, h * 128:(h + 1) * 128], first, False, skip_group_check=True),
                             reads=[('pts', bb % 2, j), ('ptsall', bb % 2), ('vr', g % RING)], writes=[('ps', ob)],
                             inc=(h == 7 and half == 1))

            def tail(bb):
                s0 = SOFF + 4 * bb
                vn = Vnew[bb % 2]
                PTs_ = PTd[bb % 2]
                rkall = [('pts', bb % 2, j) for j in range(16)]
                t.op('dve', lambda e: e.tensor_reduce(out=PTsum, in_=PTs_[:, 0:1024].rearrange("p (j e) -> p e j", j=16),
                                                      axis=AX.X, op=ALU.add), reads=rkall, writes=['ptsum'])
                t.op('dve', lambda e: e.tensor_copy(out=PTsumH[:, 0:64], in_=PTsum), reads=['ptsum'], writes=['ptsumh'])
                t.op('dve', lambda e: e.tensor_tensor(out=PTsumL[:, 0:64], in0=PTsum, in1=PTsumH[:, 0:64], op=ALU.subtract),
                     reads=['ptsum', 'ptsumh'], writes=['ptsuml'])
                for h in range(H):
                    t.op('pe', mm(PS[:, 7, h * 8:(h + 1) * 8], KTflat[:, h * T + s0:h * T + s0 + 128], QBLK[:, bb, h, :],
                                  h == 0, h == 7, skip_group_check=True), reads=['qblk'], writes=[('ps', 7)], inc=(h == 7))
                t.op('act', lambda e: e.activation(out=PTn32[0:4, :], in_=PS[0:4, 7, 0:64], func=AF.Exp, scale=SCALE),
                     reads=[('ps', 7)], writes=['ptn32'])
                t.op('dve', lambda e: e.tensor_tensor(out=PTn32[0:4, :], in0=PTn32[0:4, :], in1=smask[0:4, :], op=ALU.mult),
                     reads=['ptn32'], writes=['ptn32'])
                t.op('dve', lambda e: e.tensor_copy(out=PTnb[0:4, 0:64], in_=PTn32[0:4, :]), reads=['ptn32'], writes=['ptnb'])
                for h in range(H):
                    for half in range(2):
                        ob = 2 + half * 2 + h // 4
                        c_ = h * 8 + half * 4
                        t.op('pe', mm(PS[:, ob, (h % 4) * 128:(h % 4 + 1) * 128], PTnb[:, c_:c_ + 128],
                                      vn[:, h * 128:(h + 1) * 128], False, True, skip_group_check=True),
                             reads=['ptnb', ('vn', bb % 2)], writes=[('ps', ob)], inc=(h == 7 and half == 1))
                for h in range(H):
                    for half in range(2):
                        c_ = 64 + half * 8 + h
                        e0 = h * 8 + half * 4
                        first = (h == 0 and half == 0)
                        t.op('pe', mm(PS[:, 7, c_:c_ + 1], PTsumH[:, e0:e0 + 128], onesb[:, 0:1], first, False,
                                      skip_group_check=True), reads=['ptsumh', 'onesb', 'ptn32'], writes=[('ps', 7)], inc=False)
                        t.op('pe', mm(PS[:, 7, c_:c_ + 1], PTsumL[:, e0:e0 + 128], onesb[:, 0:1], False, False,
                                      skip_group_check=True), reads=['ptsuml'], writes=[('ps', 7)], inc=False)
                        t.op('pe', mm(PS[:, 7, c_:c_ + 1], PTnb[:, e0:e0 + 128], onesb[:, 0:1], False,
                                      (h == 7 and half == 1), skip_group_check=True), reads=['ptnb'], writes=[('ps', 7)],
                             inc=(h == 7 and half == 1))
                t.op('dve', lambda e: e.reciprocal(out=SSM[0:4, 0:16], in_=PS[0:4, 7, 64:80]), reads=[('ps', 7)], writes=['ssm'])
                t.op('dve', lambda e: e.tensor_scalar(out=SSM[0:4, 8:16], in0=SSM[0:4, 8:16], scalar1=neglam[0:4, 0:1],
                                                      scalar2=None, op0=ALU.mult), reads=['ssm'], writes=['ssm'])
                for k in range(2):
                    t.op('dve', lambda e: e.tensor_tensor(
                        out=SO[0:4, k * 512:(k + 1) * 512].rearrange("p (h d) -> p h d", h=4),
                        in0=PS[0:4, 2 + k, :].rearrange("p (h d) -> p h d", h=4),
                        in1=SSM[0:4, 4 * k:4 * k + 4].unsqueeze(2).to_broadcast([4, 4, 128]), op=ALU.mult),
                        reads=[('ps', 2 + k), 'ssm'], writes=[('so', k)])
                    t.op('dve', lambda e: e.tensor_tensor(
                        out=SO2[0:4, k * 512:(k + 1) * 512].rearrange("p (h d) -> p h d", h=4),
                        in0=PS[0:4, 4 + k, :].rearrange("p (h d) -> p h d", h=4),
                        in1=SSM[0:4, 8 + 4 * k:12 + 4 * k].unsqueeze(2).to_broadcast([4, 4, 128]), op=ALU.mult),
                        reads=[('ps', 4 + k), 'ssm'], writes=[('so2', k)])
                t.op('dve', lambda e: e.tensor_tensor(out=SO[0:4, :], in0=SO[0:4, :], in1=SO2[0:4, :], op=ALU.add),
                     reads=[('so', 0), ('so', 1), ('so2', 0), ('so2', 1)], writes=['sof'])
                t.op('dve', lambda e: e.tensor_tensor(out=SSQ[0:4, :], in0=SO[0:4, :], in1=SO[0:4, :], op=ALU.mult),
                     reads=['sof'], writes=['ssq'])
                t.op('dve', lambda e: e.tensor_reduce(out=SSM[0:4, 16:24], in_=SSQ[0:4, :].rearrange("p (h d) -> p h d", h=8),
                                                      axis=AX.X, op=ALU.add), reads=['ssq'], writes=['ssm2'])
                t.op('act', lambda e: e.activation(out=SSM[0:4, 24:32], in_=SSM[0:4, 16:24], func=AF.Sqrt, scale=1.0 / 128,
                                                   bias=eps5[0:4, :]), reads=['ssm2'], writes=['ssm3'])
                t.op('dve', lambda e: e.reciprocal(out=SSM[0:4, 32:40], in_=SSM[0:4, 24:32]), reads=['ssm3'], writes=['ssm4'])
                t.op('dve', lambda e: e.tensor_tensor(
                    out=SSQ[0:4, :].rearrange("p (h d) -> p h d", h=8), in0=SO[0:4, :].rearrange("p (h d) -> p h d", h=8),
                    in1=SSM[0:4, 32:40].unsqueeze(2).to_broadcast([4, 8, 128]), op=ALU.mult),
                    reads=['sof', 'ssm4', 'ssq'], writes=['ssq'])
                t.op('dve', lambda e: e.tensor_tensor(
                    out=SSQ[0:4, :].rearrange("p (h d) -> p h d", h=8), in0=SSQ[0:4, :].rearrange("p (h d) -> p h d", h=8),
                    in1=gsubrow[0:4, :].unsqueeze(1).to_broadcast([4, 8, 128]), op=ALU.mult),
                    reads=['ssq', 'gsubrow'], writes=['ssq'])
                t.dma('sp', lambda e: e.dma_start(out=OSAMP[4 * bb:4 * bb + 4, :], in_=SSQ[0:4, :]), reads=['ssq'],
                      writes=['osamp'])

            NG = NB * 16
            stage_T(0)
            for g in range(NG):
                if g + 1 < NG:
                    stage_T(g + 1)
                stage_PV(g)
                if g % 16 == 15:
                    tail(g // 16)
            for g in range(2):
                for j in range(4):
                    h = g * 4 + j
                    t.op('pe', lambda e: e.transpose(out=PS[:, g, j * 128:(j + 1) * 128], in_=OSAMP[:, h * 128:(h + 1) * 128],
                                                     identity=identf),
                         reads=['osamp'], writes=[('ps', g)], inc=(j == 3))
                t.op('act', lambda e: e.copy(out=hT[:, g * 4:g * 4 + 4, SOFF:T],
                                             in_=PS[:, g, :].rearrange("p (j n) -> p j n", j=4)[:, :, 0:64]),
                     reads=[('ps', g)], writes=[('oTs', g)])
            ckpt(13)

            t.dma('sp', lambda e: e.dma_start(out=xT.rearrange("p k n -> p (k n)"), in_=xspill), writes=['xTall'])
            woutv = wout.rearrange("(k p) f -> p k f", p=128)

            def load_wo(cb, b):
                t.dma('pool', lambda e: e.dma_start(out=WIN[b], in_=woutv[:, :, cb * 512:(cb + 1) * 512]), writes=[('win', b)])
            load_wo(0, 0)
            for cb in range(4):
                b = cb % 2
                if cb + 1 < 4:
                    load_wo(cb + 1, (cb + 1) % 2)
                for c in range(4):
                    dc = cb * 4 + c
                    for bi, (c0, n) in enumerate(TBS):
                        bank = pcnt[0] % 4
                        pcnt[0] += 1
                        for kc in range(16):
                            rhs = hT[:, kc, c0:c0 + n] if kc < 8 else zT[:, kc - 8, c0:c0 + n]
                            t.op('pe', mm(PS[:, bank, 0:n], WIN[b][:, kc, c * 128:(c + 1) * 128], rhs, kc == 0, kc == 15),
                                 reads=[('win', b)], writes=[('ps', bank)], inc=(kc == 15))
                        t.op('dve', lambda e: e.tensor_tensor(out=xT[:, dc, c0:c0 + n], in0=PS[:, bank, 0:n],
                                                              in1=xT[:, dc, c0:c0 + n], op=ALU.add),
                             reads=[('ps', bank), 'xTall', ('xT', dc, bi)], writes=[('xT', dc, bi)])
            ckpt(14)

            ffn(w2g, w2u, w2d, TBS, head_only=True)
            norm(2, TBS)
            ffn(w2g, w2u, w2d, TBS, skip_head=True)
            norm(3, TBS, out_f32=True)
            for i, (tt, c0, n) in enumerate(ALL_TILES):
                st = XS[i % 2]
                for g in range(4):
                    bank = g
                    for j in range(4):
                        kc = g * 4 + j
                        t.op('pe', lambda e: e.transpose(out=PS[0:n, bank, j * 128:(j + 1) * 128], in_=xT[:, kc, c0:c0 + n],
                                                         identity=identf),
                             reads=[('xT', kc, c0 // 512)], writes=[('ps', bank)], inc=(j == 3))
                    eng = 'act' if g % 2 == 0 else 'dve'
                    if eng == 'act':
                        t.op('act', lambda e: e.copy(out=st[0:n, g * 512:(g + 1) * 512], in_=PS[0:n, bank, :]),
                             reads=[('ps', bank)], writes=[('xs', i % 2, g)])
                    else:
                        t.op('dve', lambda e: e.tensor_copy(out=st[0:n, g * 512:(g + 1) * 512], in_=PS[0:n, bank, :]),
                             reads=[('ps', bank)], writes=[('xs', i % 2, g)])
                rk = [('xs', i % 2, g) for g in range(4)]
                if tt < 8:
                    t.dma('sp', lambda e: e.dma_start(out=y_p[tt * 128:(tt + 1) * 128, :], in_=st[0:128, :]), reads=rk)
                else:
                    t.dma('sp', lambda e: e.dma_start(out=y_s, in_=st[16:80, :]), reads=rk)
            ckpt(18)

        try:
            body()
        except _Stop:
            pass
    return nc


_NC_CACHE = {}


def kernel(x_prompt, x_sample, cache_k, cache_v, state_pool, page_table,
           ffn1_norm, ffn1_w_gate, ffn1_w_up, ffn1_w_down,
           mix_norm, w_in, lambda_q1, lambda_k1, lambda_q2, lambda_k2,
           subln_gain, w_pool, pool_scale, w_out,
           ffn2_norm, ffn2_w_gate, ffn2_w_up, ffn2_w_down, final_norm):
    f = np.float32
    a = lambda v: np.ascontiguousarray(np.asarray(v))
    x_prompt = a(x_prompt); x_sample = a(x_sample)
    ckf = a(cache_k).reshape(NPHYS * 128, 1024)
    cvf = a(cache_v).reshape(NPHYS * 128, 1024)
    state_pool = a(state_pool); page_table = a(page_table).astype(np.int32)
    if "nc" not in _NC_CACHE:
        _NC_CACHE["nc"] = build_nc()
    nc = _NC_CACHE["nc"]

    def colmajor(v):
        return a(v).reshape(16, 128).T

    gains = np.concatenate([colmajor(ffn1_norm[0]), colmajor(mix_norm[0]), colmajor(ffn2_norm[0]),
                            colmajor(final_norm)], axis=1).astype(f)
    pscale = a(pool_scale)[0].reshape(8, 128).T.astype(f)
    subcol = a(subln_gain)[0].reshape(128, 1).astype(f)
    subrow = np.broadcast_to(a(subln_gain)[0][None, :], (128, 128)).astype(f)
    lams = np.broadcast_to(np.concatenate([a(lambda_q1)[0], a(lambda_k1)[0], a(lambda_q2)[0], a(lambda_k2)[0]])[None, :],
                           (128, 256)).astype(f)
    ident = np.eye(128, dtype=f)
    tri = (np.arange(128)[:, None] <= np.arange(128)[None, :]).astype(f)
    smask = np.zeros((128, 64), f)
    for k in range(4):
        for e in range(64):
            if k <= e % 4:
                smask[k, e] = 1.0
    shared = {
        "ck": ckf, "cv": cvf,
        "w1g": a(ffn1_w_gate)[0], "w1u": a(ffn1_w_up)[0], "w1d": a(ffn1_w_down)[0],
        "w2g": a(ffn2_w_gate)[0], "w2u": a(ffn2_w_up)[0], "w2d": a(ffn2_w_down)[0],
        "win": a(w_in)[0], "wpool": a(w_pool)[0], "wout": a(w_out)[0],
        "c_ident": ident, "c_tri": tri, "c_smask": smask, "c_gains": a(gains), "c_pscale": a(pscale),
        "c_subcol": subcol, "c_subrow": a(subrow), "c_lams": a(lams),
    }
    in_maps = []
    for c in range(8):
        b, h = c // 2, c % 2
        xm = np.zeros((T, D), f)
        xm[0:TP] = x_prompt[b, h * TP:(h + 1) * TP]
        if h == 1:
            xm[TP:TP + TH] = x_prompt[b, TP - TH:TP]
        xm[SOFF:T] = x_sample[NB * c:NB * (c + 1)].reshape(TS, D)
        xp = x_prompt[b, 0:TP] if h == 1 else np.zeros((TP, D), f)
        invc = np.zeros((128, 4, 16), f)
        for g in range(4):
            w = 2 ** (g + 1)
            for tkn in range(16):
                pos = h * TP + tkn
                invc[:, g, tkn] = 1.0 / min(w, pos + 1)
        m = dict(shared)
        m.update({
            "xmain": xm, "xprev": a(xp),
            "state": a(state_pool[0, NB * c:NB * (c + 1)]),
            "ptab": a(page_table[NB * c:NB * (c + 1)].reshape(1, NB * 16)),
            "c_invc": invc.reshape(128, 64),
            "c_pbias": np.full((128, 1), 0.0 if h == 1 else -30000.0, f),
        })
        in_maps.append(m)
    res = run_bass_kernel_spmd(nc, in_maps, core_ids=list(range(8)))
    R = res.results
    y_prompt = np.zeros((4, 2048, D), f)
    y_sample = np.zeros((128, 4, D), f)
    k_prompt = np.zeros((1, 4, 2048, 8, 128), f)
    v_prompt = np.zeros((1, 4, 2048, 8, 128), f)
    pool_prompt = np.zeros((1, 4, 15, 1024), f)
    k_sample = np.zeros((1, 128, 4, 8, 128), f)
    v_sample = np.zeros((1, 128, 4, 8, 128), f)
    pool_sample = np.zeros((1, 128, 15, 1024), f)
    for c in range(8):
        b, h = c // 2, c % 2
        r = R[c]
        y_prompt[b, h * TP:(h + 1) * TP] = r["y_p"]
        y_sample[NB * c:NB * (c + 1)] = r["y_s"].reshape(NB, 4, D)
        k_prompt[0, b, h * TP:(h + 1) * TP] = r["k_p"].reshape(TP, 8, 128)
        v_prompt[0, b, h * TP:(h + 1) * TP] = r["v_p"].reshape(TP, 8, 128)
        if h == 1:
            pool_prompt[0, b] = r["pool_p"]
        k_sample[0, NB * c:NB * (c + 1)] = r["k_s"].reshape(NB, 4, 8, 128)
        v_sample[0, NB * c:NB * (c + 1)] = r["v_s"].reshape(NB, 4, 8, 128)
        pool_sample[0, NB * c:NB * (c + 1)] = r["pool_s"]
    return (y_prompt, y_sample, k_prompt, v_prompt, pool_prompt, k_sample, v_sample, pool_sample)
```
